# Optimizing a Trainium2 kernel written in Bass

```python
import math
import jax, jax.numpy as jnp
from jax import lax
import numpy as np

D_MODEL = 1024
BATCH = 16
SEQ = 4096
DEPTH = 1
DEC_BATCH = 16
DEC_SEQ = 64
PAST_LEN = 4096

CHUNK = 64
EPS = 1e-6
NEG_INF = -1e30

RET_HEADS = 4
RET_DK = 256
RET_DV = D_MODEL // RET_HEADS
ROPE_BASE = 10000.0

SWA_HEADS = 16
SWA_KV_HEADS = 2
SWA_GROUP = SWA_HEADS // SWA_KV_HEADS
SWA_HD = 64
WINDOW = 128
WIN_CHUNKS = WINDOW // CHUNK

REL_BUCKETS = 32
REL_MAX_DIST = 128

N_MEM = 256
MEM_HEADS = 4
MEM_HD = D_MODEL // MEM_HEADS

D_FF = -(-(8 * D_MODEL) // (3 * 256)) * 256

SPLIT_SIZES = (RET_HEADS * RET_DK, RET_HEADS * RET_DK, RET_HEADS * RET_DV, RET_HEADS * RET_DV,
               SWA_HEADS * SWA_HD, SWA_KV_HEADS * SWA_HD, SWA_KV_HEADS * SWA_HD, D_MODEL, D_MODEL)
SPLIT_OFFS = tuple(int(o) for o in np.cumsum(SPLIT_SIZES)[:-1])
D_IN = sum(SPLIT_SIZES)

kernel_name = 'hybrid_retention_swa_stream_step'


def rms_norm(x, g=None):
    xf = x.astype(jnp.float32)
    y = xf * lax.rsqrt(jnp.mean(xf * xf, axis=-1, keepdims=True) + EPS)
    if g is not None:
        y = y * g.astype(jnp.float32)
    return y.astype(x.dtype)


def rope(x, pos):
    half = x.shape[-1] // 2
    inv = ROPE_BASE ** (-jnp.arange(half, dtype=jnp.float32) / half)
    ang = pos.astype(jnp.float32)[:, None] * inv[None, :]
    cos = jnp.cos(ang)[:, None, :]
    sin = jnp.sin(ang)[:, None, :]
    xf = x.astype(jnp.float32)
    x1, x2 = xf[..., :half], xf[..., half:]
    return jnp.concatenate([x1 * cos - x2 * sin, x2 * cos + x1 * sin], axis=-1).astype(x.dtype)


def retention_log_decay():
    return jnp.log(1.0 - 2.0 ** (-5.0 - jnp.arange(RET_HEADS, dtype=jnp.float32)))


def retention_block(state, q, k, v, log_gamma):
    C = q.shape[1]
    idx = jnp.arange(C, dtype=jnp.float32)
    lg = log_gamma[None, :]
    intra = jnp.exp(log_gamma[:, None, None] * jnp.abs(idx[:, None] - idx[None, :]))
    s = jnp.einsum('bihd,bjhd->bhij', q, k, preferred_element_type=jnp.float32) * intra
    o = jnp.einsum('bhij,bjhe->bihe', s, v.astype(jnp.float32))
    q_dec = jnp.exp(lg * (idx[:, None] + 1.0))
    o = o + jnp.einsum('bihd,bhde->bihe', q.astype(jnp.float32), state) * q_dec[None, :, :, None]
    k_dec = jnp.exp(lg * (C - 1.0 - idx[:, None]))
    kv = jnp.einsum('bjhd,bjhe->bhde', k.astype(jnp.float32) * k_dec[None, :, :, None],
                    v.astype(jnp.float32))
    new_state = state * jnp.exp(log_gamma * C)[None, :, None, None] + kv
    return new_state, o


def retention_qkv(rq, rk, rv, pos):
    B, S = rq.shape[:2]
    q = rope(rq.reshape(B, S, RET_HEADS, RET_DK), pos)
    k = rope(rk.reshape(B, S, RET_HEADS, RET_DK), pos) * (RET_DK ** -0.5)
    v = rv.reshape(B, S, RET_HEADS, RET_DV)
    return q, k, v


def rel_bias_block(table, n_q, n_past):
    i = jnp.arange(n_q, dtype=jnp.int32)[:, None]
    j = jnp.arange(n_past + n_q, dtype=jnp.int32)[None, :]
    rel = (j - n_past) - i
    half = REL_BUCKETS // 2
    max_exact = half // 2
    n = jnp.abs(rel)
    large = max_exact + (jnp.log(jnp.maximum(n, 1).astype(jnp.float32) / max_exact)
                         / math.log(REL_MAX_DIST / max_exact) * (half - max_exact)).astype(jnp.int32)
    large = jnp.minimum(large, half - 1)
    bucket = jnp.where(rel > 0, half, 0) + jnp.where(n < max_exact, n, large)
    b = table[bucket]
    return jnp.transpose(b, (2, 0, 1)).reshape(SWA_KV_HEADS, SWA_GROUP, n_q, n_past + n_q)


def band_blocks(t):
    B, S = t.shape[:2]
    NC = S // CHUNK
    tp = jnp.pad(t, ((0, 0), (WINDOW, 0), (0, 0), (0, 0))).reshape(B, NC + WIN_CHUNKS, CHUNK, *t.shape[2:])
    return jnp.concatenate([tp[:, w:w + NC] for w in range(WIN_CHUNKS + 1)], axis=2)


def swa_attend(q, k, v, bias, sinks, valid):
    s = jnp.einsum('bnqhgd,bnkhd->bnhgqk', q, k, preferred_element_type=jnp.float32) * (SWA_HD ** -0.5)
    s = s + bias.astype(jnp.float32)
    if valid is not None:
        s = jnp.where(valid, s, NEG_INF)
    sink = sinks.astype(jnp.float32)[:, :, None, None]
    m = jnp.maximum(jnp.max(s, axis=-1, keepdims=True), sink)
    e = jnp.exp(s - m)
    p = e / (jnp.sum(e, axis=-1, keepdims=True) + jnp.exp(sink - m))
    return jnp.einsum('bnhgqk,bnkhd->bnqhgd', p.astype(v.dtype), v)


def in_proj(x, g_attn, w_in):
    return jnp.split(rms_norm(x, g_attn) @ w_in, SPLIT_OFFS, axis=-1)


def merge_branches(ret_o, ret_g, swa_o, gate_a, gate_b, w_ret_out, w_swa_out, w_mix_out):
    B, S = ret_g.shape[:2]
    a = (rms_norm(ret_o).astype(ret_g.dtype).reshape(B, S, -1) * jax.nn.silu(ret_g)) @ w_ret_out
    b = swa_o.reshape(B, S, -1) @ w_swa_out
    return (jax.nn.sigmoid(gate_a) * a + jax.nn.sigmoid(gate_b) * b) @ w_mix_out


def mixer_prompt(h, g_attn, w_in, w_ret_out, w_swa_out, w_mix_out, sinks, bias, log_gamma):
    B, S, _ = h.shape
    NC = S // CHUNK
    rq, rk, rv, rg, sq, sk, sv, ga, gb = in_proj(h, g_attn, w_in)
    pos = jnp.arange(S, dtype=jnp.int32)
    q, k, v = retention_qkv(rq, rk, rv, pos)
    blk = lambda t: jnp.swapaxes(t.reshape(B, NC, CHUNK, *t.shape[2:]), 0, 1)
    s0 = jnp.zeros((B, RET_HEADS, RET_DK, RET_DV), jnp.float32)
    ret_state, o = lax.scan(lambda st, xs: retention_block(st, xs[0], xs[1], xs[2], log_gamma),
                            s0, (blk(q), blk(k), blk(v)))
    ret_o = jnp.swapaxes(o, 0, 1).reshape(B, S, RET_HEADS, RET_DV)
    q_s = sq.reshape(B, NC, CHUNK, SWA_KV_HEADS, SWA_GROUP, SWA_HD)
    k_s = sk.reshape(B, S, SWA_KV_HEADS, SWA_HD)
    v_s = sv.reshape(B, S, SWA_KV_HEADS, SWA_HD)
    key_pos = (jnp.arange(NC, dtype=jnp.int32)[:, None] * CHUNK
               + jnp.arange(WINDOW + CHUNK, dtype=jnp.int32)[None, :] - WINDOW)
    valid = (key_pos >= 0)[None, :, None, None, None, :]
    swa_o = swa_attend(q_s, band_blocks(k_s), band_blocks(v_s), bias,
                       sinks.reshape(SWA_KV_HEADS, SWA_GROUP), valid)
    y = merge_branches(ret_o, rg, swa_o, ga, gb, w_ret_out, w_swa_out, w_mix_out)
    return y, ret_state, k_s[:, S - WINDOW:], v_s[:, S - WINDOW:]


def mixer_sample(h, ret_state, ck, cv, g_attn, w_in, w_ret_out, w_swa_out, w_mix_out, sinks, bias, log_gamma):
    B, T, _ = h.shape
    rq, rk, rv, rg, sq, sk, sv, ga, gb = in_proj(h, g_attn, w_in)
    pos = PAST_LEN + jnp.arange(T, dtype=jnp.int32)
    q, k, v = retention_qkv(rq, rk, rv, pos)
    new_state, ret_o = retention_block(ret_state.astype(jnp.float32), q, k, v, log_gamma)
    q_s = sq.reshape(B, 1, T, SWA_KV_HEADS, SWA_GROUP, SWA_HD)
    k_all = jnp.concatenate([ck, sk.reshape(B, T, SWA_KV_HEADS, SWA_HD).astype(ck.dtype)], axis=1)
    v_all = jnp.concatenate([cv, sv.reshape(B, T, SWA_KV_HEADS, SWA_HD).astype(cv.dtype)], axis=1)
    swa_o = swa_attend(q_s, k_all[:, None], v_all[:, None], bias,
                       sinks.reshape(SWA_KV_HEADS, SWA_GROUP), None)
    y = merge_branches(ret_o, rg, swa_o, ga, gb, w_ret_out, w_swa_out, w_mix_out)
    return y, new_state, k_all[:, T:], v_all[:, T:]


def mem_kv(mem, g_mem, w_mk, w_mv):
    B = mem.shape[0]
    mn = rms_norm(mem, g_mem)
    return ((mn @ w_mk).reshape(B, N_MEM, MEM_HEADS, MEM_HD),
            (mn @ w_mv).reshape(B, N_MEM, MEM_HEADS, MEM_HD))


def cross_attn(x, mk, mv, g_cross, w_cq, w_co):
    B, S, _ = x.shape
    q = (rms_norm(x, g_cross) @ w_cq).reshape(B, S, MEM_HEADS, MEM_HD)
    s = jnp.einsum('bqhd,bkhd->bhqk', q, mk, preferred_element_type=jnp.float32) * (MEM_HD ** -0.5)
    p = jax.nn.softmax(s, axis=-1).astype(mv.dtype)
    o = jnp.einsum('bhqk,bkhd->bqhd', p, mv).reshape(B, S, MEM_HEADS * MEM_HD)
    return o @ w_co


def swiglu(x, g_ffn, w_gate, w_up, w_down):
    hn = rms_norm(x, g_ffn)
    return (jax.nn.silu(hn @ w_gate) * (hn @ w_up)) @ w_down


def setup_inputs(seed: int = 0) -> dict:
    key = jax.random.key(seed)
    ks = jax.random.split(key, 26)

    def nrm(i, shape, scale=1.0):
        return jax.random.normal(ks[i], shape, jnp.float32) * scale

    def gain(i, shape):
        return 1.0 + 0.01 * jax.random.normal(ks[i], shape, jnp.float32)

    n_swa = min(WINDOW, PAST_LEN)
    L = DEPTH
    return {
        'x_prompt': nrm(0, (BATCH, SEQ, D_MODEL)),
        'x_sample': nrm(1, (DEC_BATCH, DEC_SEQ, D_MODEL)),
        'cache_ret_state': nrm(2, (L, DEC_BATCH, RET_HEADS, RET_DK, RET_DV), 0.1),
        'cache_swa_k': nrm(3, (L, DEC_BATCH, n_swa, SWA_KV_HEADS, SWA_HD)),
        'cache_swa_v': nrm(4, (L, DEC_BATCH, n_swa, SWA_KV_HEADS, SWA_HD)),
        'cache_mem_k': nrm(5, (L, DEC_BATCH, N_MEM, MEM_HEADS, MEM_HD)),
        'cache_mem_v': nrm(6, (L, DEC_BATCH, N_MEM, MEM_HEADS, MEM_HD)),
        'mem_prompt': nrm(7, (BATCH, N_MEM, D_MODEL)),
        'rel_bias': nrm(8, (REL_BUCKETS, SWA_HEADS), 0.1),
        'g_attn': gain(9, (L, D_MODEL)),
        'w_in': nrm(10, (L, D_MODEL, D_IN), D_MODEL ** -0.5),
        'w_ret_out': nrm(11, (L, RET_HEADS * RET_DV, D_MODEL), (RET_HEADS * RET_DV) ** -0.5),
        'w_swa_out': nrm(12, (L, SWA_HEADS * SWA_HD, D_MODEL), (SWA_HEADS * SWA_HD) ** -0.5),
        'w_mix_out': nrm(13, (L, D_MODEL, D_MODEL), D_MODEL ** -0.5),
        'swa_sinks': nrm(14, (L, SWA_HEADS)),
        'g_cross': gain(15, (L, D_MODEL)),
        'g_mem': gain(16, (L, D_MODEL)),
        'w_cq': nrm(17, (L, D_MODEL, MEM_HEADS * MEM_HD), D_MODEL ** -0.5),
        'w_mk': nrm(18, (L, D_MODEL, MEM_HEADS * MEM_HD), D_MODEL ** -0.5),
        'w_mv': nrm(19, (L, D_MODEL, MEM_HEADS * MEM_HD), D_MODEL ** -0.5),
        'w_co': nrm(20, (L, MEM_HEADS * MEM_HD, D_MODEL), (MEM_HEADS * MEM_HD) ** -0.5),
        'g_ffn': gain(21, (L, D_MODEL)),
        'w_gate': nrm(22, (L, D_MODEL, D_FF), D_MODEL ** -0.5),
        'w_up': nrm(23, (L, D_MODEL, D_FF), D_MODEL ** -0.5),
        'w_down': nrm(24, (L, D_FF, D_MODEL), D_FF ** -0.5),
        'g_final': gain(25, (D_MODEL,)),
    }


def reference(x_prompt, x_sample, cache_ret_state, cache_swa_k, cache_swa_v, cache_mem_k, cache_mem_v,
              mem_prompt, rel_bias, g_attn, w_in, w_ret_out, w_swa_out, w_mix_out, swa_sinks,
              g_cross, g_mem, w_cq, w_mk, w_mv, w_co, g_ffn, w_gate, w_up, w_down, g_final):
    log_gamma = retention_log_decay()
    T = x_sample.shape[1]
    bias_p = rel_bias_block(rel_bias, CHUNK, WINDOW)
    bias_s = rel_bias_block(rel_bias, T, cache_swa_k.shape[2])
    hp, hs = x_prompt, x_sample
    ret_p, ret_s, kp, ksm, vp, vsm, mkp, mvp = [], [], [], [], [], [], [], []
    for l in range(DEPTH):
        y, st, kb, vb = mixer_prompt(hp, g_attn[l], w_in[l], w_ret_out[l], w_swa_out[l], w_mix_out[l],
                                     swa_sinks[l], bias_p, log_gamma)
        hp = hp + y
        mk, mv = mem_kv(mem_prompt, g_mem[l], w_mk[l], w_mv[l])
        hp = hp + cross_attn(hp, mk, mv, g_cross[l], w_cq[l], w_co[l])
        hp = hp + swiglu(hp, g_ffn[l], w_gate[l], w_up[l], w_down[l])
        ret_p.append(st); kp.append(kb); vp.append(vb); mkp.append(mk); mvp.append(mv)
        y, st, kb, vb = mixer_sample(hs, cache_ret_state[l], cache_swa_k[l], cache_swa_v[l], g_attn[l], w_in[l],
                                     w_ret_out[l], w_swa_out[l], w_mix_out[l], swa_sinks[l], bias_s, log_gamma)
        hs = hs + y
        hs = hs + cross_attn(hs, cache_mem_k[l], cache_mem_v[l], g_cross[l], w_cq[l], w_co[l])
        hs = hs + swiglu(hs, g_ffn[l], w_gate[l], w_up[l], w_down[l])
        ret_s.append(st); ksm.append(kb); vsm.append(vb)
    y_prompt = rms_norm(hp, g_final)
    y_sample = rms_norm(hs, g_final)
    return (y_prompt, y_sample, jnp.stack(ret_p), jnp.stack(ret_s), jnp.stack(kp), jnp.stack(ksm),
            jnp.stack(vp), jnp.stack(vsm), jnp.stack(mkp), jnp.stack(mvp))
```

```python
import math
from contextlib import ExitStack
import numpy as np
import ml_dtypes
import concourse.bass as bass
import concourse.mybir as mybir
from concourse.bass_utils import run_bass_kernel_spmd

F32 = mybir.dt.float32
BF16 = mybir.dt.bfloat16
AF = mybir.ActivationFunctionType
ALU = mybir.AluOpType

D = 1024
SEQ = 4096
PAST = 4096
DFF = 2816
EPS = 1e-6
ENGS = ('pe', 'act', 'dve', 'pool', 'sp')
RR_A = 1
RR_B = 1
STRICT_SAME_ENGINE = False
ALLN = 10 ** 9


class _Op:
    __slots__ = ('fn', 'waits', 'signal', 'dma', 'sigval')

    def __init__(self, fn, dma):
        self.fn = fn
        self.waits = []
        self.signal = False
        self.dma = dma
        self.sigval = 0


class Prog:
    def __init__(self):
        self.ops = {e: [] for e in ENGS}
        self.lastw = {}
        self.rd = {}
        self.seen = {e: {} for e in ENGS}
        self.dcount = {}
        self.strict = True

    def add(self, eng, fn, reads=(), writes=(), dma=None):
        ops = self.ops[eng]
        idx = len(ops)
        op = _Op(fn, dma)
        seen = self.seen[eng]
        cand = []
        for k in reads:
            w = self.lastw.get(k)
            if w is not None:
                cand.append((w, True))
        for k in writes:
            w = self.lastw.get(k)
            if w is not None:
                cand.append((w, False))
            r = self.rd.get(k)
            if r:
                for ref in r.values():
                    cand.append((ref, False))
        for ref, is_raw in cand:
            if ref[0] == 'c':
                _, e2, i2 = ref
                if e2 == eng and dma is None:
                    if eng == 'pe' or (not is_raw and not self.strict):
                        continue
                if seen.get(e2, -1) >= i2:
                    continue
                seen[e2] = i2
                op.waits.append(ref)
                self.ops[e2][i2].signal = True
            else:
                _, key, n = ref
                sk = ('d', key)
                if seen.get(sk, 0) >= n:
                    continue
                seen[sk] = n
                op.waits.append(ref)
        if dma is None:
            ref = ('c', eng, idx)
            rk = eng
        else:
            n = self.dcount.get(dma, 0) + 1
            self.dcount[dma] = n
            ref = ('d', dma, n)
            rk = ('d', dma)
        for k in writes:
            self.lastw[k] = ref
            self.rd[k] = {}
        for k in reads:
            self.rd.setdefault(k, {})[rk] = ref
        ops.append(op)
        return ref

    def emit(self, nc, es):
        sems = {e: es.enter_context(nc.semaphore('s_' + e)) for e in ENGS}
        dsem = {}
        for i, k in enumerate(self.dcount):
            dsem[k] = es.enter_context(nc.semaphore('d%d' % i))
        for e in ENGS:
            c = 0
            for op in self.ops[e]:
                if op.signal:
                    c += 1
                    op.sigval = c
        block = es.enter_context(nc.Block())
        prog = self

        def run(eng_name, e):
            for op in prog.ops[eng_name]:
                for ref in op.waits:
                    if ref[0] == 'c':
                        e.wait_ge(sems[ref[1]], prog.ops[ref[1]][ref[2]].sigval)
                    else:
                        n = ref[2]
                        if n == ALLN:
                            n = prog.dcount[ref[1]]
                        e.wait_ge(dsem[ref[1]], 16 * n)
                ins = op.fn(e)
                if op.dma is not None:
                    ins.then_inc(dsem[op.dma], 16)
                elif op.signal:
                    ins.then_inc(sems[eng_name], 1)
            if eng_name == 'pool':
                for k, n in prog.dcount.items():
                    e.wait_ge(dsem[k], 16 * n)

        @block.tensor
        def _(e):
            run('pe', e)

        @block.scalar
        def _(e):
            run('act', e)

        @block.vector
        def _(e):
            run('dve', e)

        @block.gpsimd
        def _(e):
            run('pool', e)

        @block.sync
        def _(e):
            run('sp', e)


def _consts(S):
    c = {}
    half = 128
    inv = (np.float32(10000.0) ** (-np.arange(half, dtype=np.float32) / np.float32(half))).astype(np.float32)
    pos = np.concatenate([np.arange(S, dtype=np.float32), PAST + np.arange(64, dtype=np.float32)]).astype(np.float32)
    ang = (pos[None, :] * inv[:, None]).astype(np.float32)
    c['cosT'] = np.cos(ang).astype(np.float32)
    c['sinT'] = np.sin(ang).astype(np.float32)
    lg = np.log(np.float32(1.0) - np.float32(2.0) ** (-5.0 - np.arange(4, dtype=np.float32))).astype(np.float32)
    idx = np.arange(128, dtype=np.float32)
    i = idx[None, :]
    j = idx[:, None]
    same = (np.floor(i / 64) == np.floor(j / 64))
    lower = (np.floor(i / 64) > np.floor(j / 64))
    dm = np.zeros((128, 4, 128), np.float32)
    for h in range(4):
        intra = np.exp(lg[h] * np.abs(i - j)).astype(np.float32)
        cross = (np.exp(lg[h] * (np.mod(i, 64) + 1.0)).astype(np.float32)
                 * np.exp(lg[h] * (63.0 - np.mod(j, 64))).astype(np.float32)).astype(np.float32)
        dm[:, h, :] = np.where(same, intra, np.where(lower, cross, 0.0)) / np.float32(16.0)
    c['dmask'] = dm
    qd = np.zeros((128, 4, 128), np.float32)
    kd128 = np.zeros((128, 4), np.float32)
    kd64 = np.zeros((128, 4), np.float32)
    g128 = []
    g64 = []
    for h in range(4):
        g64h = np.exp(lg[h] * np.float32(64.0)).astype(np.float32)
        qd64 = np.exp(lg[h] * (np.arange(64, dtype=np.float32) + 1.0)).astype(np.float32)
        kdec64 = np.exp(lg[h] * (63.0 - np.arange(64, dtype=np.float32))).astype(np.float32)
        qd[:, h, :64] = qd64[None, :]
        qd[:, h, 64:] = (qd64 * g64h)[None, :]
        kd64[:64, h] = kdec64 / 16.0
        kd128[:64, h] = kdec64 * g64h / 16.0
        kd128[64:, h] = kdec64 / 16.0
        g64.append(float(g64h))
        g128.append(float(np.float32(g64h * g64h)))
    c['qdec'] = qd
    c['kd128'] = kd128
    c['kd64'] = kd64
    c['ident'] = np.eye(128, dtype=np.float32).astype(ml_dtypes.bfloat16)
    q = np.arange(64, dtype=np.int32)[:, None]
    jj = np.arange(192, dtype=np.int32)[None, :]
    rel = (jj - 128) - q
    n = np.abs(rel)
    large = 8 + (np.log(np.maximum(n, 1).astype(np.float32) / 8) / math.log(128 / 8) * 8).astype(np.int32)
    large = np.minimum(large, 15)
    bucket = np.where(rel > 0, 16, 0) + np.where(n < 8, n, large)
    kk = np.arange(128)
    maps = [kk, np.where(kk < 64, 128 + kk, -1), 64 + kk, np.where(kk >= 64, kk - 64, -1)]
    oh = np.zeros((4, 32, 64, 128), np.float32)
    for v, m in enumerate(maps):
        for k in range(128):
            if m[k] >= 0:
                oh[v, bucket[:, m[k]], np.arange(64), k] = 1.0
    c['oh'] = oh.astype(ml_dtypes.bfloat16)
    return c, g128, g64


def _wchunks():
    ch = {}
    offs = dict(rq=0, rk=1024, rv=2048, rg=3072, sq=4096, sk=5120, sv=5248, ga=5376, gb=6400)
    for nm in ('rq', 'rk', 'rv', 'rg', 'sq', 'ga', 'gb'):
        for i in range(2):
            ch['%s%d' % (nm, i)] = ('w_in', 0, 8, [(offs[nm] + 512 * i, 512)])
    ch['skd'] = ('w_in', 0, 8, [(5120, 64), (5120, 64), (5184, 64), (5184, 64)])
    ch['skv'] = ('w_in', 0, 8, [(5120, 256)])
    for nm in ('w_ret_out', 'w_swa_out', 'w_mix_out', 'w_cq', 'w_co', 'w_mk', 'w_mv'):
        for i in range(2):
            ch['%s%d' % (nm, i)] = (nm, 0, 8, [(512 * i, 512)])
    for i in range(11):
        ch['gu%d' % i] = ('w_gu', 0, 8, [(256 * i, 256)])
    for hf in range(2):
        for sb, (k0, nk) in enumerate(((0, 8), (8, 8), (16, 6))):
            ch['dn%d_%d' % (hf, sb)] = ('w_down', k0, nk, [(512 * hf, 512)])
    return ch


WNAMES = ('w_in', 'w_ret_out', 'w_swa_out', 'w_mix_out', 'w_cq', 'w_mk', 'w_mv', 'w_co', 'w_gate', 'w_up', 'w_down')
WSHAPES = dict(w_in=(D, 7424), w_ret_out=(D, D), w_swa_out=(D, D), w_mix_out=(D, D), w_cq=(D, D), w_mk=(D, D),
               w_mv=(D, D), w_co=(D, D), w_gate=(D, DFF), w_up=(D, DFF), w_down=(DFF, D))


def build(S, dbg=False):
    nc = bass.Bass("TRN2", target_bir_lowering=False)
    P = Prog()
    consts, G128, G64 = _consts(S)
    NBLK = S // 512

    def din(name, shape, dt=F32):
        return nc.dram_tensor(name, list(shape), dt, kind="ExternalInput").ap()

    def dout(name, shape):
        return nc.dram_tensor(name, list(shape), F32, kind="ExternalOutput").ap()

    xp = din('xp', [2, S, D]); xs = din('xs', [2, 64, D])
    cret = din('cret', [2, 4, 256, 256]); cswk = din('cswk', [2, 128, 128]); cswv = din('cswv', [2, 128, 128])
    cmk = din('cmk', [2, 256, D]); cmv = din('cmv', [2, 256, D]); memp = din('memp', [2, 256, D])
    relb = din('relb', [32, 16]); sinks = din('sinks', [16])
    gvec = {n: din(n, [D]) for n in ('g_attn', 'g_cross', 'g_mem', 'g_ffn', 'g_final')}
    W = {n: din(n, WSHAPES[n]) for n in WNAMES}
    cd = {}
    for n, a in consts.items():
        cd[n] = din('c_' + n, a.shape, BF16 if a.dtype == ml_dtypes.bfloat16 else F32)

    y_p = dout('y_p', [2, S, D]); y_s = dout('y_s', [2, 64, D])
    rst_p = dout('rst_p', [2, 4, 256, 256]); rst_s = dout('rst_s', [2, 4, 256, 256])
    swk_p = dout('swk_p', [2, 128, 128]); swk_s = dout('swk_s', [2, 128, 128])
    swv_p = dout('swv_p', [2, 128, 128]); swv_s = dout('swv_s', [2, 128, 128])
    mk_p = dout('mk_p', [2, 256, D]); mv_p = dout('mv_p', [2, 256, D])

    chunks = _wchunks()
    wb = {}
    for nm, (src, k0, nk, cols) in chunks.items():
        ncol = sum(c[1] for c in cols) * (2 if src == 'w_gu' else 1)
        wb[nm] = nc.dram_tensor('wb_' + nm, [128, nk, ncol], BF16, kind="Internal").ap()

    es = ExitStack()
    with es:
        def sb(name, shape, dt=F32):
            return es.enter_context(nc.sbuf_tensor(name, list(shape), dt))

        XT = [sb('xt%d' % i, [128, D]) for i in range(8)]
        xoff = [0]
        slab = [sb('slab%d' % i, [128, 8, 512], BF16) for i in range(7)]
        S_XN, S_Q, S_K, S_QS, S_RG, S_SQ, S_SWA = range(7)
        hT = sb('hT_extra', [128, 1, 1], BF16)
        vtok = sb('vtok', [128, 4, D], BF16)
        ktok = [sb('ktok%d' % i, [128, 4, 256], BF16) for i in range(2)]
        wsl = [sb('wsl%d' % i, [128, 8, 512], BF16) for i in range(3)]
        gfin = sb('gfin', [128, D])
        gT = {n: sb('gT_' + n, [128, 8]) for n in ('g_attn', 'g_cross', 'g_mem', 'g_ffn')}
        cosb = sb('cosb', [128, 512]); sinb = sb('sinb', [128, 512])
        dmask = sb('dmask', [128, 4, 128]); qdec = sb('qdec', [128, 4, 128])
        kd128 = sb('kd128', [128, 4]); kd64 = sb('kd64', [128, 4])
        ident = sb('ident', [128, 128], BF16)
        ones = sb('ones', [128, 128], BF16)
        onesp = sb('onesp', [128, 2, 128], BF16)
        EB = [sb('EB%d' % v, [128, 16, 64], BF16) for v in range(4)]
        esink = sb('esink', [128, 8])
        tabf = sb('tabf', [128, 16]); tabh = sb('tabh', [128, 16], BF16); tabl = sb('tabl', [128, 16], BF16)
        tabr = sb('tabr', [128, 16])
        st = sb('st', [128, 4, 2, 256]); stb = sb('stb', [128, 4, 2, 256], BF16)
        KT = sb('KT', [128, 2, 2, 640], BF16)
        VP = sb('VP', [128, 5, 2, 2, 128], BF16)
        mkT = sb('mkT', [128, 8, 256], BF16)
        mvt = sb('mvt', [128, 2, D], BF16)
        xnbs = [sb('xnb%d' % i, [128, D], BF16) for i in range(2)]
        ssq = sb('ssq', [128, 8]); rstd = sb('rstd', [128, 8])
        rt = [sb('rt%d' % i, [128, 512]) for i in range(4)]
        smb = sb('smb', [128, 4, 128], BF16)
        osq = sb('osq', [128, 8, 128], BF16)
        rsd = sb('rsd', [128, 4, 128])
        u1 = sb('u1', [128, 8, 128])
        junk = u1[:].rearrange("p a b -> p (a b)").bitcast(BF16)[:, 0:D]
        xnbs.append(u1[:].rearrange("p a b -> p (a b)").bitcast(BF16)[:, D:2 * D])
        ef = [rt[0], rt[1]]
        ebb = sb('ebb', [128, 8, 512], BF16)
        eb = [ebb[:, i] for i in range(8)]
        den = rt[2]; rden = rt[3]
        ceb = [ebb[:, 2 * i:2 * i + 2] for i in range(4)]
        tb = [sb('tb%d' % i, [128, 512], BF16) for i in range(2)]
        sf = [rt[0], rt[1]]
        f32o = sb('f32o', [128, 256])
        epsb = sb('epsb', [128, 2])

        banks = [es.enter_context(nc.psum_tensor('pb%d' % i, [128, 512], F32)) for i in range(8)]
        if dbg == 'mem':
            print('SBUF bytes remaining per partition:', nc.sbuf_bytes_remaining)
        bctr = [0]

        def nb():
            i = bctr[0] % 8
            bctr[0] += 1
            return i

        def bk(i):
            return ('ps', i)

        def mkalloc(pool):
            c = [0]

            def f():
                i = pool[c[0] % len(pool)]
                c[0] += 1
                return i
            return f

        def MM(out, lhsT, rhs, start, stop, reads, bank):
            P.add('pe', lambda e: e.matmul(out, lhsT, rhs, start=start, stop=stop, skip_group_check=True),
                  reads=reads, writes=[bk(bank)])

        def TR(out, in_, reads, bank):
            P.add('pe', lambda e: e.transpose(out, in_, ident[0:in_.shape[0], 0:in_.shape[0]]),
                  reads=list(reads) + ['ident'], writes=[bk(bank)])

        def ACT(out, in_, func, reads, writes, scale=1.0, bias=0.0, accum=None):
            if accum is None:
                P.add('act', lambda e: e.activation(out, in_, func, bias=bias, scale=scale), reads=reads, writes=writes)
            else:
                P.add('act', lambda e: e.activation(out, in_, func, bias=bias, scale=scale, accum_out=accum),
                      reads=reads, writes=writes)

        def TT(eng, out, in0, in1, op, reads, writes):
            P.add(eng, lambda e: e.tensor_tensor(out, in0, in1, op), reads=reads, writes=writes)

        def STT(out, in0, scalar, in1, op0, op1, reads, writes):
            P.add('dve', lambda e: e.scalar_tensor_tensor(out, in0, scalar, in1, op0, op1), reads=reads, writes=writes)

        def TS(eng, out, in0, s1, s2, op0, op1, reads, writes):
            if s2 is None:
                P.add(eng, lambda e: e.tensor_scalar(out, in0, s1, None, op0), reads=reads, writes=writes)
            else:
                P.add(eng, lambda e: e.tensor_scalar(out, in0, s1, s2, op0, op1), reads=reads, writes=writes)

        def CP(eng, out, in_, reads, writes):
            if eng == 'act':
                P.add('act', lambda e: e.copy(out, in_), reads=reads, writes=writes)
            else:
                P.add(eng, lambda e: e.tensor_copy(out, in_), reads=reads, writes=writes)

        def MEMSET(eng, ap, val, writes):
            P.add(eng, lambda e: e.memset(ap, val), writes=writes)

        def DMA(q, out, in_, key, reads=(), writes=(), slow=False):
            if slow:
                return P.add(q, lambda e: e.dma_start(out=out, in_=in_, allow_slow_non_contiguous=True),
                             reads=reads, writes=writes, dma=key)
            return P.add(q, lambda e: e.dma_start(out=out, in_=in_), reads=reads, writes=writes, dma=key)

        cast_order = (['w_mk0', 'w_mk1', 'w_mv0', 'w_mv1'] +
                      ['rq0', 'rq1', 'rk0', 'rk1', 'rv0', 'rv1', 'rg0', 'rg1', 'sq0', 'sq1', 'skd', 'skv',
                       'ga0', 'ga1', 'gb0', 'gb1', 'w_ret_out0', 'w_ret_out1', 'w_swa_out0', 'w_swa_out1',
                       'w_mix_out0', 'w_mix_out1', 'w_cq0', 'w_cq1', 'w_co0', 'w_co1'] +
                      ['gu%d' % i for i in range(11)] +
                      ['dn%d_%d' % (h, s) for h in range(2) for s in range(3)])
        cast_idx = {nm: i for i, nm in enumerate(cast_order)}
        cast_done = [0]
        LOOK = 10

        def cast_upto(n):
            n = min(n, len(cast_order))
            while cast_done[0] < n:
                nm = cast_order[cast_done[0]]
                cast_done[0] += 1
                src, k0, nk, cols = chunks[nm]
                key = ('cast', nm)
                c0 = 0
                if src == 'w_gu':
                    col0, ncw = cols[0]
                    for wn in ('w_gate', 'w_up'):
                        sap = W[wn][k0 * 128:(k0 + nk) * 128, col0:col0 + ncw].rearrange("(kt p) n -> p kt n", p=128)
                        DMA('pool', wb[nm][:, :, c0:c0 + ncw], sap, key)
                        c0 += ncw
                else:
                    for (col0, ncw) in cols:
                        sap = W[src][k0 * 128:(k0 + nk) * 128, col0:col0 + ncw].rearrange("(kt p) n -> p kt n", p=128)
                        DMA('pool', wb[nm][:, :, c0:c0 + ncw], sap, key)
                        c0 += ncw
                P.lastw[('wb', nm)] = ('d', key, ALLN)

        cast_upto(LOOK)

        wctr = [0]

        def wload(nm):
            cast_upto(cast_idx[nm] + LOOK)
            i = wctr[0] % 3
            wctr[0] += 1
            nk = chunks[nm][2]
            ncol = wb[nm].shape[2]
            DMA('sp', wsl[i][:, 0:nk, 0:ncol], wb[nm], ('wl', i), reads=[('wb', nm)], writes=[('wsl', i)])
            return i

        def cload(dst, src, key, wkey, slow=False):
            DMA('sp', dst, src, key, writes=[wkey], slow=slow)

        cload(ident[:], cd['ident'], 'c0', 'ident')
        cload(dmask[:], cd['dmask'], 'c1', 'dmask')
        cload(qdec[:], cd['qdec'], 'c2', 'qdec')
        cload(kd128[:], cd['kd128'], 'c3', 'kd128')
        cload(kd64[:], cd['kd64'], 'c4', 'kd64')
        cload(gfin[:], gvec['g_final'].partition_broadcast(128), 'c5', 'gfin')
        for i, n in enumerate(('g_attn', 'g_cross', 'g_mem', 'g_ffn')):
            cload(gT[n][:].unsqueeze(2), gvec[n].rearrange("(kt p o) -> p kt o", p=128, o=1), 'c6%d' % i, ('gT', n), slow=True)
        MEMSET('dve', tabf[:], 0.0, ['tabf'])
        cload(tabf[0:32, :], relb, 'c7', 'tabf')
        sk2 = sinks.rearrange("(t two o) -> two t o", two=2, o=1)
        cload(esink[0:64, :].unsqueeze(2), sk2[0].partition_broadcast(64), 'c8', 'esink_a', slow=True)
        cload(esink[64:128, :].unsqueeze(2), sk2[1].partition_broadcast(64), 'c9', 'esink_b', slow=True)
        ACT(esink[:], esink[:], AF.Exp, ['esink_a', 'esink_b'], ['esink'])
        MEMSET('dve', ones[:], 1.0, ['ones'])
        MEMSET('dve', epsb[:, 0:1], EPS, ['epsb'])
        MEMSET('dve', epsb[:, 1:2], 256.0 * EPS, ['epsb'])
        MEMSET('dve', onesp[:], 0.0, ['onesp'])
        MEMSET('dve', onesp[:, 0, 0:64], 1.0, ['onesp'])
        MEMSET('dve', onesp[:, 1, 64:128], 1.0, ['onesp'])
        MEMSET('dve', KT[:], 0.0, ['KT'])
        MEMSET('dve', VP[:], 0.0, ['VP'])
        CP('dve', tabh[:], tabf[:], ['tabf'], ['tabh'])
        CP('dve', tabr[:], tabh[:], ['tabh'], ['tabr'])
        TT('dve', tabr[:], tabf[:], tabr[:], ALU.subtract, ['tabf', 'tabr'], ['tabr'])
        CP('dve', tabl[:], tabr[:], ['tabr'], ['tabl'])
        ohv = slab[S_Q][:].rearrange("p a b -> p (a b)").rearrange("p (q k) -> p q k", k=128)
        MEMSET('dve', ohv, 0.0, [('slab', S_Q)])
        for v in range(4):
            for hq in range(2):
                DMA('sp', ohv[0:32], cd['oh'][v, :, hq * 32:(hq + 1) * 32, :], 'coh', writes=[('slab', S_Q)])
                b0 = nb()
                for qq in range(32):
                    o = banks[b0][:, qq * 16:(qq + 1) * 16]
                    MM(o, ohv[:, qq, :], tabh[:], True, False, [('slab', S_Q), 'tabh'], b0)
                    MM(o, ohv[:, qq, :], tabl[:], False, True, [('slab', S_Q), 'tabl'], b0)
                src_v = banks[b0][:].rearrange("p (q kv tq par) -> p kv par tq q", q=32, kv=2, tq=4, par=2)
                dst_v = EB[v][:, :, hq * 32:(hq + 1) * 32].rearrange("p (kv par tq) q -> p kv par tq q", kv=2, par=2, tq=4)
                for kv in range(2):
                    for par in range(2):
                        for tq in range(4):
                            ACT(dst_v[:, kv, par, tq, :], src_v[:, kv, par, tq, :], AF.Copy, [], [bk(b0), ('EB', v)], scale=8.0)
        MEMSET('dve', EB[1][64:128], -400.0, [('EB', 1)])
        MEMSET('dve', EB[3][0:64], -400.0, [('EB', 3)])

        def norm_a(tt, xa, xk, pt, col):
            ACT(junk[0:pt, :], xa, AF.Square, [xk], ['u1', ('ssq', col)], accum=ssq[0:pt, col:col + 1])
            ACT(rstd[0:pt, col:col + 1], ssq[0:pt, col:col + 1], AF.Ln, [('ssq', col), 'epsb'], [('rstd', col)], scale=1.0 / D, bias=epsb[0:pt, 0:1])
            ACT(rstd[0:pt, col:col + 1], rstd[0:pt, col:col + 1], AF.Exp, [('rstd', col)], [('rstd', col)], scale=-0.5)
            TS('dve', xnbs[tt % 3][0:pt, :], xa, rstd[0:pt, col:col + 1], None, ALU.mult, None, [xk, ('rstd', col)],
               [('xnb', tt % 3)] + (['u1'] if tt % 3 == 2 else []))

        def norm_b(tt, pt, gname, dst):
            b0 = nb()
            pv = banks[b0][:].bitcast(BF16)
            xb = xnbs[tt % 3]
            for kt in range(8):
                TR(pv[:, kt * 128:kt * 128 + pt], xb[0:pt, kt * 128:(kt + 1) * 128], [('xnb', tt % 3)], b0)
            for kt in range(8):
                eng = 'act' if kt % 2 == 0 else 'dve'
                o = slab[dst][:, kt, tt * 128:tt * 128 + pt]
                i_ = pv[:, kt * 128:kt * 128 + pt]
                if eng == 'act':
                    ACT(o, i_, AF.Copy, [('gT', gname)], [bk(b0), ('slab', dst)], scale=gT[gname][:, kt:kt + 1])
                else:
                    TS('dve', o, i_, gT[gname][:, kt:kt + 1], None, ALU.mult, None, [('gT', gname)], [bk(b0), ('slab', dst)])

        def norm_T(tiles, pts, gname, dst):
            for tt, ((xa, xk), pt) in enumerate(zip(tiles, pts)):
                norm_a(tt, xa, xk, pt, tt)
                norm_b(tt, pt, gname, dst)

        def proj_fm(wname, ntile, src_slab, N, col_base=0, alloc=None):
            wi = wload(wname)
            outs = []
            for j in range(ntile):
                b0 = (alloc or nb)()
                for kt in range(8):
                    MM(banks[b0][:, 0:N], wsl[wi][:, kt, col_base + j * 128:col_base + (j + 1) * 128],
                       slab[src_slab][:, kt, 0:N], kt == 0, kt == 7, [('wsl', wi), ('slab', src_slab)], b0)
                outs.append(b0)
                yield j, b0

        def mem_finish():
            for kt2 in range(2):
                CP('act', vtok[:, kt2, :], XT[xoff[0] + kt2][:], [('xt', xoff[0] + kt2)], ['vtok'])
                CP('dve', mvt[:, kt2, :], XT[xoff[0] + 2 + kt2][:], [('xt', xoff[0] + 2 + kt2)], ['mvt'])
            for kt2 in range(2):
                b0 = nb()
                pv = banks[b0][:].bitcast(BF16)
                for t8 in range(8):
                    TR(pv[:, t8 * 128:(t8 + 1) * 128], vtok[:, kt2, t8 * 128:(t8 + 1) * 128], ['vtok'], b0)
                CP('act', mkT[:, :, kt2 * 128:(kt2 + 1) * 128], pv.rearrange("p (t k) -> p t k", k=128), [], [bk(b0), 'mkT'])

        def mem_prompt(b):
            for tt in range(2):
                DMA('sp', XT[xoff[0] + tt][:], memp[b, tt * 128:(tt + 1) * 128, :], ('xl', xoff[0] + tt), writes=[('xt', xoff[0] + tt)])
            norm_T([(XT[xoff[0] + 0][:], ('xt', xoff[0] + 0)), (XT[xoff[0] + 1][:], ('xt', xoff[0] + 1))], [128, 128], 'g_mem', S_XN)
            for wi_, (wn, dst0, outap) in enumerate((('w_mk', 0, mk_p), ('w_mv', 2, mv_p))):
                for hf in range(2):
                    wi = wload('%s%d' % (wn, hf))
                    for tt in range(2):
                        b0 = nb()
                        for kt in range(8):
                            MM(banks[b0][:], slab[S_XN][:, kt, tt * 128:(tt + 1) * 128], wsl[wi][:, kt, :], kt == 0, kt == 7,
                               [('wsl', wi), ('slab', S_XN)], b0)
                        eng = 'act' if (hf + tt) % 2 == 0 else 'dve'
                        CP(eng, XT[xoff[0] + dst0 + tt][:, hf * 512:(hf + 1) * 512], banks[b0][:], [], [bk(b0), ('xt', xoff[0] + dst0 + tt)])
                for tt in range(2):
                    DMA('pool', outap[b, tt * 128:(tt + 1) * 128, :], XT[xoff[0] + dst0 + tt][:], ('xo', xoff[0] + dst0 + tt), reads=[('xt', xoff[0] + dst0 + tt)])
            mem_finish()

        def mem_sample(b):
            for tt in range(2):
                DMA('sp', XT[xoff[0] + tt][:], cmk[b, tt * 128:(tt + 1) * 128, :], ('xl', xoff[0] + tt), writes=[('xt', xoff[0] + tt)])
                DMA('sp', XT[xoff[0] + 2 + tt][:], cmv[b, tt * 128:(tt + 1) * 128, :], ('xl', xoff[0] + 2 + tt), writes=[('xt', xoff[0] + 2 + tt)])
            mem_finish()

        def pro_load(xsrc, N, pos0):
            NT = (N + 127) // 128
            pts = [min(128, N - 128 * t) for t in range(NT)]
            for tt in range(NT):
                DMA('sp', XT[xoff[0] + tt][0:pts[tt], :], xsrc[tt * 128:tt * 128 + pts[tt], :], ('xl', xoff[0] + tt), writes=[('xt', xoff[0] + tt)])
            DMA('sp', cosb[:, 0:N], cd['cosT'][:, pos0:pos0 + N], 'cosl', writes=['cosb'])
            DMA('sp', sinb[:, 0:N], cd['sinT'][:, pos0:pos0 + N], 'sinl', writes=['sinb'])

        def pro_norm(N):
            NT = (N + 127) // 128
            pts = [min(128, N - 128 * t) for t in range(NT)]
            xtl = [(XT[xoff[0] + t][0:pts[t], :], ('xt', xoff[0] + t)) for t in range(NT)]
            norm_T(xtl, pts, 'g_attn', S_XN)

        def block(xsrc, ydst, N, pos0, units, first_chunk, is_last, kind, b, pro_done=False, nxt=None):
            NT = (N + 127) // 128
            pts = [min(128, N - 128 * t) for t in range(NT)]
            if not pro_done:
                pro_load(xsrc, N, pos0)
                pro_norm(N)
            xtl = [(XT[xoff[0] + t][0:pts[t], :], ('xt', xoff[0] + t)) for t in range(NT)]

            for nm, dst in (('rq', S_Q), ('rk', S_K)):
                for hf in range(2):
                    pend = []
                    for j, b0 in proj_fm('%s%d' % (nm, hf), 4, S_XN, N):
                        pend.append(b0)
                        if len(pend) == 2:
                            bA, bB = pend
                            pend = []
                            tA = hf * 4 + j - 1
                            A = banks[bA][:, 0:N]; B = banks[bB][:, 0:N]
                            cs = cosb[:, 0:N]; sn = sinb[:, 0:N]
                            TT('dve', rt[0][:, 0:N], A, cs, ALU.mult, ['cosb'], [bk(bA), 'rt0'])
                            TT('dve', rt[1][:, 0:N], B, sn, ALU.mult, ['sinb'], [bk(bB), 'rt1'])
                            TT('dve', rt[2][:, 0:N], B, cs, ALU.mult, ['cosb'], [bk(bB), 'rt2'])
                            TT('dve', rt[3][:, 0:N], A, sn, ALU.mult, ['sinb'], [bk(bA), 'rt3'])
                            TT('pool', slab[dst][:, tA, 0:N], rt[0][:, 0:N], rt[1][:, 0:N], ALU.subtract,
                               ['rt0', 'rt1'], [('slab', dst)])
                            TT('pool', slab[dst][:, tA + 1, 0:N], rt[2][:, 0:N], rt[3][:, 0:N], ALU.add,
                               ['rt2', 'rt3'], [('slab', dst)])
            for (u0, C) in units:
                TT('pool', slab[S_QS][:, :, u0:u0 + C].rearrange("p (h t) c -> p h t c", t=2),
                   slab[S_Q][:, :, u0:u0 + C].rearrange("p (h t) c -> p h t c", t=2),
                   qdec[:, :, 0:C].unsqueeze(2).to_broadcast([128, 4, 2, C]), ALU.mult,
                   [('slab', S_Q), 'qdec'], [('slab', S_QS)])
            for hf in range(2):
                wi = wload('rv%d' % hf)
                for tt in range(NT):
                    b0 = nb()
                    pt = pts[tt]
                    for kt in range(8):
                        MM(banks[b0][0:pt, :], slab[S_XN][:, kt, tt * 128:tt * 128 + pt], wsl[wi][:, kt, :], kt == 0, kt == 7,
                           [('wsl', wi), ('slab', S_XN)], b0)
                    CP('act', vtok[0:pt, tt, hf * 512:(hf + 1) * 512], banks[b0][0:pt, :], [], [bk(b0), 'vtok'])
            for hf in range(2):
                for j, b0 in proj_fm('rg%d' % hf, 4, S_XN, N):
                    t8 = hf * 4 + j
                    k = t8 % 2
                    ACT(tb[k][:, 0:N], banks[b0][:, 0:N], AF.Tanh, [], [bk(b0), ('tb', k)], scale=0.5)
                    STT(slab[S_RG][:, t8, 0:N], tb[k][:, 0:N], 1.0, banks[b0][:, 0:N], ALU.add, ALU.mult,
                        [('tb', k)], [bk(b0), ('slab', S_RG)])
            ra = mkalloc([0, 1])
            rg4 = mkalloc([0, 1, 2, 3])
            sa = mkalloc([4, 5, 6, 7])

            def gen_projB():
                for hf in range(2):
                    for j, b0 in proj_fm('sq%d' % hf, 4, S_XN, N, alloc=sa):
                        CP('act', slab[S_SQ][:, hf * 4 + j, 0:N], banks[b0][:, 0:N], [], [bk(b0), ('slab', S_SQ)])
                        yield
                for j, b0 in proj_fm('skd', 2, S_XN, N, alloc=sa):
                    kv = j
                    for var in range(2):
                        lo = var * 64
                        CP('act' if var == 0 else 'dve', KT[lo:lo + 64, kv, var, 128:128 + N], banks[b0][lo:lo + 64, 0:N], [], [bk(b0), 'KT'])
                    yield
                wi = wload('skv')
                for tt in range(NT):
                    b0 = sa()
                    pt = pts[tt]
                    for kt in range(8):
                        MM(banks[b0][0:pt, 0:256], slab[S_XN][:, kt, tt * 128:tt * 128 + pt], wsl[wi][:, kt, 0:256], kt == 0, kt == 7,
                           [('wsl', wi), ('slab', S_XN)], b0)
                    for kv in range(2):
                        CP('act', VP[0:pt, 1 + tt, kv, 0, 0:64], banks[b0][0:pt, 128 + kv * 64:128 + (kv + 1) * 64], [], [bk(b0), 'VP'])
                        CP('dve', VP[0:pt, 1 + tt, kv, 1, 64:128], banks[b0][0:pt, 128 + kv * 64:128 + (kv + 1) * 64], [], [bk(b0), 'VP'])
                    if is_last and tt == NT - 1:
                        CP('act', f32o[0:pt, :], banks[b0][0:pt, 0:256], [], [bk(b0), 'f32o'])
                        ok, ov = (swk_p, swv_p) if kind == 'p' else (swk_s, swv_s)
                        r0 = 128 - pt
                        DMA('pool', ok[b, r0:128, :], f32o[0:pt, 0:128], 'fo1', reads=['f32o'])
                        DMA('pool', ov[b, r0:128, :], f32o[0:pt, 128:256], 'fo2', reads=['f32o'])
                    yield

            def gen_ret():
                for ui, (u0, C) in enumerate(units):
                    tt = u0 // 128
                    kdv = kd128 if C == 128 else kd64
                    kdk = 'kd128' if C == 128 else 'kd64'
                    gam = G128 if C == 128 else G64
                    kb = ktok[ui % 2]
                    kbk = ('ktok', ui % 2)
                    b0 = ra()
                    pv = banks[b0][:].bitcast(BF16)
                    for t8 in range(8):
                        TR(pv[0:C, t8 * 128:(t8 + 1) * 128], slab[S_K][:, t8, u0:u0 + C], [('slab', S_K)], b0)
                    for h in range(4):
                        if h % 2 == 0:
                            ACT(kb[0:C, h, :], pv[0:C, h * 256:(h + 1) * 256], AF.Copy, [kdk], [bk(b0), kbk], scale=kdv[0:C, h:h + 1])
                        else:
                            TS('dve', kb[0:C, h, :], pv[0:C, h * 256:(h + 1) * 256], kdv[0:C, h:h + 1], None, ALU.mult, None,
                               [kdk], [bk(b0), kbk])
                    b1 = ra()
                    for h in range(4):
                        for dt in range(2):
                            MM(banks[b1][0:C, h * 128:h * 128 + C], slab[S_K][:, 2 * h + dt, u0:u0 + C],
                               slab[S_Q][:, 2 * h + dt, u0:u0 + C], dt == 0, dt == 1, [('slab', S_K), ('slab', S_Q)], b1)
                    TT('dve', smb[0:C, :, 0:C], banks[b1][0:C, :].rearrange("p (h i) -> p h i", i=128)[:, :, 0:C],
                       dmask[0:C, :, 0:C], ALU.mult, ['dmask'], [bk(b1), 'smb'])
                    yield
                    bo = [2, 3]
                    for h in range(4):
                        for et in range(2):
                            bb = bo[h // 2]
                            o = banks[bb][:, ((h % 2) * 2 + et) * 128:((h % 2) * 2 + et) * 128 + C]
                            MM(o, vtok[0:C, tt, h * 256 + et * 128:h * 256 + (et + 1) * 128], smb[0:C, h, 0:C], True, False,
                               ['vtok', 'smb'], bb)
                            for dt in range(2):
                                MM(o, stb[:, h, dt, et * 128:(et + 1) * 128], slab[S_QS][:, 2 * h + dt, u0:u0 + C], False, dt == 1,
                                   ['stb', ('slab', S_QS)], bb)
                    for h in range(4):
                        b3 = ra()
                        for dt in range(2):
                            MM(banks[b3][:, dt * 256:(dt + 1) * 256], kb[0:C, h, dt * 128:(dt + 1) * 128],
                               vtok[0:C, tt, h * 256:(h + 1) * 256], True, True, [kbk, 'vtok'], b3)
                        sth = st[:, h].rearrange("p t e -> p (t e)")
                        STT(sth, sth, gam[h], banks[b3][:], ALU.mult, ALU.add, ['st'], [bk(b3), 'st'])
                    CP('pool', stb[:].rearrange("p h t e -> p (h t e)"), st[:].rearrange("p h t e -> p (h t e)"), ['st'], ['stb'])
                    yield
                    for i2 in range(2):
                        ACT(osq[:, i2 * 4:(i2 + 1) * 4, 0:C], banks[bo[i2]][:].rearrange("p (t i) -> p t i", i=128)[:, :, 0:C],
                            AF.Square, [], [bk(bo[i2]), 'osq'])
                    b2 = ra()
                    for h in range(4):
                        for et in range(2):
                            MM(banks[b2][:, h * 128:h * 128 + C], ones[:], osq[:, 2 * h + et, 0:C], et == 0, et == 1, ['ones', 'osq'], b2)
                    ACT(rsd[:, :, 0:C], banks[b2][:].rearrange("p (h i) -> p h i", i=128)[:, :, 0:C], AF.Ln, ['epsb'], [bk(b2), 'rsd'],
                        bias=epsb[:, 1:2])
                    ACT(rsd[:, :, 0:C], rsd[:, :, 0:C], AF.Exp, ['rsd'], ['rsd'], scale=-0.5)
                    yield
                    for i2 in range(2):
                        TT('dve', u1[:, i2 * 4:(i2 + 1) * 4, 0:C].rearrange("p (h t) c -> p h t c", t=2),
                           banks[bo[i2]][:].rearrange("p (h t i) -> p h t i", t=2, i=128)[:, :, :, 0:C],
                           rsd[:, i2 * 2:(i2 + 1) * 2, 0:C].unsqueeze(2).to_broadcast([128, 2, 2, C]), ALU.mult,
                           ['rsd'], [bk(bo[i2]), 'u1'])
                    STT(slab[S_RG][:, :, u0:u0 + C], u1[:, :, 0:C], 8.0, slab[S_RG][:, :, u0:u0 + C], ALU.mult, ALU.mult,
                        ['u1', ('slab', S_RG)], [('slab', S_RG)])
                    yield
                if is_last:
                    ro = rst_p if kind == 'p' else rst_s
                    DMA('pool', ro[b].rearrange("h (t p) e -> p h t e", p=128), st[:], 'sto', reads=['st'])

            def gen_swa():
                nch = N // 64
                for ci in range(nch):
                    n_glob = first_chunk + ci
                    c = ci // 2
                    q0 = ci * 64
                    if ci % 2 == 0:
                        tiles = [(128 * c, c, 0), (128 + 128 * c, c + 1, 1)]
                        if n_glob == 0:
                            tiles = tiles[1:]
                    else:
                        tiles = [(128 + 128 * c, c + 1, 2), (128 * c, c, 3)]
                        if n_glob == 1:
                            tiles = tiles[:1]
                    allebs = []
                    for kv in range(2):
                        ebs = []
                        for xi, (kc, vpi, ebv) in enumerate(tiles):
                            b0 = sa()
                            for par in range(2):
                                o_ = banks[b0][:, par * 256:(par + 1) * 256]
                                MM(o_, KT[:, kv, par, kc:kc + 128],
                                   slab[S_SQ][:, kv * 4:kv * 4 + 4, q0:q0 + 64], True, False, ['KT', ('slab', S_SQ)], b0)
                                MM(o_, ident[:], EB[ebv][:, kv * 8 + par * 4:kv * 8 + par * 4 + 4, :], False, True,
                                   ['ident', ('EB', ebv)], b0)
                            bi = (ci % 2) * 4 + kv * 2 + xi
                            ACT(eb[bi][:], banks[b0][:], AF.Exp, [], [bk(b0), ('eb', bi)], scale=0.125)
                            ebs.append((bi, vpi))
                        allebs.append(ebs)
                        yield
                    bpo = sa()
                    bpd = sa()
                    for kv in range(2):
                        ebs = allebs[kv]
                        nmm = len(ebs) * 2
                        for which, bb in ((0, bpo), (1, bpd)):
                            k = 0
                            for (bi, vpi) in ebs:
                                for par in range(2):
                                    lhs = VP[:, vpi, kv, par, :] if which == 0 else onesp[:, par, :]
                                    MM(banks[bb][:, kv * 256:(kv + 1) * 256], lhs, eb[bi][:, par * 256:(par + 1) * 256],
                                       k == 0, k == nmm - 1, ['VP', 'onesp', ('eb', bi)], bb)
                                    k += 1
                    TT('dve', den[:].rearrange("p (t q) -> p t q", q=64), banks[bpd][:].rearrange("p (t q) -> p t q", q=64),
                       esink[:].unsqueeze(2).to_broadcast([128, 8, 64]), ALU.add, ['esink'], [bk(bpd), 'rt2'])
                    ACT(rden[:], den[:], AF.Ln, ['rt2'], ['rt3'])
                    ACT(rden[:], rden[:], AF.Exp, ['rt3'], ['rt3'], scale=-1.0)
                    TT('dve', slab[S_SWA][:, :, q0:q0 + 64], banks[bpo][:].rearrange("p (t q) -> p t q", q=64),
                       rden[:].rearrange("p (t q) -> p t q", q=64), ALU.mult, ['rt3'], [bk(bpo), ('slab', S_SWA)])
                    yield
                if N == 512:
                    CP('pool', KT[:, :, :, 0:128], KT[:, :, :, 512:640], ['KT'], ['KT'])
                    CP('pool', VP[:, 0], VP[:, 4], ['VP'], ['VP'])

            def gen_gates():
                for gi, (nm, dst) in enumerate((('ga', S_Q), ('gb', S_K))):
                    for hf in range(2):
                        for j, b0 in proj_fm('%s%d' % (nm, hf), 4, S_XN, N, alloc=rg4):
                            ACT(slab[dst][:, hf * 4 + j, 0:N], banks[b0][:, 0:N], AF.Tanh, [], [bk(b0), ('slab', dst)], scale=0.5)
                            yield
                for hf in range(2):
                    for j, b0 in proj_fm('w_ret_out%d' % hf, 4, S_RG, N, alloc=rg4):
                        t8 = hf * 4 + j
                        STT(slab[S_Q][:, t8, 0:N], slab[S_Q][:, t8, 0:N], 1.0, banks[b0][:, 0:N], ALU.add, ALU.mult,
                            [('slab', S_Q)], [bk(b0), ('slab', S_Q)])
                        yield

            def chain(*gs):
                for g in gs:
                    for _ in g:
                        yield

            streams = [[chain(gen_ret(), gen_gates()), RR_A], [chain(gen_projB(), gen_swa()), RR_B]]
            while streams:
                for sg in list(streams):
                    for _ in range(sg[1]):
                        try:
                            next(sg[0])
                        except StopIteration:
                            streams.remove(sg)
                            break

            if dbg == 'eb' and N == 512:
                for v in range(4):
                    DMA('pool', ydst[v * 128:(v + 1) * 128, :], EB[v][:].rearrange("p h q -> p (h q)"), ('xo', xoff[0] + 0), reads=[('EB', v)])
                return
            if dbg in ('swa', 'ret'):
                src = slab[S_SWA] if dbg == 'swa' else slab[S_RG]
                nt_ = min(N, 128)
                for t8 in range(8):
                    for hh in range(nt_ // 64):
                        DMA('pool', ydst[hh * 64:(hh + 1) * 64, t8 * 128:(t8 + 1) * 128].rearrange("tok p -> p tok"),
                            src[:, t8, hh * 64:(hh + 1) * 64], ('xo', xoff[0] + 0), reads=[('slab', S_SWA), ('slab', S_RG)], slow=True)
                return
            for hf in range(2):
                for j, b0 in proj_fm('w_swa_out%d' % hf, 4, S_SWA, N):
                    t8 = hf * 4 + j
                    k = t8 % 2
                    STT(sf[k][:, 0:N], slab[S_K][:, t8, 0:N], 1.0, banks[b0][:, 0:N], ALU.add, ALU.mult,
                        [('slab', S_K)], [bk(b0), 'rt%d' % k])
                    TT('pool', slab[S_Q][:, t8, 0:N], slab[S_Q][:, t8, 0:N], sf[k][:, 0:N], ALU.add,
                       [('slab', S_Q), 'rt%d' % k], [('slab', S_Q)])

            pre = {}

            def proj_tm_resid(wn, src_slab, scale, next_g=None, mid=None):
                wis = [wload('%s%d' % (wn, hf)) for hf in range(2)]
                for tt in range(NT):
                    pt = pts[tt]
                    for hf in range(2):
                        wi = wis[hf]
                        b0 = nb()
                        for kt in range(8):
                            MM(banks[b0][0:pt, :], slab[src_slab][:, kt, tt * 128:tt * 128 + pt], wsl[wi][:, kt, :], kt == 0, kt == 7,
                               [('wsl', wi), ('slab', src_slab)], b0)
                        xa = XT[xoff[0] + tt][0:pt, hf * 512:(hf + 1) * 512]
                        STT(xa, banks[b0][0:pt, :], scale, xa, ALU.mult, ALU.add, [('xt', xoff[0] + tt)], [bk(b0), ('xt', xoff[0] + tt)])
                    if next_g is not None:
                        norm_a(tt, XT[xoff[0] + tt][0:pt, :], ('xt', xoff[0] + tt), pt, tt)
                        if tt >= 2:
                            norm_b(tt - 2, pts[tt - 2], next_g, S_XN)
                if next_g is not None:
                    for t_ in range(max(0, NT - 2), NT):
                        if t_ == NT - 1 and mid is not None and N == 512:
                            mid()
                        norm_b(t_, pts[t_], next_g, S_XN)

            NS = 384

            def cq_mid():
                wi = wload('w_cq0')
                pre['cq_wi'] = wi
                pre['cq_b'] = []
                for j in range(2):
                    b0 = nb()
                    pre['cq_b'].append(b0)
                    for kt in range(8):
                        MM(banks[b0][:, 0:NS], wsl[wi][:, kt, j * 128:(j + 1) * 128], slab[S_XN][:, kt, 0:NS], kt == 0, kt == 7,
                           [('wsl', wi), ('slab', S_XN)], b0)

            def gu_mid():
                wi = wload('gu0')
                pre['gu_wi'] = wi
                bg = nb()
                bu = nb()
                pre['gu_b'] = (bg, bu)
                for (bb, c0) in ((bg, 0), (bu, 256)):
                    for kt in range(8):
                        MM(banks[bb][:, 0:NS], wsl[wi][:, kt, c0:c0 + 128], slab[S_XN][:, kt, 0:NS], kt == 0, kt == 7,
                           [('wsl', wi), ('slab', S_XN)], bb)

            proj_tm_resid('w_mix_out', S_Q, 0.5, None if dbg == 'mix' else 'g_cross', mid=cq_mid)

            def dbg_store():
                for tt in range(NT):
                    DMA('pool', ydst[tt * 128:tt * 128 + pts[tt], :], XT[xoff[0] + tt][0:pts[tt], :], ('xo', xoff[0] + tt), reads=[('xt', xoff[0] + tt)])
            if dbg == 'mix':
                dbg_store()
                return

            if nxt is not None:
                xoff[0] ^= 4
                pro_load(*nxt)
                xoff[0] ^= 4
            def cross_finish(h):
                ce = ceb[h]
                cks = [('eb', 2 * h), ('eb', 2 * h + 1)]
                bd = nb()
                for kt2 in range(2):
                    MM(banks[bd][:, 0:N], ones[:], ce[:, kt2, 0:N], kt2 == 0, kt2 == 1, ['ones'] + cks, bd)
                ACT(rt[h][:, 0:N], banks[bd][:, 0:N], AF.Ln, [], [bk(bd), 'rt%d' % h])
                ACT(rt[h][:, 0:N], rt[h][:, 0:N], AF.Exp, ['rt%d' % h], ['rt%d' % h], scale=-1.0)
                for dt in range(2):
                    b2 = nb()
                    for kt2 in range(2):
                        MM(banks[b2][:, 0:N], mvt[:, kt2, h * 256 + dt * 128:h * 256 + (dt + 1) * 128], ce[:, kt2, 0:N],
                           kt2 == 0, kt2 == 1, ['mvt'] + cks, b2)
                    TT('dve', slab[S_K][:, h * 2 + dt, 0:N], banks[b2][:, 0:N], rt[h][:, 0:N], ALU.mult, ['rt%d' % h],
                       [bk(b2), ('slab', S_K)])

            for hf in range(2):
                if hf == 0 and 'cq_wi' in pre:
                    wi_c = pre['cq_wi']
                else:
                    wi_c = wload('w_cq%d' % hf)
                for j in range(4):
                    t8 = hf * 4 + j
                    if hf == 0 and j < 2 and 'cq_b' in pre:
                        b0 = pre['cq_b'][j]
                        for kt in range(8):
                            MM(banks[b0][:, NS:N], wsl[wi_c][:, kt, j * 128:(j + 1) * 128], slab[S_XN][:, kt, NS:N], kt == 0, kt == 7,
                               [('wsl', wi_c), ('slab', S_XN)], b0)
                    else:
                        b0 = nb()
                        for kt in range(8):
                            MM(banks[b0][:, 0:N], wsl[wi_c][:, kt, j * 128:(j + 1) * 128], slab[S_XN][:, kt, 0:N], kt == 0, kt == 7,
                               [('wsl', wi_c), ('slab', S_XN)], b0)
                    CP('act', slab[S_Q][:, t8, 0:N], banks[b0][:, 0:N], [], [bk(b0), ('slab', S_Q)])
                    if t8 % 2 == 1:
                        h = t8 // 2
                        ce = ceb[h]
                        cks = [('eb', 2 * h), ('eb', 2 * h + 1)]
                        for kt2 in range(2):
                            b1 = nb()
                            for dt in range(2):
                                MM(banks[b1][:, 0:N], mkT[:, h * 2 + dt, kt2 * 128:(kt2 + 1) * 128], slab[S_Q][:, h * 2 + dt, 0:N],
                                   dt == 0, dt == 1, ['mkT', ('slab', S_Q)], b1)
                            ACT(ce[:, kt2, 0:N], banks[b1][:, 0:N], AF.Exp, [], [bk(b1)] + cks, scale=1.0 / 16.0)
                        if h >= 1:
                            cross_finish(h - 1)
            cross_finish(3)
            proj_tm_resid('w_co', S_K, 1.0, None if dbg == 'cross' else 'g_ffn', mid=gu_mid)
            if dbg == 'cross':
                dbg_store()
                return

            for gi in range(11):
                if gi == 0 and 'gu_wi' in pre:
                    wi = pre['gu_wi']
                else:
                    wi = wload('gu%d' % gi)
                for j in range(2):
                    if gi == 0 and j == 0 and 'gu_b' in pre:
                        bg, bu = pre['gu_b']
                        lo = NS
                    else:
                        bg = nb()
                        bu = nb()
                        lo = 0
                    for kt in range(8):
                        MM(banks[bg][:, lo:N], wsl[wi][:, kt, j * 128:(j + 1) * 128], slab[S_XN][:, kt, lo:N], kt == 0, kt == 7,
                           [('wsl', wi), ('slab', S_XN)], bg)
                    for kt in range(8):
                        MM(banks[bu][:, lo:N], wsl[wi][:, kt, 256 + j * 128:256 + (j + 1) * 128], slab[S_XN][:, kt, lo:N], kt == 0, kt == 7,
                           [('wsl', wi), ('slab', S_XN)], bu)
                    hj = gi * 2 + j
                    k = hj % 2
                    ACT(tb[k][:, 0:N], banks[bg][:, 0:N], AF.Tanh, [], [bk(bg), ('tb', k)], scale=0.5)
                    STT(sf[k][:, 0:N], tb[k][:, 0:N], 1.0, banks[bg][:, 0:N], ALU.add, ALU.mult, [('tb', k)], [bk(bg), 'rt%d' % k])
                    hs = 1 + hj // 8
                    TT('dve', slab[hs][:, hj % 8, 0:N], sf[k][:, 0:N], banks[bu][:, 0:N], ALU.mult, ['rt%d' % k],
                       [bk(bu), ('slab', hs)])
            npend = []
            if nxt is not None:
                xo2 = xoff[0] ^ 4

                def mk_a(t_):
                    return lambda: norm_a(t_, XT[xo2 + t_][:], ('xt', xo2 + t_), 128, t_)

                def mk_b(t_):
                    return lambda: norm_b(t_, 128, 'g_attn', S_XN)
                mk_a(0)()
                mk_a(1)()
                mk_a(2)()
                npend = [mk_b(0), mk_a(3), mk_b(1), mk_b(2), mk_b(3)]
            for hf in range(2):
                bt = [nb() for _ in range(NT)]
                for sbi, (k0, nk) in enumerate(((0, 8), (8, 8), (16, 6))):
                    if npend and not (hf == 0 and sbi == 0):
                        npend.pop(0)()
                        if len(npend) == 4:
                            npend.pop(0)()
                    wi = wload('dn%d_%d' % (hf, sbi))
                    for tt in range(NT):
                        pt = pts[tt]
                        for kl in range(nk):
                            kt = k0 + kl
                            hs = 1 + kt // 8
                            MM(banks[bt[tt]][0:pt, :], slab[hs][:, kt % 8, tt * 128:tt * 128 + pt], wsl[wi][:, kl, :],
                               kt == 0, kt == 21, [('wsl', wi), ('slab', hs)], bt[tt])
                for tt in range(NT):
                    pt = pts[tt]
                    xa = XT[xoff[0] + tt][0:pt, hf * 512:(hf + 1) * 512]
                    STT(xa, banks[bt[tt]][0:pt, :], 0.5, xa, ALU.mult, ALU.add, [('xt', xoff[0] + tt)], [bk(bt[tt]), ('xt', xoff[0] + tt)])
            while npend:
                npend.pop(0)()

            for tt in range(NT):
                pt = pts[tt]
                xa = XT[xoff[0] + tt][0:pt, :]
                col = 4 + tt
                ACT(junk[0:pt, :], xa, AF.Square, [('xt', xoff[0] + tt)], ['u1', ('ssq', col)], accum=ssq[0:pt, col:col + 1])
                ACT(rstd[0:pt, col:col + 1], ssq[0:pt, col:col + 1], AF.Ln, [('ssq', col), 'epsb'], [('rstd', col)], scale=1.0 / D, bias=epsb[0:pt, 0:1])
                ACT(rstd[0:pt, col:col + 1], rstd[0:pt, col:col + 1], AF.Exp, [('rstd', col)], [('rstd', col)], scale=-0.5)
                STT(xa, xa, rstd[0:pt, col:col + 1], gfin[0:pt, :], ALU.mult, ALU.mult, [('xt', xoff[0] + tt), ('rstd', col), 'gfin'], [('xt', xoff[0] + tt)])
                DMA('pool', ydst[tt * 128:tt * 128 + pt, :], xa, ('xo', xoff[0] + tt), reads=[('xt', xoff[0] + tt)])

        P.strict = STRICT_SAME_ENGINE
        gblk = 0
        for b in range(2):
            xoff[0] = 4 * (gblk % 2) ^ 4
            mem_prompt(b)
            MEMSET('dve', st[:], 0.0, ['st'])
            MEMSET('pool', stb[:], 0.0, ['stb'])
            for blk in range(NBLK):
                xoff[0] = 4 * (gblk % 2)
                nxt = None
                if blk + 1 < NBLK and not dbg:
                    nxt = (xp[b, (blk + 1) * 512:(blk + 2) * 512, :], 512, (blk + 1) * 512)
                block(xp[b, blk * 512:(blk + 1) * 512, :], y_p[b, blk * 512:(blk + 1) * 512, :], 512, blk * 512,
                      [(u * 128, 128) for u in range(4)], blk * 8, blk == NBLK - 1, 'p', b, pro_done=(blk > 0 and not dbg), nxt=nxt)
                gblk += 1
        for b in range(2):
            xoff[0] = 4 * (gblk % 2) ^ 4
            mem_sample(b)
            xoff[0] = 4 * (gblk % 2)
            gblk += 1
            DMA('pool', st[:], cret[b].rearrange("h (t p) e -> p h t e", p=128), 'stl', writes=['st'])
            CP('pool', stb[:].rearrange("p h t e -> p (h t e)"), st[:].rearrange("p h t e -> p (h t e)"), ['st'], ['stb'])
            DMA('pool', rt[0][:, 0:128], cswk[b], 'ckl', writes=['rt0'])
            DMA('pool', rt[1][:, 0:128], cswv[b], 'cvl', writes=['rt1'])
            kd = junk[:, 0:256].rearrange("p (kv du d) -> p kv du d", kv=2, du=2)
            CP('dve', kd, rt[0][:, 0:128].rearrange("p (kv d) -> p kv d", kv=2).unsqueeze(2).to_broadcast([128, 2, 2, 64]),
               ['rt0'], ['u1'])
            b0 = nb()
            pv = banks[b0][:].bitcast(BF16)
            for kv in range(2):
                TR(pv[:, kv * 128:(kv + 1) * 128], junk[:, kv * 128:(kv + 1) * 128], ['u1'], b0)
            for kv in range(2):
                for var in range(2):
                    lo = var * 64
                    CP('act', KT[lo:lo + 64, kv, var, 0:128], pv[lo:lo + 64, kv * 128:(kv + 1) * 128], [], [bk(b0), 'KT'])
            for kv in range(2):
                CP('dve', VP[:, 0, kv, 0, 0:64], rt[1][:, kv * 64:(kv + 1) * 64], ['rt1'], ['VP'])
                CP('dve', VP[:, 0, kv, 1, 64:128], rt[1][:, kv * 64:(kv + 1) * 64], ['rt1'], ['VP'])
            DMA('pool', swk_s[b, 0:64, :], cswk[b, 64:128, :], 'pk')
            DMA('pool', swv_s[b, 0:64, :], cswv[b, 64:128, :], 'pvv')
            block(xs[b], y_s[b], 64, S, [(0, 64)], 64, True, 's', b)

        P.emit(nc, es)
    return nc, consts


_CACHE = {}


DBG = False


def _get(S):
    if S not in _CACHE:
        _CACHE[S] = build(S, DBG)
    return _CACHE[S]


def kernel(x_prompt, x_sample, cache_ret_state, cache_swa_k, cache_swa_v, cache_mem_k, cache_mem_v,
           mem_prompt, rel_bias, g_attn, w_in, w_ret_out, w_swa_out, w_mix_out, swa_sinks,
           g_cross, g_mem, w_cq, w_mk, w_mv, w_co, g_ffn, w_gate, w_up, w_down, g_final):
    f = lambda a: np.ascontiguousarray(np.asarray(a, dtype=np.float32))
    S = x_prompt.shape[1]
    nc, consts = _get(S)
    ncore = x_prompt.shape[0] // 2
    shared = {'relb': f(rel_bias), 'sinks': f(swa_sinks).reshape(16),
              'g_attn': f(g_attn).reshape(D), 'g_cross': f(g_cross).reshape(D), 'g_mem': f(g_mem).reshape(D),
              'g_ffn': f(g_ffn).reshape(D), 'g_final': f(g_final).reshape(D),
              'w_in': f(w_in)[0], 'w_ret_out': f(w_ret_out)[0], 'w_swa_out': f(w_swa_out)[0],
              'w_mix_out': f(w_mix_out)[0], 'w_cq': f(w_cq)[0], 'w_mk': f(w_mk)[0], 'w_mv': f(w_mv)[0],
              'w_co': f(w_co)[0], 'w_gate': f(w_gate)[0], 'w_up': f(w_up)[0], 'w_down': f(w_down)[0]}
    for n, a in consts.items():
        shared['c_' + n] = a
    in_maps = []
    for c in range(ncore):
        sl = slice(2 * c, 2 * c + 2)
        m = dict(shared)
        m['xp'] = f(x_prompt[sl]); m['xs'] = f(x_sample[sl])
        m['cret'] = f(cache_ret_state[0, sl])
        m['cswk'] = f(cache_swa_k[0, sl]).reshape(2, 128, 128)
        m['cswv'] = f(cache_swa_v[0, sl]).reshape(2, 128, 128)
        m['cmk'] = f(cache_mem_k[0, sl]).reshape(2, 256, D)
        m['cmv'] = f(cache_mem_v[0, sl]).reshape(2, 256, D)
        m['memp'] = f(mem_prompt[sl])
        in_maps.append(m)
    res = run_bass_kernel_spmd(nc, in_maps, core_ids=list(range(ncore)))
    R = res.results
    cat = lambda k: np.concatenate([r[k] for r in R], axis=0)
    B = 2 * ncore
    return (cat('y_p'), cat('y_s'),
            cat('rst_p')[None], cat('rst_s')[None],
            cat('swk_p').reshape(1, B, 128, 2, 64), cat('swk_s').reshape(1, B, 128, 2, 64),
            cat('swv_p').reshape(1, B, 128, 2, 64), cat('swv_s').reshape(1, B, 128, 2, 64),
            cat('mk_p').reshape(1, B, 256, 4, 256), cat('mv_p').reshape(1, B, 256, 4, 256))
```

```python
import math
from contextlib import ExitStack
import numpy as np
import ml_dtypes
import concourse.bass as bass
import concourse.mybir as mybir
from concourse.bass_utils import run_bass_kernel_spmd

F32 = mybir.dt.float32
BF16 = mybir.dt.bfloat16
AF = mybir.ActivationFunctionType
ALU = mybir.AluOpType

D = 1024
SEQ = 4096
PAST = 4096
DFF = 2816
EPS = 1e-6
ENGS = ('pe', 'act', 'dve', 'pool', 'sp')
RR_A = 1
RR_B = 1
STRICT_SAME_ENGINE = False
ALLN = 10 ** 9


class _Op:
    __slots__ = ('fn', 'waits', 'signal', 'dma', 'sigval')

    def __init__(self, fn, dma):
        self.fn = fn
        self.waits = []
        self.signal = False
        self.dma = dma
        self.sigval = 0


class Prog:
    def __init__(self):
        self.ops = {e: [] for e in ENGS}
        self.lastw = {}
        self.rd = {}
        self.seen = {e: {} for e in ENGS}
        self.dcount = {}
        self.strict = True

    def add(self, eng, fn, reads=(), writes=(), dma=None):
        ops = self.ops[eng]
        idx = len(ops)
        op = _Op(fn, dma)
        seen = self.seen[eng]
        cand = []
        for k in reads:
            w = self.lastw.get(k)
            if w is not None:
                cand.append((w, True))
        for k in writes:
            w = self.lastw.get(k)
            if w is not None:
                cand.append((w, False))
            r = self.rd.get(k)
            if r:
                for ref in r.values():
                    cand.append((ref, False))
        for ref, is_raw in cand:
            if ref[0] == 'c':
                _, e2, i2 = ref
                if e2 == eng and dma is None:
                    if eng == 'pe' or (not is_raw and not self.strict):
                        continue
                if seen.get(e2, -1) >= i2:
                    continue
                seen[e2] = i2
                op.waits.append(ref)
                self.ops[e2][i2].signal = True
            else:
                _, key, n = ref
                sk = ('d', key)
                if seen.get(sk, 0) >= n:
                    continue
                seen[sk] = n
                op.waits.append(ref)
        if dma is None:
            ref = ('c', eng, idx)
            rk = eng
        else:
            n = self.dcount.get(dma, 0) + 1
            self.dcount[dma] = n
            ref = ('d', dma, n)
            rk = ('d', dma)
        for k in writes:
            self.lastw[k] = ref
            self.rd[k] = {}
        for k in reads:
            self.rd.setdefault(k, {})[rk] = ref
        ops.append(op)
        return ref

    def emit(self, nc, es):
        sems = {e: es.enter_context(nc.semaphore('s_' + e)) for e in ENGS}
        dsem = {}
        for i, k in enumerate(self.dcount):
            dsem[k] = es.enter_context(nc.semaphore('d%d' % i))
        for e in ENGS:
            c = 0
            for op in self.ops[e]:
                if op.signal:
                    c += 1
                    op.sigval = c
        block = es.enter_context(nc.Block())
        prog = self

        def run(eng_name, e):
            for op in prog.ops[eng_name]:
                for ref in op.waits:
                    if ref[0] == 'c':
                        e.wait_ge(sems[ref[1]], prog.ops[ref[1]][ref[2]].sigval)
                    else:
                        n = ref[2]
                        if n == ALLN:
                            n = prog.dcount[ref[1]]
                        e.wait_ge(dsem[ref[1]], 16 * n)
                ins = op.fn(e)
                if op.dma is not None:
                    ins.then_inc(dsem[op.dma], 16)
                elif op.signal:
                    ins.then_inc(sems[eng_name], 1)
            if eng_name == 'pool':
                for k, n in prog.dcount.items():
                    e.wait_ge(dsem[k], 16 * n)

        @block.tensor
        def _(e):
            run('pe', e)

        @block.scalar
        def _(e):
            run('act', e)

        @block.vector
        def _(e):
            run('dve', e)

        @block.gpsimd
        def _(e):
            run('pool', e)

        @block.sync
        def _(e):
            run('sp', e)


def _consts(S):
    c = {}
    half = 128
    inv = (np.float32(10000.0) ** (-np.arange(half, dtype=np.float32) / np.float32(half))).astype(np.float32)
    pos = np.concatenate([np.arange(S, dtype=np.float32), PAST + np.arange(64, dtype=np.float32)]).astype(np.float32)
    ang = (pos[None, :] * inv[:, None]).astype(np.float32)
    c['cosT'] = np.cos(ang).astype(np.float32)
    c['sinT'] = np.sin(ang).astype(np.float32)
    lg = np.log(np.float32(1.0) - np.float32(2.0) ** (-5.0 - np.arange(4, dtype=np.float32))).astype(np.float32)
    idx = np.arange(128, dtype=np.float32)
    i = idx[None, :]
    j = idx[:, None]
    same = (np.floor(i / 64) == np.floor(j / 64))
    lower = (np.floor(i / 64) > np.floor(j / 64))
    dm = np.zeros((128, 4, 128), np.float32)
    for h in range(4):
        intra = np.exp(lg[h] * np.abs(i - j)).astype(np.float32)
        cross = (np.exp(lg[h] * (np.mod(i, 64) + 1.0)).astype(np.float32)
                 * np.exp(lg[h] * (63.0 - np.mod(j, 64))).astype(np.float32)).astype(np.float32)
        dm[:, h, :] = np.where(same, intra, np.where(lower, cross, 0.0)) / np.float32(16.0)
    c['dmask'] = dm
    qd = np.zeros((128, 4, 128), np.float32)
    kd128 = np.zeros((128, 4), np.float32)
    kd64 = np.zeros((128, 4), np.float32)
    g128 = []
    g64 = []
    for h in range(4):
        g64h = np.exp(lg[h] * np.float32(64.0)).astype(np.float32)
        qd64 = np.exp(lg[h] * (np.arange(64, dtype=np.float32) + 1.0)).astype(np.float32)
        kdec64 = np.exp(lg[h] * (63.0 - np.arange(64, dtype=np.float32))).astype(np.float32)
        qd[:, h, :64] = qd64[None, :]
        qd[:, h, 64:] = (qd64 * g64h)[None, :]
        kd64[:64, h] = kdec64 / 16.0
        kd128[:64, h] = kdec64 * g64h / 16.0
        kd128[64:, h] = kdec64 / 16.0
        g64.append(float(g64h))
        g128.append(float(np.float32(g64h * g64h)))
    c['qdec'] = qd
    c['kd128'] = kd128
    c['kd64'] = kd64
    c['ident'] = np.eye(128, dtype=np.float32).astype(ml_dtypes.bfloat16)
    q = np.arange(64, dtype=np.int32)[:, None]
    jj = np.arange(192, dtype=np.int32)[None, :]
    rel = (jj - 128) - q
    n = np.abs(rel)
    large = 8 + (np.log(np.maximum(n, 1).astype(np.float32) / 8) / math.log(128 / 8) * 8).astype(np.int32)
    large = np.minimum(large, 15)
    bucket = np.where(rel > 0, 16, 0) + np.where(n < 8, n, large)
    kk = np.arange(128)
    maps = [kk, np.where(kk < 64, 128 + kk, -1), 64 + kk, np.where(kk >= 64, kk - 64, -1)]
    oh = np.zeros((4, 32, 64, 128), np.float32)
    for v, m in enumerate(maps):
        for k in range(128):
            if m[k] >= 0:
                oh[v, bucket[:, m[k]], np.arange(64), k] = 1.0
    c['oh'] = oh.astype(ml_dtypes.bfloat16)
    return c, g128, g64


def _wchunks():
    ch = {}
    offs = dict(rq=0, rk=1024, rv=2048, rg=3072, sq=4096, sk=5120, sv=5248, ga=5376, gb=6400)
    for nm in ('rq', 'rk', 'rv', 'rg', 'sq', 'ga', 'gb'):
        for i in range(2):
            ch['%s%d' % (nm, i)] = ('w_in', 0, 8, [(offs[nm] + 512 * i, 512)])
    ch['skd'] = ('w_in', 0, 8, [(5120, 64), (5120, 64), (5184, 64), (5184, 64)])
    ch['skv'] = ('w_in', 0, 8, [(5120, 256)])
    for nm in ('w_ret_out', 'w_swa_out', 'w_mix_out', 'w_cq', 'w_co', 'w_mk', 'w_mv'):
        for i in range(2):
            ch['%s%d' % (nm, i)] = (nm, 0, 8, [(512 * i, 512)])
    for i in range(11):
        ch['gu%d' % i] = ('w_gu', 0, 8, [(256 * i, 256)])
    for hf in range(2):
        for sb, (k0, nk) in enumerate(((0, 8), (8, 8), (16, 6))):
            ch['dn%d_%d' % (hf, sb)] = ('w_down', k0, nk, [(512 * hf, 512)])
    return ch


WNAMES = ('w_in', 'w_ret_out', 'w_swa_out', 'w_mix_out', 'w_cq', 'w_mk', 'w_mv', 'w_co', 'w_gate', 'w_up', 'w_down')
WSHAPES = dict(w_in=(D, 7424), w_ret_out=(D, D), w_swa_out=(D, D), w_mix_out=(D, D), w_cq=(D, D), w_mk=(D, D),
               w_mv=(D, D), w_co=(D, D), w_gate=(D, DFF), w_up=(D, DFF), w_down=(DFF, D))


def build(S, dbg=False):
    nc = bass.Bass("TRN2", target_bir_lowering=False)
    P = Prog()
    consts, G128, G64 = _consts(S)
    NBLK = S // 512

    def din(name, shape, dt=F32):
        return nc.dram_tensor(name, list(shape), dt, kind="ExternalInput").ap()

    def dout(name, shape):
        return nc.dram_tensor(name, list(shape), F32, kind="ExternalOutput").ap()

    xp = din('xp', [2, S, D]); xs = din('xs', [2, 64, D])
    cret = din('cret', [2, 4, 256, 256]); cswk = din('cswk', [2, 128, 128]); cswv = din('cswv', [2, 128, 128])
    cmk = din('cmk', [2, 256, D]); cmv = din('cmv', [2, 256, D]); memp = din('memp', [2, 256, D])
    relb = din('relb', [32, 16]); sinks = din('sinks', [16])
    gvec = {n: din(n, [D]) for n in ('g_attn', 'g_cross', 'g_mem', 'g_ffn', 'g_final')}
    W = {n: din(n, WSHAPES[n]) for n in WNAMES}
    cd = {}
    for n, a in consts.items():
        cd[n] = din('c_' + n, a.shape, BF16 if a.dtype == ml_dtypes.bfloat16 else F32)

    y_p = dout('y_p', [2, S, D]); y_s = dout('y_s', [2, 64, D])
    rst_p = dout('rst_p', [2, 4, 256, 256]); rst_s = dout('rst_s', [2, 4, 256, 256])
    swk_p = dout('swk_p', [2, 128, 128]); swk_s = dout('swk_s', [2, 128, 128])
    swv_p = dout('swv_p', [2, 128, 128]); swv_s = dout('swv_s', [2, 128, 128])
    mk_p = dout('mk_p', [2, 256, D]); mv_p = dout('mv_p', [2, 256, D])

    chunks = _wchunks()
    wb = {}
    for nm, (src, k0, nk, cols) in chunks.items():
        ncol = sum(c[1] for c in cols) * (2 if src == 'w_gu' else 1)
        wb[nm] = nc.dram_tensor('wb_' + nm, [128, nk, ncol], BF16, kind="Internal").ap()

    es = ExitStack()
    with es:
        def sb(name, shape, dt=F32):
            return es.enter_context(nc.sbuf_tensor(name, list(shape), dt))

        XT = [sb('xt%d' % i, [128, D]) for i in range(8)]
        xoff = [0]
        slab = [sb('slab%d' % i, [128, 8, 512], BF16) for i in range(7)]
        S_XN, S_Q, S_K, S_QS, S_RG, S_SQ, S_SWA = range(7)
        hT = sb('hT_extra', [128, 1, 1], BF16)
        vtok = sb('vtok', [128, 4, D], BF16)
        ktok = [sb('ktok%d' % i, [128, 4, 256], BF16) for i in range(2)]
        wsl = [sb('wsl%d' % i, [128, 8, 512], BF16) for i in range(3)]
        gfin = sb('gfin', [128, D])
        gT = {n: sb('gT_' + n, [128, 8]) for n in ('g_attn', 'g_cross', 'g_mem', 'g_ffn')}
        cosb = sb('cosb', [128, 512]); sinb = sb('sinb', [128, 512])
        dmask = sb('dmask', [128, 4, 128]); qdec = sb('qdec', [128, 4, 128])
        kd128 = sb('kd128', [128, 4]); kd64 = sb('kd64', [128, 4])
        ident = sb('ident', [128, 128], BF16)
        ones = sb('ones', [128, 128], BF16)
        onesp = sb('onesp', [128, 2, 128], BF16)
        EB = [sb('EB%d' % v, [128, 16, 64], BF16) for v in range(4)]
        esink = sb('esink', [128, 8])
        tabf = sb('tabf', [128, 16]); tabh = sb('tabh', [128, 16], BF16); tabl = sb('tabl', [128, 16], BF16)
        tabr = sb('tabr', [128, 16])
        st = sb('st', [128, 4, 2, 256]); stb = sb('stb', [128, 4, 2, 256], BF16)
        KT = sb('KT', [128, 2, 2, 640], BF16)
        VP = sb('VP', [128, 5, 2, 2, 128], BF16)
        mkT = sb('mkT', [128, 8, 256], BF16)
        mvt = sb('mvt', [128, 2, D], BF16)
        xnbs = [sb('xnb%d' % i, [128, D], BF16) for i in range(2)]
        ssq = sb('ssq', [128, 8]); rstd = sb('rstd', [128, 8])
        rt = [sb('rt%d' % i, [128, 512]) for i in range(4)]
        smb = sb('smb', [128, 4, 128], BF16)
        osq = sb('osq', [128, 8, 128], BF16)
        rsd = sb('rsd', [128, 4, 128])
        u1 = sb('u1', [128, 8, 128])
        junk = u1[:].rearrange("p a b -> p (a b)").bitcast(BF16)[:, 0:D]
        xnbs.append(u1[:].rearrange("p a b -> p (a b)").bitcast(BF16)[:, D:2 * D])
        ef = [rt[0], rt[1]]
        ebb = sb('ebb', [128, 8, 512], BF16)
        eb = [ebb[:, i] for i in range(8)]
        den = rt[2]; rden = rt[3]
        ceb = [ebb[:, 2 * i:2 * i + 2] for i in range(4)]
        tb = [sb('tb%d' % i, [128, 512], BF16) for i in range(2)]
        sf = [rt[0], rt[1]]
        f32o = sb('f32o', [128, 256])
        epsb = sb('epsb', [128, 2])

        banks = [es.enter_context(nc.psum_tensor('pb%d' % i, [128, 512], F32)) for i in range(8)]
        if dbg == 'mem':
            print('SBUF bytes remaining per partition:', nc.sbuf_bytes_remaining)
        bctr = [0]

        def nb():
            i = bctr[0] % 8
            bctr[0] += 1
            return i

        def bk(i):
            return ('ps', i)

        def mkalloc(pool):
            c = [0]

            def f():
                i = pool[c[0] % len(pool)]
                c[0] += 1
                return i
            return f

        def MM(out, lhsT, rhs, start, stop, reads, bank):
            P.add('pe', lambda e: e.matmul(out, lhsT, rhs, start=start, stop=stop, skip_group_check=True),
                  reads=reads, writes=[bk(bank)])

        def TR(out, in_, reads, bank):
            P.add('pe', lambda e: e.transpose(out, in_, ident[0:in_.shape[0], 0:in_.shape[0]]),
                  reads=list(reads) + ['ident'], writes=[bk(bank)])

        def ACT(out, in_, func, reads, writes, scale=1.0, bias=0.0, accum=None):
            if accum is None:
                P.add('act', lambda e: e.activation(out, in_, func, bias=bias, scale=scale), reads=reads, writes=writes)
            else:
                P.add('act', lambda e: e.activation(out, in_, func, bias=bias, scale=scale, accum_out=accum),
                      reads=reads, writes=writes)

        def TT(eng, out, in0, in1, op, reads, writes):
            P.add(eng, lambda e: e.tensor_tensor(out, in0, in1, op), reads=reads, writes=writes)

        def STT(out, in0, scalar, in1, op0, op1, reads, writes):
            P.add('dve', lambda e: e.scalar_tensor_tensor(out, in0, scalar, in1, op0, op1), reads=reads, writes=writes)

        def TS(eng, out, in0, s1, s2, op0, op1, reads, writes):
            if s2 is None:
                P.add(eng, lambda e: e.tensor_scalar(out, in0, s1, None, op0), reads=reads, writes=writes)
            else:
                P.add(eng, lambda e: e.tensor_scalar(out, in0, s1, s2, op0, op1), reads=reads, writes=writes)

        def CP(eng, out, in_, reads, writes):
            if eng == 'act':
                P.add('act', lambda e: e.copy(out, in_), reads=reads, writes=writes)
            else:
                P.add(eng, lambda e: e.tensor_copy(out, in_), reads=reads, writes=writes)

        def MEMSET(eng, ap, val, writes):
            P.add(eng, lambda e: e.memset(ap, val), writes=writes)

        def DMA(q, out, in_, key, reads=(), writes=(), slow=False):
            if slow:
                return P.add(q, lambda e: e.dma_start(out=out, in_=in_, allow_slow_non_contiguous=True),
                             reads=reads, writes=writes, dma=key)
            return P.add(q, lambda e: e.dma_start(out=out, in_=in_), reads=reads, writes=writes, dma=key)

        cast_order = (['w_mk0', 'w_mk1', 'w_mv0', 'w_mv1'] +
                      ['rq0', 'rq1', 'rk0', 'rk1', 'rv0', 'rv1', 'rg0', 'rg1', 'sq0', 'sq1', 'skd', 'skv',
                       'ga0', 'ga1', 'gb0', 'gb1', 'w_ret_out0', 'w_ret_out1', 'w_swa_out0', 'w_swa_out1',
                       'w_mix_out0', 'w_mix_out1', 'w_cq0', 'w_cq1', 'w_co0', 'w_co1'] +
                      ['gu%d' % i for i in range(11)] +
                      ['dn%d_%d' % (h, s) for h in range(2) for s in range(3)])
        cast_idx = {nm: i for i, nm in enumerate(cast_order)}
        cast_done = [0]
        LOOK = 10

        def cast_upto(n):
            n = min(n, len(cast_order))
            while cast_done[0] < n:
                nm = cast_order[cast_done[0]]
                cast_done[0] += 1
                src, k0, nk, cols = chunks[nm]
                key = ('cast', nm)
                c0 = 0
                if src == 'w_gu':
                    col0, ncw = cols[0]
                    for wn in ('w_gate', 'w_up'):
                        sap = W[wn][k0 * 128:(k0 + nk) * 128, col0:col0 + ncw].rearrange("(kt p) n -> p kt n", p=128)
                        DMA('pool', wb[nm][:, :, c0:c0 + ncw], sap, key)
                        c0 += ncw
                else:
                    for (col0, ncw) in cols:
                        sap = W[src][k0 * 128:(k0 + nk) * 128, col0:col0 + ncw].rearrange("(kt p) n -> p kt n", p=128)
                        DMA('pool', wb[nm][:, :, c0:c0 + ncw], sap, key)
                        c0 += ncw
                P.lastw[('wb', nm)] = ('d', key, ALLN)

        cast_upto(LOOK)

        wctr = [0]

        def wload(nm):
            cast_upto(cast_idx[nm] + LOOK)
            i = wctr[0] % 3
            wctr[0] += 1
            nk = chunks[nm][2]
            ncol = wb[nm].shape[2]
            DMA('sp', wsl[i][:, 0:nk, 0:ncol], wb[nm], ('wl', i), reads=[('wb', nm)], writes=[('wsl', i)])
            return i

        def cload(dst, src, key, wkey, slow=False):
            DMA('sp', dst, src, key, writes=[wkey], slow=slow)

        cload(ident[:], cd['ident'], 'c0', 'ident')
        cload(dmask[:], cd['dmask'], 'c1', 'dmask')
        cload(qdec[:], cd['qdec'], 'c2', 'qdec')
        cload(kd128[:], cd['kd128'], 'c3', 'kd128')
        cload(kd64[:], cd['kd64'], 'c4', 'kd64')
        cload(gfin[:], gvec['g_final'].partition_broadcast(128), 'c5', 'gfin')
        for i, n in enumerate(('g_attn', 'g_cross', 'g_mem', 'g_ffn')):
            cload(gT[n][:].unsqueeze(2), gvec[n].rearrange("(kt p o) -> p kt o", p=128, o=1), 'c6%d' % i, ('gT', n), slow=True)
        MEMSET('dve', tabf[:], 0.0, ['tabf'])
        cload(tabf[0:32, :], relb, 'c7', 'tabf')
        sk2 = sinks.rearrange("(t two o) -> two t o", two=2, o=1)
        cload(esink[0:64, :].unsqueeze(2), sk2[0].partition_broadcast(64), 'c8', 'esink_a', slow=True)
        cload(esink[64:128, :].unsqueeze(2), sk2[1].partition_broadcast(64), 'c9', 'esink_b', slow=True)
        ACT(esink[:], esink[:], AF.Exp, ['esink_a', 'esink_b'], ['esink'])
        MEMSET('dve', ones[:], 1.0, ['ones'])
        MEMSET('dve', epsb[:, 0:1], EPS, ['epsb'])
        MEMSET('dve', epsb[:, 1:2], 256.0 * EPS, ['epsb'])
        MEMSET('dve', onesp[:], 0.0, ['onesp'])
        MEMSET('dve', onesp[:, 0, 0:64], 1.0, ['onesp'])
        MEMSET('dve', onesp[:, 1, 64:128], 1.0, ['onesp'])
        MEMSET('dve', KT[:], 0.0, ['KT'])
        MEMSET('dve', VP[:], 0.0, ['VP'])
        CP('dve', tabh[:], tabf[:], ['tabf'], ['tabh'])
        CP('dve', tabr[:], tabh[:], ['tabh'], ['tabr'])
        TT('dve', tabr[:], tabf[:], tabr[:], ALU.subtract, ['tabf', 'tabr'], ['tabr'])
        CP('dve', tabl[:], tabr[:], ['tabr'], ['tabl'])
        ohv = slab[S_Q][:].rearrange("p a b -> p (a b)").rearrange("p (q k) -> p q k", k=128)
        MEMSET('dve', ohv, 0.0, [('slab', S_Q)])
        for v in range(4):
            for hq in range(2):
                DMA('sp', ohv[0:32], cd['oh'][v, :, hq * 32:(hq + 1) * 32, :], 'coh', writes=[('slab', S_Q)])
                b0 = nb()
                for qq in range(32):
                    o = banks[b0][:, qq * 16:(qq + 1) * 16]
                    MM(o, ohv[:, qq, :], tabh[:], True, False, [('slab', S_Q), 'tabh'], b0)
                    MM(o, ohv[:, qq, :], tabl[:], False, True, [('slab', S_Q), 'tabl'], b0)
                src_v = banks[b0][:].rearrange("p (q kv tq par) -> p kv par tq q", q=32, kv=2, tq=4, par=2)
                dst_v = EB[v][:, :, hq * 32:(hq + 1) * 32].rearrange("p (kv par tq) q -> p kv par tq q", kv=2, par=2, tq=4)
                for kv in range(2):
                    for par in range(2):
                        for tq in range(4):
                            ACT(dst_v[:, kv, par, tq, :], src_v[:, kv, par, tq, :], AF.Copy, [], [bk(b0), ('EB', v)], scale=8.0)
        MEMSET('dve', EB[1][64:128], -400.0, [('EB', 1)])
        MEMSET('dve', EB[3][0:64], -400.0, [('EB', 3)])

        def norm_a(tt, xa, xk, pt, col):
            ACT(junk[0:pt, :], xa, AF.Square, [xk], ['u1', ('ssq', col)], accum=ssq[0:pt, col:col + 1])
            ACT(rstd[0:pt, col:col + 1], ssq[0:pt, col:col + 1], AF.Ln, [('ssq', col), 'epsb'], [('rstd', col)], scale=1.0 / D, bias=epsb[0:pt, 0:1])
            ACT(rstd[0:pt, col:col + 1], rstd[0:pt, col:col + 1], AF.Exp, [('rstd', col)], [('rstd', col)], scale=-0.5)
            TS('dve', xnbs[tt % 3][0:pt, :], xa, rstd[0:pt, col:col + 1], None, ALU.mult, None, [xk, ('rstd', col)],
               [('xnb', tt % 3)] + (['u1'] if tt % 3 == 2 else []))

        def norm_b(tt, pt, gname, dst):
            b0 = nb()
            pv = banks[b0][:].bitcast(BF16)
            xb = xnbs[tt % 3]
            for kt in range(8):
                TR(pv[:, kt * 128:kt * 128 + pt], xb[0:pt, kt * 128:(kt + 1) * 128], [('xnb', tt % 3)], b0)
            for kt in range(8):
                eng = 'act' if kt % 2 == 0 else 'dve'
                o = slab[dst][:, kt, tt * 128:tt * 128 + pt]
                i_ = pv[:, kt * 128:kt * 128 + pt]
                if eng == 'act':
                    ACT(o, i_, AF.Copy, [('gT', gname)], [bk(b0), ('slab', dst)], scale=gT[gname][:, kt:kt + 1])
                else:
                    TS('dve', o, i_, gT[gname][:, kt:kt + 1], None, ALU.mult, None, [('gT', gname)], [bk(b0), ('slab', dst)])

        def norm_T(tiles, pts, gname, dst):
            for tt, ((xa, xk), pt) in enumerate(zip(tiles, pts)):
                norm_a(tt, xa, xk, pt, tt)
                norm_b(tt, pt, gname, dst)

        def proj_fm(wname, ntile, src_slab, N, col_base=0, alloc=None):
            wi = wload(wname)
            outs = []
            for j in range(ntile):
                b0 = (alloc or nb)()
                for kt in range(8):
                    MM(banks[b0][:, 0:N], wsl[wi][:, kt, col_base + j * 128:col_base + (j + 1) * 128],
                       slab[src_slab][:, kt, 0:N], kt == 0, kt == 7, [('wsl', wi), ('slab', src_slab)], b0)
                outs.append(b0)
                yield j, b0

        def mem_finish():
            for kt2 in range(2):
                CP('act', vtok[:, kt2, :], XT[xoff[0] + kt2][:], [('xt', xoff[0] + kt2)], ['vtok'])
                CP('dve', mvt[:, kt2, :], XT[xoff[0] + 2 + kt2][:], [('xt', xoff[0] + 2 + kt2)], ['mvt'])
            for kt2 in range(2):
                b0 = nb()
                pv = banks[b0][:].bitcast(BF16)
                for t8 in range(8):
                    TR(pv[:, t8 * 128:(t8 + 1) * 128], vtok[:, kt2, t8 * 128:(t8 + 1) * 128], ['vtok'], b0)
                CP('act', mkT[:, :, kt2 * 128:(kt2 + 1) * 128], pv.rearrange("p (t k) -> p t k", k=128), [], [bk(b0), 'mkT'])

        def mem_prompt(b):
            for tt in range(2):
                DMA('sp', XT[xoff[0] + tt][:], memp[b, tt * 128:(tt + 1) * 128, :], ('xl', xoff[0] + tt), writes=[('xt', xoff[0] + tt)])
            norm_T([(XT[xoff[0] + 0][:], ('xt', xoff[0] + 0)), (XT[xoff[0] + 1][:], ('xt', xoff[0] + 1))], [128, 128], 'g_mem', S_XN)
            for wi_, (wn, dst0, outap) in enumerate((('w_mk', 0, mk_p), ('w_mv', 2, mv_p))):
                for hf in range(2):
                    wi = wload('%s%d' % (wn, hf))
                    for tt in range(2):
                        b0 = nb()
                        for kt in range(8):
                            MM(banks[b0][:], slab[S_XN][:, kt, tt * 128:(tt + 1) * 128], wsl[wi][:, kt, :], kt == 0, kt == 7,
                               [('wsl', wi), ('slab', S_XN)], b0)
                        eng = 'act' if (hf + tt) % 2 == 0 else 'dve'
                        CP(eng, XT[xoff[0] + dst0 + tt][:, hf * 512:(hf + 1) * 512], banks[b0][:], [], [bk(b0), ('xt', xoff[0] + dst0 + tt)])
                for tt in range(2):
                    DMA('pool', outap[b, tt * 128:(tt + 1) * 128, :], XT[xoff[0] + dst0 + tt][:], ('xo', xoff[0] + dst0 + tt), reads=[('xt', xoff[0] + dst0 + tt)])
            mem_finish()

        def mem_sample(b):
            for tt in range(2):
                DMA('sp', XT[xoff[0] + tt][:], cmk[b, tt * 128:(tt + 1) * 128, :], ('xl', xoff[0] + tt), writes=[('xt', xoff[0] + tt)])
                DMA('sp', XT[xoff[0] + 2 + tt][:], cmv[b, tt * 128:(tt + 1) * 128, :], ('xl', xoff[0] + 2 + tt), writes=[('xt', xoff[0] + 2 + tt)])
            mem_finish()

        def pro_load(xsrc, N, pos0):
            NT = (N + 127) // 128
            pts = [min(128, N - 128 * t) for t in range(NT)]
            for tt in range(NT):
                DMA('sp', XT[xoff[0] + tt][0:pts[tt], :], xsrc[tt * 128:tt * 128 + pts[tt], :], ('xl', xoff[0] + tt), writes=[('xt', xoff[0] + tt)])
            DMA('sp', cosb[:, 0:N], cd['cosT'][:, pos0:pos0 + N], 'cosl', writes=['cosb'])
            DMA('sp', sinb[:, 0:N], cd['sinT'][:, pos0:pos0 + N], 'sinl', writes=['sinb'])

        def pro_norm(N):
            NT = (N + 127) // 128
            pts = [min(128, N - 128 * t) for t in range(NT)]
            xtl = [(XT[xoff[0] + t][0:pts[t], :], ('xt', xoff[0] + t)) for t in range(NT)]
            norm_T(xtl, pts, 'g_attn', S_XN)

        def block(xsrc, ydst, N, pos0, units, first_chunk, is_last, kind, b, pro_done=False, nxt=None):
            NT = (N + 127) // 128
            pts = [min(128, N - 128 * t) for t in range(NT)]
            if not pro_done:
                pro_load(xsrc, N, pos0)
                pro_norm(N)
            xtl = [(XT[xoff[0] + t][0:pts[t], :], ('xt', xoff[0] + t)) for t in range(NT)]

            for nm, dst in (('rq', S_Q), ('rk', S_K)):
                for hf in range(2):
                    pend = []
                    for j, b0 in proj_fm('%s%d' % (nm, hf), 4, S_XN, N):
                        pend.append(b0)
                        if len(pend) == 2:
                            bA, bB = pend
                            pend = []
                            tA = hf * 4 + j - 1
                            A = banks[bA][:, 0:N]; B = banks[bB][:, 0:N]
                            cs = cosb[:, 0:N]; sn = sinb[:, 0:N]
                            TT('dve', rt[0][:, 0:N], A, cs, ALU.mult, ['cosb'], [bk(bA), 'rt0'])
                            TT('dve', rt[1][:, 0:N], B, sn, ALU.mult, ['sinb'], [bk(bB), 'rt1'])
                            TT('dve', rt[2][:, 0:N], B, cs, ALU.mult, ['cosb'], [bk(bB), 'rt2'])
                            TT('dve', rt[3][:, 0:N], A, sn, ALU.mult, ['sinb'], [bk(bA), 'rt3'])
                            TT('pool', slab[dst][:, tA, 0:N], rt[0][:, 0:N], rt[1][:, 0:N], ALU.subtract,
                               ['rt0', 'rt1'], [('slab', dst)])
                            TT('pool', slab[dst][:, tA + 1, 0:N], rt[2][:, 0:N], rt[3][:, 0:N], ALU.add,
                               ['rt2', 'rt3'], [('slab', dst)])
            for (u0, C) in units:
                TT('pool', slab[S_QS][:, :, u0:u0 + C].rearrange("p (h t) c -> p h t c", t=2),
                   slab[S_Q][:, :, u0:u0 + C].rearrange("p (h t) c -> p h t c", t=2),
                   qdec[:, :, 0:C].unsqueeze(2).to_broadcast([128, 4, 2, C]), ALU.mult,
                   [('slab', S_Q), 'qdec'], [('slab', S_QS)])
            for hf in range(2):
                wi = wload('rv%d' % hf)
                for tt in range(NT):
                    b0 = nb()
                    pt = pts[tt]
                    for kt in range(8):
                        MM(banks[b0][0:pt, :], slab[S_XN][:, kt, tt * 128:tt * 128 + pt], wsl[wi][:, kt, :], kt == 0, kt == 7,
                           [('wsl', wi), ('slab', S_XN)], b0)
                    CP('act', vtok[0:pt, tt, hf * 512:(hf + 1) * 512], banks[b0][0:pt, :], [], [bk(b0), 'vtok'])
            for hf in range(2):
                for j, b0 in proj_fm('rg%d' % hf, 4, S_XN, N):
                    t8 = hf * 4 + j
                    k = t8 % 2
                    ACT(tb[k][:, 0:N], banks[b0][:, 0:N], AF.Tanh, [], [bk(b0), ('tb', k)], scale=0.5)
                    STT(slab[S_RG][:, t8, 0:N], tb[k][:, 0:N], 1.0, banks[b0][:, 0:N], ALU.add, ALU.mult,
                        [('tb', k)], [bk(b0), ('slab', S_RG)])
            ra = mkalloc([0, 1])
            rg4 = mkalloc([0, 1, 2, 3])
            sa = mkalloc([4, 5, 6, 7])

            def gen_projB():
                for hf in range(2):
                    for j, b0 in proj_fm('sq%d' % hf, 4, S_XN, N, alloc=sa):
                        CP('act', slab[S_SQ][:, hf * 4 + j, 0:N], banks[b0][:, 0:N], [], [bk(b0), ('slab', S_SQ)])
                        yield
                for j, b0 in proj_fm('skd', 2, S_XN, N, alloc=sa):
                    kv = j
                    for var in range(2):
                        lo = var * 64
                        CP('act' if var == 0 else 'dve', KT[lo:lo + 64, kv, var, 128:128 + N], banks[b0][lo:lo + 64, 0:N], [], [bk(b0), 'KT'])
                    yield
                wi = wload('skv')
                for tt in range(NT):
                    b0 = sa()
                    pt = pts[tt]
                    for kt in range(8):
                        MM(banks[b0][0:pt, 0:256], slab[S_XN][:, kt, tt * 128:tt * 128 + pt], wsl[wi][:, kt, 0:256], kt == 0, kt == 7,
                           [('wsl', wi), ('slab', S_XN)], b0)
                    for kv in range(2):
                        CP('act', VP[0:pt, 1 + tt, kv, 0, 0:64], banks[b0][0:pt, 128 + kv * 64:128 + (kv + 1) * 64], [], [bk(b0), 'VP'])
                        CP('dve', VP[0:pt, 1 + tt, kv, 1, 64:128], banks[b0][0:pt, 128 + kv * 64:128 + (kv + 1) * 64], [], [bk(b0), 'VP'])
                    if is_last and tt == NT - 1:
                        CP('act', f32o[0:pt, :], banks[b0][0:pt, 0:256], [], [bk(b0), 'f32o'])
                        ok, ov = (swk_p, swv_p) if kind == 'p' else (swk_s, swv_s)
                        r0 = 128 - pt
                        DMA('pool', ok[b, r0:128, :], f32o[0:pt, 0:128], 'fo1', reads=['f32o'])
                        DMA('pool', ov[b, r0:128, :], f32o[0:pt, 128:256], 'fo2', reads=['f32o'])
                    yield

            def gen_ret():
                for ui, (u0, C) in enumerate(units):
                    tt = u0 // 128
                    kdv = kd128 if C == 128 else kd64
                    kdk = 'kd128' if C == 128 else 'kd64'
                    gam = G128 if C == 128 else G64
                    kb = ktok[ui % 2]
                    kbk = ('ktok', ui % 2)
                    b0 = ra()
                    pv = banks[b0][:].bitcast(BF16)
                    for t8 in range(8):
                        TR(pv[0:C, t8 * 128:(t8 + 1) * 128], slab[S_K][:, t8, u0:u0 + C], [('slab', S_K)], b0)
                    for h in range(4):
                        if h % 2 == 0:
                            ACT(kb[0:C, h, :], pv[0:C, h * 256:(h + 1) * 256], AF.Copy, [kdk], [bk(b0), kbk], scale=kdv[0:C, h:h + 1])
                        else:
                            TS('dve', kb[0:C, h, :], pv[0:C, h * 256:(h + 1) * 256], kdv[0:C, h:h + 1], None, ALU.mult, None,
                               [kdk], [bk(b0), kbk])
                    b1 = ra()
                    for h in range(4):
                        for dt in range(2):
                            MM(banks[b1][0:C, h * 128:h * 128 + C], slab[S_K][:, 2 * h + dt, u0:u0 + C],
                               slab[S_Q][:, 2 * h + dt, u0:u0 + C], dt == 0, dt == 1, [('slab', S_K), ('slab', S_Q)], b1)
                    TT('dve', smb[0:C, :, 0:C], banks[b1][0:C, :].rearrange("p (h i) -> p h i", i=128)[:, :, 0:C],
                       dmask[0:C, :, 0:C], ALU.mult, ['dmask'], [bk(b1), 'smb'])
                    yield
                    bo = [2, 3]
                    for h in range(4):
                        for et in range(2):
                            bb = bo[h // 2]
                            o = banks[bb][:, ((h % 2) * 2 + et) * 128:((h % 2) * 2 + et) * 128 + C]
                            MM(o, vtok[0:C, tt, h * 256 + et * 128:h * 256 + (et + 1) * 128], smb[0:C, h, 0:C], True, False,
                               ['vtok', 'smb'], bb)
                            for dt in range(2):
                                MM(o, stb[:, h, dt, et * 128:(et + 1) * 128], slab[S_QS][:, 2 * h + dt, u0:u0 + C], False, dt == 1,
                                   ['stb', ('slab', S_QS)], bb)
                    for h in range(4):
                        b3 = ra()
                        for dt in range(2):
                            MM(banks[b3][:, dt * 256:(dt + 1) * 256], kb[0:C, h, dt * 128:(dt + 1) * 128],
                               vtok[0:C, tt, h * 256:(h + 1) * 256], True, True, [kbk, 'vtok'], b3)
                        sth = st[:, h].rearrange("p t e -> p (t e)")
                        STT(sth, sth, gam[h], banks[b3][:], ALU.mult, ALU.add, ['st'], [bk(b3), 'st'])
                    CP('pool', stb[:].rearrange("p h t e -> p (h t e)"), st[:].rearrange("p h t e -> p (h t e)"), ['st'], ['stb'])
                    yield
                    for i2 in range(2):
                        ACT(osq[:, i2 * 4:(i2 + 1) * 4, 0:C], banks[bo[i2]][:].rearrange("p (t i) -> p t i", i=128)[:, :, 0:C],
                            AF.Square, [], [bk(bo[i2]), 'osq'])
                    b2 = ra()
                    for h in range(4):
                        for et in range(2):
                            MM(banks[b2][:, h * 128:h * 128 + C], ones[:], osq[:, 2 * h + et, 0:C], et == 0, et == 1, ['ones', 'osq'], b2)
                    ACT(rsd[:, :, 0:C], banks[b2][:].rearrange("p (h i) -> p h i", i=128)[:, :, 0:C], AF.Ln, ['epsb'], [bk(b2), 'rsd'],
                        bias=epsb[:, 1:2])
                    ACT(rsd[:, :, 0:C], rsd[:, :, 0:C], AF.Exp, ['rsd'], ['rsd'], scale=-0.5)
                    yield
                    for i2 in range(2):
                        TT('dve', u1[:, i2 * 4:(i2 + 1) * 4, 0:C].rearrange("p (h t) c -> p h t c", t=2),
                           banks[bo[i2]][:].rearrange("p (h t i) -> p h t i", t=2, i=128)[:, :, :, 0:C],
                           rsd[:, i2 * 2:(i2 + 1) * 2, 0:C].unsqueeze(2).to_broadcast([128, 2, 2, C]), ALU.mult,
                           ['rsd'], [bk(bo[i2]), 'u1'])
                    STT(slab[S_RG][:, :, u0:u0 + C], u1[:, :, 0:C], 8.0, slab[S_RG][:, :, u0:u0 + C], ALU.mult, ALU.mult,
                        ['u1', ('slab', S_RG)], [('slab', S_RG)])
                    yield
                if is_last:
                    ro = rst_p if kind == 'p' else rst_s
                    DMA('pool', ro[b].rearrange("h (t p) e -> p h t e", p=128), st[:], 'sto', reads=['st'])

            def gen_swa():
                nch = N // 64
                for ci in range(nch):
                    n_glob = first_chunk + ci
                    c = ci // 2
                    q0 = ci * 64
                    if ci % 2 == 0:
                        tiles = [(128 * c, c, 0), (128 + 128 * c, c + 1, 1)]
                        if n_glob == 0:
                            tiles = tiles[1:]
                    else:
                        tiles = [(128 + 128 * c, c + 1, 2), (128 * c, c, 3)]
                        if n_glob == 1:
                            tiles = tiles[:1]
                    allebs = []
                    for kv in range(2):
                        ebs = []
                        for xi, (kc, vpi, ebv) in enumerate(tiles):
                            b0 = sa()
                            for par in range(2):
                                o_ = banks[b0][:, par * 256:(par + 1) * 256]
                                MM(o_, KT[:, kv, par, kc:kc + 128],
                                   slab[S_SQ][:, kv * 4:kv * 4 + 4, q0:q0 + 64], True, False, ['KT', ('slab', S_SQ)], b0)
                                MM(o_, ident[:], EB[ebv][:, kv * 8 + par * 4:kv * 8 + par * 4 + 4, :], False, True,
                                   ['ident', ('EB', ebv)], b0)
                            bi = (ci % 2) * 4 + kv * 2 + xi
                            ACT(eb[bi][:], banks[b0][:], AF.Exp, [], [bk(b0), ('eb', bi)], scale=0.125)
                            ebs.append((bi, vpi))
                        allebs.append(ebs)
                        yield
                    bpo = sa()
                    bpd = sa()
                    for kv in range(2):
                        ebs = allebs[kv]
                        nmm = len(ebs) * 2
                        for which, bb in ((0, bpo), (1, bpd)):
                            k = 0
                            for (bi, vpi) in ebs:
                                for par in range(2):
                                    lhs = VP[:, vpi, kv, par, :] if which == 0 else onesp[:, par, :]
                                    MM(banks[bb][:, kv * 256:(kv + 1) * 256], lhs, eb[bi][:, par * 256:(par + 1) * 256],
                                       k == 0, k == nmm - 1, ['VP', 'onesp', ('eb', bi)], bb)
                                    k += 1
                    TT('dve', den[:].rearrange("p (t q) -> p t q", q=64), banks[bpd][:].rearrange("p (t q) -> p t q", q=64),
                       esink[:].unsqueeze(2).to_broadcast([128, 8, 64]), ALU.add, ['esink'], [bk(bpd), 'rt2'])
                    ACT(rden[:], den[:], AF.Ln, ['rt2'], ['rt3'])
                    ACT(rden[:], rden[:], AF.Exp, ['rt3'], ['rt3'], scale=-1.0)
                    TT('dve', slab[S_SWA][:, :, q0:q0 + 64], banks[bpo][:].rearrange("p (t q) -> p t q", q=64),
                       rden[:].rearrange("p (t q) -> p t q", q=64), ALU.mult, ['rt3'], [bk(bpo), ('slab', S_SWA)])
                    yield
                if N == 512:
                    CP('pool', KT[:, :, :, 0:128], KT[:, :, :, 512:640], ['KT'], ['KT'])
                    CP('pool', VP[:, 0], VP[:, 4], ['VP'], ['VP'])

            def gen_gates():
                for gi, (nm, dst) in enumerate((('ga', S_Q), ('gb', S_K))):
                    for hf in range(2):
                        for j, b0 in proj_fm('%s%d' % (nm, hf), 4, S_XN, N, alloc=rg4):
                            ACT(slab[dst][:, hf * 4 + j, 0:N], banks[b0][:, 0:N], AF.Tanh, [], [bk(b0), ('slab', dst)], scale=0.5)
                            yield
                for hf in range(2):
                    for j, b0 in proj_fm('w_ret_out%d' % hf, 4, S_RG, N, alloc=rg4):
                        t8 = hf * 4 + j
                        STT(slab[S_Q][:, t8, 0:N], slab[S_Q][:, t8, 0:N], 1.0, banks[b0][:, 0:N], ALU.add, ALU.mult,
                            [('slab', S_Q)], [bk(b0), ('slab', S_Q)])
                        yield

            def chain(*gs):
                for g in gs:
                    for _ in g:
                        yield

            pb_left = [14]

            def gen_b():
                for _ in gen_projB():
                    pb_left[0] -= 1
                    yield
                pb_left[0] = 0
                for _ in gen_swa():
                    yield

            sB = [gen_b(), RR_B]
            streams = [[chain(gen_ret(), gen_gates()), RR_A], sB]
            while streams:
                for sg in list(streams):
                    nstep = sg[1] * (2 if (sg is sB and pb_left[0] > 0) else 1)
                    for _ in range(nstep):
                        try:
                            next(sg[0])
                        except StopIteration:
                            streams.remove(sg)
                            break

            if dbg == 'eb' and N == 512:
                for v in range(4):
                    DMA('pool', ydst[v * 128:(v + 1) * 128, :], EB[v][:].rearrange("p h q -> p (h q)"), ('xo', xoff[0] + 0), reads=[('EB', v)])
                return
            if dbg in ('swa', 'ret'):
                src = slab[S_SWA] if dbg == 'swa' else slab[S_RG]
                nt_ = min(N, 128)
                for t8 in range(8):
                    for hh in range(nt_ // 64):
                        DMA('pool', ydst[hh * 64:(hh + 1) * 64, t8 * 128:(t8 + 1) * 128].rearrange("tok p -> p tok"),
                            src[:, t8, hh * 64:(hh + 1) * 64], ('xo', xoff[0] + 0), reads=[('slab', S_SWA), ('slab', S_RG)], slow=True)
                return
            for hf in range(2):
                for j, b0 in proj_fm('w_swa_out%d' % hf, 4, S_SWA, N):
                    t8 = hf * 4 + j
                    k = t8 % 2
                    STT(sf[k][:, 0:N], slab[S_K][:, t8, 0:N], 1.0, banks[b0][:, 0:N], ALU.add, ALU.mult,
                        [('slab', S_K)], [bk(b0), 'rt%d' % k])
                    TT('pool', slab[S_Q][:, t8, 0:N], slab[S_Q][:, t8, 0:N], sf[k][:, 0:N], ALU.add,
                       [('slab', S_Q), 'rt%d' % k], [('slab', S_Q)])

            def proj_tm_resid(wn, src_slab, scale, next_g=None):
                wis = [wload('%s%d' % (wn, hf)) for hf in range(2)]
                for tt in range(NT):
                    pt = pts[tt]
                    for hf in range(2):
                        wi = wis[hf]
                        b0 = nb()
                        for kt in range(8):
                            MM(banks[b0][0:pt, :], slab[src_slab][:, kt, tt * 128:tt * 128 + pt], wsl[wi][:, kt, :], kt == 0, kt == 7,
                               [('wsl', wi), ('slab', src_slab)], b0)
                        xa = XT[xoff[0] + tt][0:pt, hf * 512:(hf + 1) * 512]
                        STT(xa, banks[b0][0:pt, :], scale, xa, ALU.mult, ALU.add, [('xt', xoff[0] + tt)], [bk(b0), ('xt', xoff[0] + tt)])
                    if next_g is not None:
                        norm_a(tt, XT[xoff[0] + tt][0:pt, :], ('xt', xoff[0] + tt), pt, tt)
                        if tt >= 2:
                            norm_b(tt - 2, pts[tt - 2], next_g, S_XN)
                if next_g is not None:
                    for t_ in range(max(0, NT - 2), NT):
                        norm_b(t_, pts[t_], next_g, S_XN)

            proj_tm_resid('w_mix_out', S_Q, 0.5, None if dbg == 'mix' else 'g_cross')

            def dbg_store():
                for tt in range(NT):
                    DMA('pool', ydst[tt * 128:tt * 128 + pts[tt], :], XT[xoff[0] + tt][0:pts[tt], :], ('xo', xoff[0] + tt), reads=[('xt', xoff[0] + tt)])
            if dbg == 'mix':
                dbg_store()
                return

            if nxt is not None:
                xoff[0] ^= 4
                pro_load(*nxt)
                xoff[0] ^= 4
            def cross_finish(h):
                ce = ceb[h]
                cks = [('eb', 2 * h), ('eb', 2 * h + 1)]
                bd = nb()
                for kt2 in range(2):
                    MM(banks[bd][:, 0:N], ones[:], ce[:, kt2, 0:N], kt2 == 0, kt2 == 1, ['ones'] + cks, bd)
                ACT(rt[h][:, 0:N], banks[bd][:, 0:N], AF.Ln, [], [bk(bd), 'rt%d' % h])
                ACT(rt[h][:, 0:N], rt[h][:, 0:N], AF.Exp, ['rt%d' % h], ['rt%d' % h], scale=-1.0)
                for dt in range(2):
                    b2 = nb()
                    for kt2 in range(2):
                        MM(banks[b2][:, 0:N], mvt[:, kt2, h * 256 + dt * 128:h * 256 + (dt + 1) * 128], ce[:, kt2, 0:N],
                           kt2 == 0, kt2 == 1, ['mvt'] + cks, b2)
                    TT('dve', slab[S_K][:, h * 2 + dt, 0:N], banks[b2][:, 0:N], rt[h][:, 0:N], ALU.mult, ['rt%d' % h],
                       [bk(b2), ('slab', S_K)])

            for hf in range(2):
                for j, b0 in proj_fm('w_cq%d' % hf, 4, S_XN, N):
                    t8 = hf * 4 + j
                    CP('act', slab[S_Q][:, t8, 0:N], banks[b0][:, 0:N], [], [bk(b0), ('slab', S_Q)])
                    if t8 % 2 == 1:
                        h = t8 // 2
                        ce = ceb[h]
                        cks = [('eb', 2 * h), ('eb', 2 * h + 1)]
                        for kt2 in range(2):
                            b1 = nb()
                            for dt in range(2):
                                MM(banks[b1][:, 0:N], mkT[:, h * 2 + dt, kt2 * 128:(kt2 + 1) * 128], slab[S_Q][:, h * 2 + dt, 0:N],
                                   dt == 0, dt == 1, ['mkT', ('slab', S_Q)], b1)
                            ACT(ce[:, kt2, 0:N], banks[b1][:, 0:N], AF.Exp, [], [bk(b1)] + cks, scale=1.0 / 16.0)
                        if h >= 1:
                            cross_finish(h - 1)
            cross_finish(3)
            proj_tm_resid('w_co', S_K, 1.0, None if dbg == 'cross' else 'g_ffn')
            if dbg == 'cross':
                dbg_store()
                return

            for gi in range(11):
                wi = wload('gu%d' % gi)
                for j in range(2):
                    bg = nb()
                    bu = nb()
                    for kt in range(8):
                        MM(banks[bg][:, 0:N], wsl[wi][:, kt, j * 128:(j + 1) * 128], slab[S_XN][:, kt, 0:N], kt == 0, kt == 7,
                           [('wsl', wi), ('slab', S_XN)], bg)
                    for kt in range(8):
                        MM(banks[bu][:, 0:N], wsl[wi][:, kt, 256 + j * 128:256 + (j + 1) * 128], slab[S_XN][:, kt, 0:N], kt == 0, kt == 7,
                           [('wsl', wi), ('slab', S_XN)], bu)
                    hj = gi * 2 + j
                    k = hj % 2
                    ACT(tb[k][:, 0:N], banks[bg][:, 0:N], AF.Tanh, [], [bk(bg), ('tb', k)], scale=0.5)
                    STT(sf[k][:, 0:N], tb[k][:, 0:N], 1.0, banks[bg][:, 0:N], ALU.add, ALU.mult, [('tb', k)], [bk(bg), 'rt%d' % k])
                    hs = 1 + hj // 8
                    TT('dve', slab[hs][:, hj % 8, 0:N], sf[k][:, 0:N], banks[bu][:, 0:N], ALU.mult, ['rt%d' % k],
                       [bk(bu), ('slab', hs)])
            npend = []
            if nxt is not None:
                xo2 = xoff[0] ^ 4

                def mk_a(t_):
                    return lambda: norm_a(t_, XT[xo2 + t_][:], ('xt', xo2 + t_), 128, t_)

                def mk_b(t_):
                    return lambda: norm_b(t_, 128, 'g_attn', S_XN)
                mk_a(0)()
                mk_a(1)()
                mk_a(2)()
                npend = [mk_b(0), mk_a(3), mk_b(1), mk_b(2), mk_b(3)]
            for hf in range(2):
                bt = [nb() for _ in range(NT)]
                for sbi, (k0, nk) in enumerate(((0, 8), (8, 8), (16, 6))):
                    if npend and not (hf == 0 and sbi == 0):
                        npend.pop(0)()
                        if len(npend) == 4:
                            npend.pop(0)()
                    wi = wload('dn%d_%d' % (hf, sbi))
                    for tt in range(NT):
                        pt = pts[tt]
                        for kl in range(nk):
                            kt = k0 + kl
                            hs = 1 + kt // 8
                            MM(banks[bt[tt]][0:pt, :], slab[hs][:, kt % 8, tt * 128:tt * 128 + pt], wsl[wi][:, kl, :],
                               kt == 0, kt == 21, [('wsl', wi), ('slab', hs)], bt[tt])
                for tt in range(NT):
                    pt = pts[tt]
                    xa = XT[xoff[0] + tt][0:pt, hf * 512:(hf + 1) * 512]
                    STT(xa, banks[bt[tt]][0:pt, :], 0.5, xa, ALU.mult, ALU.add, [('xt', xoff[0] + tt)], [bk(bt[tt]), ('xt', xoff[0] + tt)])
            while npend:
                npend.pop(0)()

            for tt in range(NT):
                pt = pts[tt]
                xa = XT[xoff[0] + tt][0:pt, :]
                col = 4 + tt
                ACT(junk[0:pt, :], xa, AF.Square, [('xt', xoff[0] + tt)], ['u1', ('ssq', col)], accum=ssq[0:pt, col:col + 1])
                ACT(rstd[0:pt, col:col + 1], ssq[0:pt, col:col + 1], AF.Ln, [('ssq', col), 'epsb'], [('rstd', col)], scale=1.0 / D, bias=epsb[0:pt, 0:1])
                ACT(rstd[0:pt, col:col + 1], rstd[0:pt, col:col + 1], AF.Exp, [('rstd', col)], [('rstd', col)], scale=-0.5)
                STT(xa, xa, rstd[0:pt, col:col + 1], gfin[0:pt, :], ALU.mult, ALU.mult, [('xt', xoff[0] + tt), ('rstd', col), 'gfin'], [('xt', xoff[0] + tt)])
                DMA('pool', ydst[tt * 128:tt * 128 + pt, :], xa, ('xo', xoff[0] + tt), reads=[('xt', xoff[0] + tt)])

        P.strict = STRICT_SAME_ENGINE
        gblk = 0
        for b in range(2):
            xoff[0] = 4 * (gblk % 2) ^ 4
            mem_prompt(b)
            MEMSET('dve', st[:], 0.0, ['st'])
            MEMSET('pool', stb[:], 0.0, ['stb'])
            for blk in range(NBLK):
                xoff[0] = 4 * (gblk % 2)
                nxt = None
                if blk + 1 < NBLK and not dbg:
                    nxt = (xp[b, (blk + 1) * 512:(blk + 2) * 512, :], 512, (blk + 1) * 512)
                block(xp[b, blk * 512:(blk + 1) * 512, :], y_p[b, blk * 512:(blk + 1) * 512, :], 512, blk * 512,
                      [(u * 128, 128) for u in range(4)], blk * 8, blk == NBLK - 1, 'p', b, pro_done=(blk > 0 and not dbg), nxt=nxt)
                gblk += 1
        for b in range(2):
            xoff[0] = 4 * (gblk % 2) ^ 4
            mem_sample(b)
            xoff[0] = 4 * (gblk % 2)
            gblk += 1
            DMA('pool', st[:], cret[b].rearrange("h (t p) e -> p h t e", p=128), 'stl', writes=['st'])
            CP('pool', stb[:].rearrange("p h t e -> p (h t e)"), st[:].rearrange("p h t e -> p (h t e)"), ['st'], ['stb'])
            DMA('pool', rt[0][:, 0:128], cswk[b], 'ckl', writes=['rt0'])
            DMA('pool', rt[1][:, 0:128], cswv[b], 'cvl', writes=['rt1'])
            kd = junk[:, 0:256].rearrange("p (kv du d) -> p kv du d", kv=2, du=2)
            CP('dve', kd, rt[0][:, 0:128].rearrange("p (kv d) -> p kv d", kv=2).unsqueeze(2).to_broadcast([128, 2, 2, 64]),
               ['rt0'], ['u1'])
            b0 = nb()
            pv = banks[b0][:].bitcast(BF16)
            for kv in range(2):
                TR(pv[:, kv * 128:(kv + 1) * 128], junk[:, kv * 128:(kv + 1) * 128], ['u1'], b0)
            for kv in range(2):
                for var in range(2):
                    lo = var * 64
                    CP('act', KT[lo:lo + 64, kv, var, 0:128], pv[lo:lo + 64, kv * 128:(kv + 1) * 128], [], [bk(b0), 'KT'])
            for kv in range(2):
                CP('dve', VP[:, 0, kv, 0, 0:64], rt[1][:, kv * 64:(kv + 1) * 64], ['rt1'], ['VP'])
                CP('dve', VP[:, 0, kv, 1, 64:128], rt[1][:, kv * 64:(kv + 1) * 64], ['rt1'], ['VP'])
            DMA('pool', swk_s[b, 0:64, :], cswk[b, 64:128, :], 'pk')
            DMA('pool', swv_s[b, 0:64, :], cswv[b, 64:128, :], 'pvv')
            block(xs[b], y_s[b], 64, S, [(0, 64)], 64, True, 's', b)

        P.emit(nc, es)
    return nc, consts


_CACHE = {}


DBG = False


def _get(S):
    if S not in _CACHE:
        _CACHE[S] = build(S, DBG)
    return _CACHE[S]


def kernel(x_prompt, x_sample, cache_ret_state, cache_swa_k, cache_swa_v, cache_mem_k, cache_mem_v,
           mem_prompt, rel_bias, g_attn, w_in, w_ret_out, w_swa_out, w_mix_out, swa_sinks,
           g_cross, g_mem, w_cq, w_mk, w_mv, w_co, g_ffn, w_gate, w_up, w_down, g_final):
    f = lambda a: np.ascontiguousarray(np.asarray(a, dtype=np.float32))
    S = x_prompt.shape[1]
    nc, consts = _get(S)
    ncore = x_prompt.shape[0] // 2
    shared = {'relb': f(rel_bias), 'sinks': f(swa_sinks).reshape(16),
              'g_attn': f(g_attn).reshape(D), 'g_cross': f(g_cross).reshape(D), 'g_mem': f(g_mem).reshape(D),
              'g_ffn': f(g_ffn).reshape(D), 'g_final': f(g_final).reshape(D),
              'w_in': f(w_in)[0], 'w_ret_out': f(w_ret_out)[0], 'w_swa_out': f(w_swa_out)[0],
              'w_mix_out': f(w_mix_out)[0], 'w_cq': f(w_cq)[0], 'w_mk': f(w_mk)[0], 'w_mv': f(w_mv)[0],
              'w_co': f(w_co)[0], 'w_gate': f(w_gate)[0], 'w_up': f(w_up)[0], 'w_down': f(w_down)[0]}
    for n, a in consts.items():
        shared['c_' + n] = a
    in_maps = []
    for c in range(ncore):
        sl = slice(2 * c, 2 * c + 2)
        m = dict(shared)
        m['xp'] = f(x_prompt[sl]); m['xs'] = f(x_sample[sl])
        m['cret'] = f(cache_ret_state[0, sl])
        m['cswk'] = f(cache_swa_k[0, sl]).reshape(2, 128, 128)
        m['cswv'] = f(cache_swa_v[0, sl]).reshape(2, 128, 128)
        m['cmk'] = f(cache_mem_k[0, sl]).reshape(2, 256, D)
        m['cmv'] = f(cache_mem_v[0, sl]).reshape(2, 256, D)
        m['memp'] = f(mem_prompt[sl])
        in_maps.append(m)
    res = run_bass_kernel_spmd(nc, in_maps, core_ids=list(range(ncore)))
    R = res.results
    cat = lambda k: np.concatenate([r[k] for r in R], axis=0)
    B = 2 * ncore
    return (cat('y_p'), cat('y_s'),
            cat('rst_p')[None], cat('rst_s')[None],
            cat('swk_p').reshape(1, B, 128, 2, 64), cat('swk_s').reshape(1, B, 128, 2, 64),
            cat('swv_p').reshape(1, B, 128, 2, 64), cat('swv_s').reshape(1, B, 128, 2, 64),
            cat('mk_p').reshape(1, B, 256, 4, 256), cat('mv_p').reshape(1, B, 256, 4, 256))
```

```python
import math
from contextlib import ExitStack
import numpy as np
import ml_dtypes
import concourse.bass as bass
import concourse.mybir as mybir
from concourse.bass_utils import run_bass_kernel_spmd

F32 = mybir.dt.float32
BF16 = mybir.dt.bfloat16
AF = mybir.ActivationFunctionType
ALU = mybir.AluOpType

D = 1024
SEQ = 4096
PAST = 4096
DFF = 2816
EPS = 1e-6
ENGS = ('pe', 'act', 'dve', 'pool', 'sp')
RR_A = 1
RR_B = 1
STRICT_SAME_ENGINE = False
ALLN = 10 ** 9


class _Op:
    __slots__ = ('fn', 'waits', 'signal', 'dma', 'sigval')

    def __init__(self, fn, dma):
        self.fn = fn
        self.waits = []
        self.signal = False
        self.dma = dma
        self.sigval = 0


class Prog:
    def __init__(self):
        self.ops = {e: [] for e in ENGS}
        self.lastw = {}
        self.rd = {}
        self.seen = {e: {} for e in ENGS}
        self.dcount = {}
        self.strict = True

    def add(self, eng, fn, reads=(), writes=(), dma=None):
        ops = self.ops[eng]
        idx = len(ops)
        op = _Op(fn, dma)
        seen = self.seen[eng]
        cand = []
        for k in reads:
            w = self.lastw.get(k)
            if w is not None:
                cand.append((w, True))
        for k in writes:
            w = self.lastw.get(k)
            if w is not None:
                cand.append((w, False))
            r = self.rd.get(k)
            if r:
                for ref in r.values():
                    cand.append((ref, False))
        for ref, is_raw in cand:
            if ref[0] == 'c':
                _, e2, i2 = ref
                if e2 == eng and dma is None:
                    if eng == 'pe' or (not is_raw and not self.strict):
                        continue
                if seen.get(e2, -1) >= i2:
                    continue
                seen[e2] = i2
                op.waits.append(ref)
                self.ops[e2][i2].signal = True
            else:
                _, key, n = ref
                sk = ('d', key)
                if seen.get(sk, 0) >= n:
                    continue
                seen[sk] = n
                op.waits.append(ref)
        if dma is None:
            ref = ('c', eng, idx)
            rk = eng
        else:
            n = self.dcount.get(dma, 0) + 1
            self.dcount[dma] = n
            ref = ('d', dma, n)
            rk = ('d', dma)
        for k in writes:
            self.lastw[k] = ref
            self.rd[k] = {}
        for k in reads:
            self.rd.setdefault(k, {})[rk] = ref
        ops.append(op)
        return ref

    def emit(self, nc, es):
        sems = {e: es.enter_context(nc.semaphore('s_' + e)) for e in ENGS}
        dsem = {}
        for i, k in enumerate(self.dcount):
            dsem[k] = es.enter_context(nc.semaphore('d%d' % i))
        for e in ENGS:
            c = 0
            for op in self.ops[e]:
                if op.signal:
                    c += 1
                    op.sigval = c
        block = es.enter_context(nc.Block())
        prog = self

        def run(eng_name, e):
            for op in prog.ops[eng_name]:
                for ref in op.waits:
                    if ref[0] == 'c':
                        e.wait_ge(sems[ref[1]], prog.ops[ref[1]][ref[2]].sigval)
                    else:
                        n = ref[2]
                        if n == ALLN:
                            n = prog.dcount[ref[1]]
                        e.wait_ge(dsem[ref[1]], 16 * n)
                ins = op.fn(e)
                if op.dma is not None:
                    ins.then_inc(dsem[op.dma], 16)
                elif op.signal:
                    ins.then_inc(sems[eng_name], 1)
            if eng_name == 'pool':
                for k, n in prog.dcount.items():
                    e.wait_ge(dsem[k], 16 * n)

        @block.tensor
        def _(e):
            run('pe', e)

        @block.scalar
        def _(e):
            run('act', e)

        @block.vector
        def _(e):
            run('dve', e)

        @block.gpsimd
        def _(e):
            run('pool', e)

        @block.sync
        def _(e):
            run('sp', e)


def _consts(S):
    c = {}
    half = 128
    inv = (np.float32(10000.0) ** (-np.arange(half, dtype=np.float32) / np.float32(half))).astype(np.float32)
    pos = np.concatenate([np.arange(S, dtype=np.float32), PAST + np.arange(64, dtype=np.float32)]).astype(np.float32)
    ang = (pos[None, :] * inv[:, None]).astype(np.float32)
    c['cosT'] = np.cos(ang).astype(np.float32)
    c['sinT'] = np.sin(ang).astype(np.float32)
    lg = np.log(np.float32(1.0) - np.float32(2.0) ** (-5.0 - np.arange(4, dtype=np.float32))).astype(np.float32)
    idx = np.arange(128, dtype=np.float32)
    i = idx[None, :]
    j = idx[:, None]
    same = (np.floor(i / 64) == np.floor(j / 64))
    lower = (np.floor(i / 64) > np.floor(j / 64))
    dm = np.zeros((128, 4, 128), np.float32)
    for h in range(4):
        intra = np.exp(lg[h] * np.abs(i - j)).astype(np.float32)
        cross = (np.exp(lg[h] * (np.mod(i, 64) + 1.0)).astype(np.float32)
                 * np.exp(lg[h] * (63.0 - np.mod(j, 64))).astype(np.float32)).astype(np.float32)
        dm[:, h, :] = np.where(same, intra, np.where(lower, cross, 0.0)) / np.float32(16.0)
    c['dmask'] = dm
    qd = np.zeros((128, 4, 128), np.float32)
    kd128 = np.zeros((128, 4), np.float32)
    kd64 = np.zeros((128, 4), np.float32)
    g128 = []
    g64 = []
    for h in range(4):
        g64h = np.exp(lg[h] * np.float32(64.0)).astype(np.float32)
        qd64 = np.exp(lg[h] * (np.arange(64, dtype=np.float32) + 1.0)).astype(np.float32)
        kdec64 = np.exp(lg[h] * (63.0 - np.arange(64, dtype=np.float32))).astype(np.float32)
        qd[:, h, :64] = qd64[None, :]
        qd[:, h, 64:] = (qd64 * g64h)[None, :]
        kd64[:64, h] = kdec64 / 16.0
        kd128[:64, h] = kdec64 * g64h / 16.0
        kd128[64:, h] = kdec64 / 16.0
        g64.append(float(g64h))
        g128.append(float(np.float32(g64h * g64h)))
    c['qdec'] = qd
    c['kd128'] = kd128
    c['kd64'] = kd64
    c['ident'] = np.eye(128, dtype=np.float32).astype(ml_dtypes.bfloat16)
    q = np.arange(64, dtype=np.int32)[:, None]
    jj = np.arange(192, dtype=np.int32)[None, :]
    rel = (jj - 128) - q
    n = np.abs(rel)
    large = 8 + (np.log(np.maximum(n, 1).astype(np.float32) / 8) / math.log(128 / 8) * 8).astype(np.int32)
    large = np.minimum(large, 15)
    bucket = np.where(rel > 0, 16, 0) + np.where(n < 8, n, large)
    kk = np.arange(128)
    maps = [kk, np.where(kk < 64, 128 + kk, -1), 64 + kk, np.where(kk >= 64, kk - 64, -1)]
    oh = np.zeros((4, 32, 64, 128), np.float32)
    for v, m in enumerate(maps):
        for k in range(128):
            if m[k] >= 0:
                oh[v, bucket[:, m[k]], np.arange(64), k] = 1.0
    c['oh'] = oh.astype(ml_dtypes.bfloat16)
    return c, g128, g64


def _wchunks():
    ch = {}
    offs = dict(rq=0, rk=1024, rv=2048, rg=3072, sq=4096, sk=5120, sv=5248, ga=5376, gb=6400)
    for nm in ('rq', 'rk', 'rv', 'rg', 'sq', 'ga', 'gb'):
        for i in range(2):
            ch['%s%d' % (nm, i)] = ('w_in', 0, 8, [(offs[nm] + 512 * i, 512)])
    ch['skd'] = ('w_in', 0, 8, [(5120, 64), (5120, 64), (5184, 64), (5184, 64)])
    ch['skv'] = ('w_in', 0, 8, [(5120, 256)])
    for nm in ('w_ret_out', 'w_swa_out', 'w_mix_out', 'w_cq', 'w_co', 'w_mk', 'w_mv'):
        for i in range(2):
            ch['%s%d' % (nm, i)] = (nm, 0, 8, [(512 * i, 512)])
    for i in range(11):
        ch['gu%d' % i] = ('w_gu', 0, 8, [(256 * i, 256)])
    for hf in range(2):
        for sb, (k0, nk) in enumerate(((0, 8), (8, 8), (16, 6))):
            ch['dn%d_%d' % (hf, sb)] = ('w_down', k0, nk, [(512 * hf, 512)])
    return ch


WNAMES = ('w_in', 'w_ret_out', 'w_swa_out', 'w_mix_out', 'w_cq', 'w_mk', 'w_mv', 'w_co', 'w_gate', 'w_up', 'w_down')
WSHAPES = dict(w_in=(D, 7424), w_ret_out=(D, D), w_swa_out=(D, D), w_mix_out=(D, D), w_cq=(D, D), w_mk=(D, D),
               w_mv=(D, D), w_co=(D, D), w_gate=(D, DFF), w_up=(D, DFF), w_down=(DFF, D))


def build(S, dbg=False):
    nc = bass.Bass("TRN2", target_bir_lowering=False)
    P = Prog()
    consts, G128, G64 = _consts(S)
    NBLK = S // 512

    def din(name, shape, dt=F32):
        return nc.dram_tensor(name, list(shape), dt, kind="ExternalInput").ap()

    def dout(name, shape):
        return nc.dram_tensor(name, list(shape), F32, kind="ExternalOutput").ap()

    xp = din('xp', [2, S, D]); xs = din('xs', [2, 64, D])
    cret = din('cret', [2, 4, 256, 256]); cswk = din('cswk', [2, 128, 128]); cswv = din('cswv', [2, 128, 128])
    cmk = din('cmk', [2, 256, D]); cmv = din('cmv', [2, 256, D]); memp = din('memp', [2, 256, D])
    relb = din('relb', [32, 16]); sinks = din('sinks', [16])
    gvec = {n: din(n, [D]) for n in ('g_attn', 'g_cross', 'g_mem', 'g_ffn', 'g_final')}
    W = {n: din(n, WSHAPES[n]) for n in WNAMES}
    cd = {}
    for n, a in consts.items():
        cd[n] = din('c_' + n, a.shape, BF16 if a.dtype == ml_dtypes.bfloat16 else F32)

    y_p = dout('y_p', [2, S, D]); y_s = dout('y_s', [2, 64, D])
    rst_p = dout('rst_p', [2, 4, 256, 256]); rst_s = dout('rst_s', [2, 4, 256, 256])
    swk_p = dout('swk_p', [2, 128, 128]); swk_s = dout('swk_s', [2, 128, 128])
    swv_p = dout('swv_p', [2, 128, 128]); swv_s = dout('swv_s', [2, 128, 128])
    mk_p = dout('mk_p', [2, 256, D]); mv_p = dout('mv_p', [2, 256, D])

    chunks = _wchunks()
    wb = {}
    for nm, (src, k0, nk, cols) in chunks.items():
        ncol = sum(c[1] for c in cols) * (2 if src == 'w_gu' else 1)
        wb[nm] = nc.dram_tensor('wb_' + nm, [128, nk, ncol], BF16, kind="Internal").ap()

    es = ExitStack()
    with es:
        def sb(name, shape, dt=F32):
            return es.enter_context(nc.sbuf_tensor(name, list(shape), dt))

        XT = [sb('xt%d' % i, [128, D]) for i in range(8)]
        xoff = [0]
        slab = [sb('slab%d' % i, [128, 8, 512], BF16) for i in range(7)]
        S_XN, S_Q, S_K, S_QS, S_RG, S_SQ, S_SWA = range(7)
        hT = sb('hT_extra', [128, 1, 1], BF16)
        vtok = sb('vtok', [128, 4, D], BF16)
        ktok = [sb('ktok%d' % i, [128, 4, 256], BF16) for i in range(2)]
        wsl = [sb('wsl%d' % i, [128, 8, 512], BF16) for i in range(3)]
        gfin = sb('gfin', [128, D])
        gT = {n: sb('gT_' + n, [128, 8]) for n in ('g_attn', 'g_cross', 'g_mem', 'g_ffn')}
        cosb = sb('cosb', [128, 512]); sinb = sb('sinb', [128, 512])
        dmask = sb('dmask', [128, 4, 128]); qdec = sb('qdec', [128, 4, 128])
        kd128 = sb('kd128', [128, 4]); kd64 = sb('kd64', [128, 4])
        ident = sb('ident', [128, 128], BF16)
        ones = sb('ones', [128, 128], BF16)
        onesp = sb('onesp', [128, 2, 128], BF16)
        EB = [sb('EB%d' % v, [128, 16, 64], BF16) for v in range(4)]
        esink = sb('esink', [128, 8])
        tabf = sb('tabf', [128, 16]); tabh = sb('tabh', [128, 16], BF16); tabl = sb('tabl', [128, 16], BF16)
        tabr = sb('tabr', [128, 16])
        st = sb('st', [128, 4, 2, 256]); stb = sb('stb', [128, 4, 2, 256], BF16)
        KT = sb('KT', [128, 2, 2, 640], BF16)
        VP = sb('VP', [128, 5, 2, 2, 128], BF16)
        mkT = sb('mkT', [128, 8, 256], BF16)
        mvt = sb('mvt', [128, 2, D], BF16)
        xnbs = [sb('xnb%d' % i, [128, D], BF16) for i in range(2)]
        ssq = sb('ssq', [128, 8]); rstd = sb('rstd', [128, 8])
        rt = [sb('rt%d' % i, [128, 512]) for i in range(4)]
        smb = sb('smb', [128, 4, 128], BF16)
        osq = sb('osq', [128, 8, 128], BF16)
        rsd = sb('rsd', [128, 4, 128])
        u1 = sb('u1', [128, 8, 128])
        junk = u1[:].rearrange("p a b -> p (a b)").bitcast(BF16)[:, 0:D]
        xnbs.append(u1[:].rearrange("p a b -> p (a b)").bitcast(BF16)[:, D:2 * D])
        ef = [rt[0], rt[1]]
        ebb = sb('ebb', [128, 8, 512], BF16)
        eb = [ebb[:, i] for i in range(8)]
        den = rt[2]; rden = rt[3]
        ceb = [ebb[:, 2 * i:2 * i + 2] for i in range(4)]
        tb = [sb('tb%d' % i, [128, 512], BF16) for i in range(2)]
        sf = [rt[0], rt[1]]
        f32o = sb('f32o', [128, 256])
        epsb = sb('epsb', [128, 2])

        banks = [es.enter_context(nc.psum_tensor('pb%d' % i, [128, 512], F32)) for i in range(8)]
        if dbg == 'mem':
            print('SBUF bytes remaining per partition:', nc.sbuf_bytes_remaining)
        bctr = [0]

        def nb():
            i = bctr[0] % 8
            bctr[0] += 1
            return i

        def bk(i):
            return ('ps', i)

        def mkalloc(pool):
            c = [0]

            def f():
                i = pool[c[0] % len(pool)]
                c[0] += 1
                return i
            return f

        def MM(out, lhsT, rhs, start, stop, reads, bank):
            P.add('pe', lambda e: e.matmul(out, lhsT, rhs, start=start, stop=stop, skip_group_check=True),
                  reads=reads, writes=[bk(bank)])

        def TR(out, in_, reads, bank):
            P.add('pe', lambda e: e.transpose(out, in_, ident[0:in_.shape[0], 0:in_.shape[0]]),
                  reads=list(reads) + ['ident'], writes=[bk(bank)])

        def ACT(out, in_, func, reads, writes, scale=1.0, bias=0.0, accum=None):
            if accum is None:
                P.add('act', lambda e: e.activation(out, in_, func, bias=bias, scale=scale), reads=reads, writes=writes)
            else:
                P.add('act', lambda e: e.activation(out, in_, func, bias=bias, scale=scale, accum_out=accum),
                      reads=reads, writes=writes)

        def TT(eng, out, in0, in1, op, reads, writes):
            P.add(eng, lambda e: e.tensor_tensor(out, in0, in1, op), reads=reads, writes=writes)

        def STT(out, in0, scalar, in1, op0, op1, reads, writes):
            P.add('dve', lambda e: e.scalar_tensor_tensor(out, in0, scalar, in1, op0, op1), reads=reads, writes=writes)

        def TS(eng, out, in0, s1, s2, op0, op1, reads, writes):
            if s2 is None:
                P.add(eng, lambda e: e.tensor_scalar(out, in0, s1, None, op0), reads=reads, writes=writes)
            else:
                P.add(eng, lambda e: e.tensor_scalar(out, in0, s1, s2, op0, op1), reads=reads, writes=writes)

        def CP(eng, out, in_, reads, writes):
            if eng == 'act':
                P.add('act', lambda e: e.copy(out, in_), reads=reads, writes=writes)
            else:
                P.add(eng, lambda e: e.tensor_copy(out, in_), reads=reads, writes=writes)

        def MEMSET(eng, ap, val, writes):
            P.add(eng, lambda e: e.memset(ap, val), writes=writes)

        def DMA(q, out, in_, key, reads=(), writes=(), slow=False):
            if slow:
                return P.add(q, lambda e: e.dma_start(out=out, in_=in_, allow_slow_non_contiguous=True),
                             reads=reads, writes=writes, dma=key)
            return P.add(q, lambda e: e.dma_start(out=out, in_=in_), reads=reads, writes=writes, dma=key)

        cast_order = (['rq0', 'rq1', 'rk0', 'w_mk0', 'rk1', 'w_mk1', 'rv0', 'w_mv0', 'rv1', 'w_mv1',
                       'rg0', 'rg1', 'sq0', 'sq1', 'skd', 'skv',
                       'ga0', 'ga1', 'gb0', 'gb1', 'w_ret_out0', 'w_ret_out1', 'w_swa_out0', 'w_swa_out1',
                       'w_mix_out0', 'w_mix_out1', 'w_cq0', 'w_cq1', 'w_co0', 'w_co1'] +
                      ['gu%d' % i for i in range(11)] +
                      ['dn%d_%d' % (h, s) for h in range(2) for s in range(3)])
        cast_idx = {nm: i for i, nm in enumerate(cast_order)}
        cast_done = [0]
        LOOK = 10

        def cast_upto(n):
            n = min(n, len(cast_order))
            while cast_done[0] < n:
                nm = cast_order[cast_done[0]]
                cast_done[0] += 1
                src, k0, nk, cols = chunks[nm]
                key = ('cast', nm)
                c0 = 0
                if src == 'w_gu':
                    col0, ncw = cols[0]
                    for wn in ('w_gate', 'w_up'):
                        sap = W[wn][k0 * 128:(k0 + nk) * 128, col0:col0 + ncw].rearrange("(kt p) n -> p kt n", p=128)
                        DMA('pool', wb[nm][:, :, c0:c0 + ncw], sap, key)
                        c0 += ncw
                else:
                    for (col0, ncw) in cols:
                        sap = W[src][k0 * 128:(k0 + nk) * 128, col0:col0 + ncw].rearrange("(kt p) n -> p kt n", p=128)
                        DMA('pool', wb[nm][:, :, c0:c0 + ncw], sap, key)
                        c0 += ncw
                P.lastw[('wb', nm)] = ('d', key, ALLN)

        cast_upto(LOOK)

        wctr = [0]

        def wload(nm):
            cast_upto(cast_idx[nm] + LOOK)
            i = wctr[0] % 3
            wctr[0] += 1
            nk = chunks[nm][2]
            ncol = wb[nm].shape[2]
            DMA('sp', wsl[i][:, 0:nk, 0:ncol], wb[nm], ('wl', i), reads=[('wb', nm)], writes=[('wsl', i)])
            return i

        def cload(dst, src, key, wkey, slow=False):
            DMA('sp', dst, src, key, writes=[wkey], slow=slow)

        cload(ident[:], cd['ident'], 'c0', 'ident')
        cload(dmask[:], cd['dmask'], 'c1', 'dmask')
        cload(qdec[:], cd['qdec'], 'c2', 'qdec')
        cload(kd128[:], cd['kd128'], 'c3', 'kd128')
        cload(kd64[:], cd['kd64'], 'c4', 'kd64')
        cload(gfin[:], gvec['g_final'].partition_broadcast(128), 'c5', 'gfin')
        for i, n in enumerate(('g_attn', 'g_cross', 'g_mem', 'g_ffn')):
            cload(gT[n][:].unsqueeze(2), gvec[n].rearrange("(kt p o) -> p kt o", p=128, o=1), 'c6%d' % i, ('gT', n), slow=True)
        MEMSET('dve', tabf[:], 0.0, ['tabf'])
        cload(tabf[0:32, :], relb, 'c7', 'tabf')
        sk2 = sinks.rearrange("(t two o) -> two t o", two=2, o=1)
        cload(esink[0:64, :].unsqueeze(2), sk2[0].partition_broadcast(64), 'c8', 'esink_a', slow=True)
        cload(esink[64:128, :].unsqueeze(2), sk2[1].partition_broadcast(64), 'c9', 'esink_b', slow=True)
        ACT(esink[:], esink[:], AF.Exp, ['esink_a', 'esink_b'], ['esink'])
        MEMSET('dve', ones[:], 1.0, ['ones'])
        MEMSET('dve', epsb[:, 0:1], EPS, ['epsb'])
        MEMSET('dve', epsb[:, 1:2], 256.0 * EPS, ['epsb'])
        MEMSET('dve', onesp[:], 0.0, ['onesp'])
        MEMSET('dve', onesp[:, 0, 0:64], 1.0, ['onesp'])
        MEMSET('dve', onesp[:, 1, 64:128], 1.0, ['onesp'])
        MEMSET('dve', KT[:], 0.0, ['KT'])
        MEMSET('dve', VP[:], 0.0, ['VP'])
        CP('dve', tabh[:], tabf[:], ['tabf'], ['tabh'])
        CP('dve', tabr[:], tabh[:], ['tabh'], ['tabr'])
        TT('dve', tabr[:], tabf[:], tabr[:], ALU.subtract, ['tabf', 'tabr'], ['tabr'])
        CP('dve', tabl[:], tabr[:], ['tabr'], ['tabl'])
        def eb_stages():
            ohv = slab[S_SWA][:].rearrange("p a b -> p (a b)").rearrange("p (q k) -> p q k", k=128)
            st_ = []

            def s0():
                MEMSET('dve', ohv, 0.0, [('slab', S_SWA)])
            st_.append(s0)

            def mk(v, hq):
                def f():
                    old = P.strict
                    P.strict = True
                    DMA('sp', ohv[0:32], cd['oh'][v, :, hq * 32:(hq + 1) * 32, :], 'coh', writes=[('slab', S_SWA)])
                    b0 = nb()
                    for qq in range(32):
                        o = banks[b0][:, qq * 16:(qq + 1) * 16]
                        MM(o, ohv[:, qq, :], tabh[:], True, False, [('slab', S_SWA), 'tabh'], b0)
                        MM(o, ohv[:, qq, :], tabl[:], False, True, [('slab', S_SWA), 'tabl'], b0)
                    src_v = banks[b0][:].rearrange("p (q kv tq par) -> p kv par tq q", q=32, kv=2, tq=4, par=2)
                    dst_v = EB[v][:, :, hq * 32:(hq + 1) * 32].rearrange("p (kv par tq) q -> p kv par tq q", kv=2, par=2, tq=4)
                    for kv in range(2):
                        for par in range(2):
                            for tq in range(4):
                                if tq % 2 == 0:
                                    ACT(dst_v[:, kv, par, tq, :], src_v[:, kv, par, tq, :], AF.Copy, [], [bk(b0), ('EB', v)], scale=8.0)
                                else:
                                    TS('dve', dst_v[:, kv, par, tq, :], src_v[:, kv, par, tq, :], 8.0, None, ALU.mult, None, [],
                                       [bk(b0), ('EB', v)])
                    if v == 3 and hq == 1:
                        MEMSET('dve', EB[1][64:128], -400.0, [('EB', 1)])
                        MEMSET('dve', EB[3][0:64], -400.0, [('EB', 3)])
                    P.strict = old
                return f
            for v in range(4):
                for hq in range(2):
                    st_.append(mk(v, hq))
            return st_

        def norm_a(tt, xa, xk, pt, col):
            ACT(junk[0:pt, :], xa, AF.Square, [xk], ['u1', ('ssq', col)], accum=ssq[0:pt, col:col + 1])
            ACT(rstd[0:pt, col:col + 1], ssq[0:pt, col:col + 1], AF.Ln, [('ssq', col), 'epsb'], [('rstd', col)], scale=1.0 / D, bias=epsb[0:pt, 0:1])
            ACT(rstd[0:pt, col:col + 1], rstd[0:pt, col:col + 1], AF.Exp, [('rstd', col)], [('rstd', col)], scale=-0.5)
            TS('dve', xnbs[tt % 3][0:pt, :], xa, rstd[0:pt, col:col + 1], None, ALU.mult, None, [xk, ('rstd', col)],
               [('xnb', tt % 3)] + (['u1'] if tt % 3 == 2 else []))

        def norm_b(tt, pt, gname, dst, out_fn=None, out_keys=None):
            b0 = nb()
            pv = banks[b0][:].bitcast(BF16)
            xb = xnbs[tt % 3]
            for kt in range(8):
                TR(pv[:, kt * 128:kt * 128 + pt], xb[0:pt, kt * 128:(kt + 1) * 128], [('xnb', tt % 3)], b0)
            for kt in range(8):
                eng = 'act' if kt % 2 == 0 else 'dve'
                o = slab[dst][:, kt, tt * 128:tt * 128 + pt] if out_fn is None else out_fn(kt)
                wk = [('slab', dst)] if out_keys is None else out_keys
                i_ = pv[:, kt * 128:kt * 128 + pt]
                if eng == 'act':
                    ACT(o, i_, AF.Copy, [('gT', gname)], [bk(b0)] + wk, scale=gT[gname][:, kt:kt + 1])
                else:
                    TS('dve', o, i_, gT[gname][:, kt:kt + 1], None, ALU.mult, None, [('gT', gname)], [bk(b0)] + wk)

        def norm_T(tiles, pts, gname, dst):
            for tt, ((xa, xk), pt) in enumerate(zip(tiles, pts)):
                norm_a(tt, xa, xk, pt, tt)
                norm_b(tt, pt, gname, dst)

        def proj_fm(wname, ntile, src_slab, N, col_base=0, alloc=None):
            wi = wload(wname)
            outs = []
            for j in range(ntile):
                b0 = (alloc or nb)()
                for kt in range(8):
                    MM(banks[b0][:, 0:N], wsl[wi][:, kt, col_base + j * 128:col_base + (j + 1) * 128],
                       slab[src_slab][:, kt, 0:N], kt == 0, kt == 7, [('wsl', wi), ('slab', src_slab)], b0)
                outs.append(b0)
                yield j, b0

        def mem_finish():
            for kt2 in range(2):
                CP('act', vtok[:, kt2, :], XT[xoff[0] + kt2][:], [('xt', xoff[0] + kt2)], ['vtok'])
                CP('dve', mvt[:, kt2, :], XT[xoff[0] + 2 + kt2][:], [('xt', xoff[0] + 2 + kt2)], ['mvt'])
            for kt2 in range(2):
                b0 = nb()
                pv = banks[b0][:].bitcast(BF16)
                for t8 in range(8):
                    TR(pv[:, t8 * 128:(t8 + 1) * 128], vtok[:, kt2, t8 * 128:(t8 + 1) * 128], ['vtok'], b0)
                CP('act', mkT[:, :, kt2 * 128:(kt2 + 1) * 128], pv.rearrange("p (t k) -> p t k", k=128), [], [bk(b0), 'mkT'])

        def mem_prompt(b):
            for tt in range(2):
                DMA('sp', XT[xoff[0] + tt][:], memp[b, tt * 128:(tt + 1) * 128, :], ('xl', xoff[0] + tt), writes=[('xt', xoff[0] + tt)])
            norm_T([(XT[xoff[0] + 0][:], ('xt', xoff[0] + 0)), (XT[xoff[0] + 1][:], ('xt', xoff[0] + 1))], [128, 128], 'g_mem', S_XN)
            for wi_, (wn, dst0, outap) in enumerate((('w_mk', 0, mk_p), ('w_mv', 2, mv_p))):
                for hf in range(2):
                    wi = wload('%s%d' % (wn, hf))
                    for tt in range(2):
                        b0 = nb()
                        for kt in range(8):
                            MM(banks[b0][:], slab[S_XN][:, kt, tt * 128:(tt + 1) * 128], wsl[wi][:, kt, :], kt == 0, kt == 7,
                               [('wsl', wi), ('slab', S_XN)], b0)
                        eng = 'act' if (hf + tt) % 2 == 0 else 'dve'
                        CP(eng, XT[xoff[0] + dst0 + tt][:, hf * 512:(hf + 1) * 512], banks[b0][:], [], [bk(b0), ('xt', xoff[0] + dst0 + tt)])
                for tt in range(2):
                    DMA('pool', outap[b, tt * 128:(tt + 1) * 128, :], XT[xoff[0] + dst0 + tt][:], ('xo', xoff[0] + dst0 + tt), reads=[('xt', xoff[0] + dst0 + tt)])
            mem_finish()

        def mem_sample(b):
            for tt in range(2):
                DMA('sp', XT[xoff[0] + tt][:], cmk[b, tt * 128:(tt + 1) * 128, :], ('xl', xoff[0] + tt), writes=[('xt', xoff[0] + tt)])
                DMA('sp', XT[xoff[0] + 2 + tt][:], cmv[b, tt * 128:(tt + 1) * 128, :], ('xl', xoff[0] + 2 + tt), writes=[('xt', xoff[0] + 2 + tt)])
            mem_finish()

        def mem_stages(b, kind):
            xo = xoff[0] ^ 4
            flat = ebb[:].rearrange("p a b -> p (a b)")
            xm = flat[:, 0:2048].rearrange("p (k t) -> p k t", t=256)
            mkb = flat[:, 2048:4096].rearrange("p (t d) -> p t d", d=D)
            EBK = [('eb', i) for i in range(8)]
            st_ = []
            if kind == 'p':
                def s_load():
                    for tt in range(2):
                        DMA('sp', XT[xo + tt][:], memp[b, tt * 128:(tt + 1) * 128, :], ('xl', xo + tt), writes=[('xt', xo + tt)])
                st_.append(s_load)

                def mk_norm(tt):
                    def f():
                        norm_a(tt, XT[xo + tt][:], ('xt', xo + tt), 128, tt)
                        norm_b(tt, 128, 'g_mem', None, out_fn=lambda kt: xm[:, kt, tt * 128:(tt + 1) * 128], out_keys=EBK)
                    return f
                st_.append(mk_norm(0))
                st_.append(mk_norm(1))

                def mk_proj(wn, dst0, outap, hf):
                    def f():
                        wi = wload('%s%d' % (wn, hf))
                        for tt in range(2):
                            b0 = nb()
                            for kt in range(8):
                                MM(banks[b0][:], xm[:, kt, tt * 128:(tt + 1) * 128], wsl[wi][:, kt, :], kt == 0, kt == 7,
                                   [('wsl', wi)] + EBK, b0)
                            eng = 'act' if (hf + tt) % 2 == 0 else 'dve'
                            CP(eng, XT[xo + dst0 + tt][:, hf * 512:(hf + 1) * 512], banks[b0][:], [], [bk(b0), ('xt', xo + dst0 + tt)])
                        if hf == 1:
                            for tt in range(2):
                                DMA('pool', outap[b, tt * 128:(tt + 1) * 128, :], XT[xo + dst0 + tt][:], ('xo', xo + dst0 + tt),
                                    reads=[('xt', xo + dst0 + tt)])
                    return f
                for (wn, dst0, outap) in (('w_mk', 0, mk_p), ('w_mv', 2, mv_p)):
                    for hf in range(2):
                        st_.append(mk_proj(wn, dst0, outap, hf))
            else:
                def s_load():
                    for tt in range(2):
                        DMA('sp', XT[xo + tt][:], cmk[b, tt * 128:(tt + 1) * 128, :], ('xl', xo + tt), writes=[('xt', xo + tt)])
                        DMA('sp', XT[xo + 2 + tt][:], cmv[b, tt * 128:(tt + 1) * 128, :], ('xl', xo + 2 + tt), writes=[('xt', xo + 2 + tt)])
                st_.append(s_load)

            def s_fin():
                for kt2 in range(2):
                    CP('act', mkb[:, kt2, :], XT[xo + kt2][:], [('xt', xo + kt2)], EBK)
                    CP('dve', mvt[:, kt2, :], XT[xo + 2 + kt2][:], [('xt', xo + 2 + kt2)], ['mvt'])
                for kt2 in range(2):
                    b0 = nb()
                    pv = banks[b0][:].bitcast(BF16)
                    for t8 in range(8):
                        TR(pv[:, t8 * 128:(t8 + 1) * 128], mkb[:, kt2, t8 * 128:(t8 + 1) * 128], EBK, b0)
                    CP('act', mkT[:, :, kt2 * 128:(kt2 + 1) * 128], pv.rearrange("p (t k) -> p t k", k=128), [], [bk(b0), 'mkT'])
            st_.append(s_fin)
            return st_

        def pro_load(xsrc, N, pos0):
            NT = (N + 127) // 128
            pts = [min(128, N - 128 * t) for t in range(NT)]
            for tt in range(NT):
                DMA('sp', XT[xoff[0] + tt][0:pts[tt], :], xsrc[tt * 128:tt * 128 + pts[tt], :], ('xl', xoff[0] + tt), writes=[('xt', xoff[0] + tt)])
            DMA('sp', cosb[:, 0:N], cd['cosT'][:, pos0:pos0 + N], 'cosl', writes=['cosb'])
            DMA('sp', sinb[:, 0:N], cd['sinT'][:, pos0:pos0 + N], 'sinl', writes=['sinb'])

        def pro_norm(N):
            NT = (N + 127) // 128
            pts = [min(128, N - 128 * t) for t in range(NT)]
            xtl = [(XT[xoff[0] + t][0:pts[t], :], ('xt', xoff[0] + t)) for t in range(NT)]
            norm_T(xtl, pts, 'g_attn', S_XN)

        def block(xsrc, ydst, N, pos0, units, first_chunk, is_last, kind, b, pro_done=False, nxt=None, mstages=None):
            mstages = list(mstages or [])

            mper = 2 if len(mstages) > 9 else 1

            def mstep(n=None):
                for _ in range(n or mper):
                    if mstages:
                        mstages.pop(0)()
            NT = (N + 127) // 128
            pts = [min(128, N - 128 * t) for t in range(NT)]
            if not pro_done:
                pro_load(xsrc, N, pos0)
                pro_norm(N)
            xtl = [(XT[xoff[0] + t][0:pts[t], :], ('xt', xoff[0] + t)) for t in range(NT)]

            for nm, dst in (('rq', S_Q), ('rk', S_K)):
                for hf in range(2):
                    pend = []
                    for j, b0 in proj_fm('%s%d' % (nm, hf), 4, S_XN, N):
                        pend.append(b0)
                        if len(pend) == 2:
                            bA, bB = pend
                            pend = []
                            tA = hf * 4 + j - 1
                            A = banks[bA][:, 0:N]; B = banks[bB][:, 0:N]
                            cs = cosb[:, 0:N]; sn = sinb[:, 0:N]
                            TT('dve', rt[0][:, 0:N], A, cs, ALU.mult, ['cosb'], [bk(bA), 'rt0'])
                            TT('dve', rt[1][:, 0:N], B, sn, ALU.mult, ['sinb'], [bk(bB), 'rt1'])
                            TT('dve', rt[2][:, 0:N], B, cs, ALU.mult, ['cosb'], [bk(bB), 'rt2'])
                            TT('dve', rt[3][:, 0:N], A, sn, ALU.mult, ['sinb'], [bk(bA), 'rt3'])
                            TT('pool', slab[dst][:, tA, 0:N], rt[0][:, 0:N], rt[1][:, 0:N], ALU.subtract,
                               ['rt0', 'rt1'], [('slab', dst)])
                            TT('pool', slab[dst][:, tA + 1, 0:N], rt[2][:, 0:N], rt[3][:, 0:N], ALU.add,
                               ['rt2', 'rt3'], [('slab', dst)])
                    mstep()
            mstep()
            for (u0, C) in units:
                TT('pool', slab[S_QS][:, :, u0:u0 + C].rearrange("p (h t) c -> p h t c", t=2),
                   slab[S_Q][:, :, u0:u0 + C].rearrange("p (h t) c -> p h t c", t=2),
                   qdec[:, :, 0:C].unsqueeze(2).to_broadcast([128, 4, 2, C]), ALU.mult,
                   [('slab', S_Q), 'qdec'], [('slab', S_QS)])
            for hf in range(2):
                wi = wload('rv%d' % hf)
                for tt in range(NT):
                    b0 = nb()
                    pt = pts[tt]
                    for kt in range(8):
                        MM(banks[b0][0:pt, :], slab[S_XN][:, kt, tt * 128:tt * 128 + pt], wsl[wi][:, kt, :], kt == 0, kt == 7,
                           [('wsl', wi), ('slab', S_XN)], b0)
                    CP('act', vtok[0:pt, tt, hf * 512:(hf + 1) * 512], banks[b0][0:pt, :], [], [bk(b0), 'vtok'])
                mstep()
            for hf in range(2):
                for j, b0 in proj_fm('rg%d' % hf, 4, S_XN, N):
                    t8 = hf * 4 + j
                    k = t8 % 2
                    ACT(tb[k][:, 0:N], banks[b0][:, 0:N], AF.Tanh, [], [bk(b0), ('tb', k)], scale=0.5)
                    STT(slab[S_RG][:, t8, 0:N], tb[k][:, 0:N], 1.0, banks[b0][:, 0:N], ALU.add, ALU.mult,
                        [('tb', k)], [bk(b0), ('slab', S_RG)])
            mstep(99)
            ra = mkalloc([0, 1])
            rg4 = mkalloc([0, 1, 2, 3])
            sa = mkalloc([4, 5, 6, 7])

            def gen_projB():
                for hf in range(2):
                    for j, b0 in proj_fm('sq%d' % hf, 4, S_XN, N, alloc=sa):
                        CP('act', slab[S_SQ][:, hf * 4 + j, 0:N], banks[b0][:, 0:N], [], [bk(b0), ('slab', S_SQ)])
                        yield
                for j, b0 in proj_fm('skd', 2, S_XN, N, alloc=sa):
                    kv = j
                    for var in range(2):
                        lo = var * 64
                        CP('act' if var == 0 else 'dve', KT[lo:lo + 64, kv, var, 128:128 + N], banks[b0][lo:lo + 64, 0:N], [], [bk(b0), 'KT'])
                    yield
                wi = wload('skv')
                for tt in range(NT):
                    b0 = sa()
                    pt = pts[tt]
                    for kt in range(8):
                        MM(banks[b0][0:pt, 0:256], slab[S_XN][:, kt, tt * 128:tt * 128 + pt], wsl[wi][:, kt, 0:256], kt == 0, kt == 7,
                           [('wsl', wi), ('slab', S_XN)], b0)
                    for kv in range(2):
                        CP('act', VP[0:pt, 1 + tt, kv, 0, 0:64], banks[b0][0:pt, 128 + kv * 64:128 + (kv + 1) * 64], [], [bk(b0), 'VP'])
                        CP('dve', VP[0:pt, 1 + tt, kv, 1, 64:128], banks[b0][0:pt, 128 + kv * 64:128 + (kv + 1) * 64], [], [bk(b0), 'VP'])
                    if is_last and tt == NT - 1:
                        CP('act', f32o[0:pt, :], banks[b0][0:pt, 0:256], [], [bk(b0), 'f32o'])
                        ok, ov = (swk_p, swv_p) if kind == 'p' else (swk_s, swv_s)
                        r0 = 128 - pt
                        DMA('pool', ok[b, r0:128, :], f32o[0:pt, 0:128], 'fo1', reads=['f32o'])
                        DMA('pool', ov[b, r0:128, :], f32o[0:pt, 128:256], 'fo2', reads=['f32o'])
                    yield

            def gen_ret():
                for ui, (u0, C) in enumerate(units):
                    tt = u0 // 128
                    kdv = kd128 if C == 128 else kd64
                    kdk = 'kd128' if C == 128 else 'kd64'
                    gam = G128 if C == 128 else G64
                    kb = ktok[ui % 2]
                    kbk = ('ktok', ui % 2)
                    b0 = ra()
                    pv = banks[b0][:].bitcast(BF16)
                    for t8 in range(8):
                        TR(pv[0:C, t8 * 128:(t8 + 1) * 128], slab[S_K][:, t8, u0:u0 + C], [('slab', S_K)], b0)
                    for h in range(4):
                        if h % 2 == 0:
                            ACT(kb[0:C, h, :], pv[0:C, h * 256:(h + 1) * 256], AF.Copy, [kdk], [bk(b0), kbk], scale=kdv[0:C, h:h + 1])
                        else:
                            TS('dve', kb[0:C, h, :], pv[0:C, h * 256:(h + 1) * 256], kdv[0:C, h:h + 1], None, ALU.mult, None,
                               [kdk], [bk(b0), kbk])
                    b1 = ra()
                    for h in range(4):
                        for dt in range(2):
                            MM(banks[b1][0:C, h * 128:h * 128 + C], slab[S_K][:, 2 * h + dt, u0:u0 + C],
                               slab[S_Q][:, 2 * h + dt, u0:u0 + C], dt == 0, dt == 1, [('slab', S_K), ('slab', S_Q)], b1)
                    TT('dve', smb[0:C, :, 0:C], banks[b1][0:C, :].rearrange("p (h i) -> p h i", i=128)[:, :, 0:C],
                       dmask[0:C, :, 0:C], ALU.mult, ['dmask'], [bk(b1), 'smb'])
                    yield
                    bo = [2, 3]
                    for h in range(4):
                        for et in range(2):
                            bb = bo[h // 2]
                            o = banks[bb][:, ((h % 2) * 2 + et) * 128:((h % 2) * 2 + et) * 128 + C]
                            MM(o, vtok[0:C, tt, h * 256 + et * 128:h * 256 + (et + 1) * 128], smb[0:C, h, 0:C], True, False,
                               ['vtok', 'smb'], bb)
                            for dt in range(2):
                                MM(o, stb[:, h, dt, et * 128:(et + 1) * 128], slab[S_QS][:, 2 * h + dt, u0:u0 + C], False, dt == 1,
                                   ['stb', ('slab', S_QS)], bb)
                    for h in range(4):
                        b3 = ra()
                        for dt in range(2):
                            MM(banks[b3][:, dt * 256:(dt + 1) * 256], kb[0:C, h, dt * 128:(dt + 1) * 128],
                               vtok[0:C, tt, h * 256:(h + 1) * 256], True, True, [kbk, 'vtok'], b3)
                        sth = st[:, h].rearrange("p t e -> p (t e)")
                        STT(sth, sth, gam[h], banks[b3][:], ALU.mult, ALU.add, ['st'], [bk(b3), 'st'])
                    CP('pool', stb[:].rearrange("p h t e -> p (h t e)"), st[:].rearrange("p h t e -> p (h t e)"), ['st'], ['stb'])
                    yield
                    for i2 in range(2):
                        ACT(osq[:, i2 * 4:(i2 + 1) * 4, 0:C], banks[bo[i2]][:].rearrange("p (t i) -> p t i", i=128)[:, :, 0:C],
                            AF.Square, [], [bk(bo[i2]), 'osq'])
                    b2 = ra()
                    for h in range(4):
                        for et in range(2):
                            MM(banks[b2][:, h * 128:h * 128 + C], ones[:], osq[:, 2 * h + et, 0:C], et == 0, et == 1, ['ones', 'osq'], b2)
                    ACT(rsd[:, :, 0:C], banks[b2][:].rearrange("p (h i) -> p h i", i=128)[:, :, 0:C], AF.Ln, ['epsb'], [bk(b2), 'rsd'],
                        bias=epsb[:, 1:2])
                    ACT(rsd[:, :, 0:C], rsd[:, :, 0:C], AF.Exp, ['rsd'], ['rsd'], scale=-0.5)
                    yield
                    for i2 in range(2):
                        TT('dve', u1[:, i2 * 4:(i2 + 1) * 4, 0:C].rearrange("p (h t) c -> p h t c", t=2),
                           banks[bo[i2]][:].rearrange("p (h t i) -> p h t i", t=2, i=128)[:, :, :, 0:C],
                           rsd[:, i2 * 2:(i2 + 1) * 2, 0:C].unsqueeze(2).to_broadcast([128, 2, 2, C]), ALU.mult,
                           ['rsd'], [bk(bo[i2]), 'u1'])
                    STT(slab[S_RG][:, :, u0:u0 + C], u1[:, :, 0:C], 8.0, slab[S_RG][:, :, u0:u0 + C], ALU.mult, ALU.mult,
                        ['u1', ('slab', S_RG)], [('slab', S_RG)])
                    yield
                if is_last:
                    ro = rst_p if kind == 'p' else rst_s
                    DMA('pool', ro[b].rearrange("h (t p) e -> p h t e", p=128), st[:], 'sto', reads=['st'])

            def gen_swa():
                nch = N // 64
                for ci in range(nch):
                    n_glob = first_chunk + ci
                    c = ci // 2
                    q0 = ci * 64
                    if ci % 2 == 0:
                        tiles = [(128 * c, c, 0), (128 + 128 * c, c + 1, 1)]
                        if n_glob == 0:
                            tiles = tiles[1:]
                    else:
                        tiles = [(128 + 128 * c, c + 1, 2), (128 * c, c, 3)]
                        if n_glob == 1:
                            tiles = tiles[:1]
                    allebs = []
                    for kv in range(2):
                        ebs = []
                        for xi, (kc, vpi, ebv) in enumerate(tiles):
                            b0 = sa()
                            for par in range(2):
                                o_ = banks[b0][:, par * 256:(par + 1) * 256]
                                MM(o_, KT[:, kv, par, kc:kc + 128],
                                   slab[S_SQ][:, kv * 4:kv * 4 + 4, q0:q0 + 64], True, False, ['KT', ('slab', S_SQ)], b0)
                                MM(o_, ident[:], EB[ebv][:, kv * 8 + par * 4:kv * 8 + par * 4 + 4, :], False, True,
                                   ['ident', ('EB', ebv)], b0)
                            bi = (ci % 2) * 4 + kv * 2 + xi
                            ACT(eb[bi][:], banks[b0][:], AF.Exp, [], [bk(b0), ('eb', bi)], scale=0.125)
                            ebs.append((bi, vpi))
                        allebs.append(ebs)
                        yield
                    bpo = sa()
                    bpd = sa()
                    for kv in range(2):
                        ebs = allebs[kv]
                        nmm = len(ebs) * 2
                        for which, bb in ((0, bpo), (1, bpd)):
                            k = 0
                            for (bi, vpi) in ebs:
                                for par in range(2):
                                    lhs = VP[:, vpi, kv, par, :] if which == 0 else onesp[:, par, :]
                                    MM(banks[bb][:, kv * 256:(kv + 1) * 256], lhs, eb[bi][:, par * 256:(par + 1) * 256],
                                       k == 0, k == nmm - 1, ['VP', 'onesp', ('eb', bi)], bb)
                                    k += 1
                    TT('dve', den[:].rearrange("p (t q) -> p t q", q=64), banks[bpd][:].rearrange("p (t q) -> p t q", q=64),
                       esink[:].unsqueeze(2).to_broadcast([128, 8, 64]), ALU.add, ['esink'], [bk(bpd), 'rt2'])
                    ACT(rden[:], den[:], AF.Ln, ['rt2'], ['rt3'])
                    ACT(rden[:], rden[:], AF.Exp, ['rt3'], ['rt3'], scale=-1.0)
                    TT('dve', slab[S_SWA][:, :, q0:q0 + 64], banks[bpo][:].rearrange("p (t q) -> p t q", q=64),
                       rden[:].rearrange("p (t q) -> p t q", q=64), ALU.mult, ['rt3'], [bk(bpo), ('slab', S_SWA)])
                    yield
                if N == 512:
                    CP('pool', KT[:, :, :, 0:128], KT[:, :, :, 512:640], ['KT'], ['KT'])
                    CP('pool', VP[:, 0], VP[:, 4], ['VP'], ['VP'])

            def gen_gates():
                for gi, (nm, dst) in enumerate((('ga', S_Q), ('gb', S_K))):
                    for hf in range(2):
                        for j, b0 in proj_fm('%s%d' % (nm, hf), 4, S_XN, N, alloc=rg4):
                            ACT(slab[dst][:, hf * 4 + j, 0:N], banks[b0][:, 0:N], AF.Tanh, [], [bk(b0), ('slab', dst)], scale=0.5)
                            yield
                for hf in range(2):
                    for j, b0 in proj_fm('w_ret_out%d' % hf, 4, S_RG, N, alloc=rg4):
                        t8 = hf * 4 + j
                        STT(slab[S_Q][:, t8, 0:N], slab[S_Q][:, t8, 0:N], 1.0, banks[b0][:, 0:N], ALU.add, ALU.mult,
                            [('slab', S_Q)], [bk(b0), ('slab', S_Q)])
                        yield

            def chain(*gs):
                for g in gs:
                    for _ in g:
                        yield

            pb_left = [14]

            def gen_b():
                for _ in gen_projB():
                    pb_left[0] -= 1
                    yield
                pb_left[0] = 0
                for _ in gen_swa():
                    yield

            sB = [gen_b(), RR_B]
            streams = [[chain(gen_ret(), gen_gates()), RR_A], sB]
            while streams:
                for sg in list(streams):
                    nstep = sg[1] * (2 if (sg is sB and pb_left[0] > 0) else 1)
                    for _ in range(nstep):
                        try:
                            next(sg[0])
                        except StopIteration:
                            streams.remove(sg)
                            break

            if dbg == 'eb' and N == 512:
                for v in range(4):
                    DMA('pool', ydst[v * 128:(v + 1) * 128, :], EB[v][:].rearrange("p h q -> p (h q)"), ('xo', xoff[0] + 0), reads=[('EB', v)])
                return
            if dbg in ('swa', 'ret'):
                src = slab[S_SWA] if dbg == 'swa' else slab[S_RG]
                nt_ = min(N, 128)
                for t8 in range(8):
                    for hh in range(nt_ // 64):
                        DMA('pool', ydst[hh * 64:(hh + 1) * 64, t8 * 128:(t8 + 1) * 128].rearrange("tok p -> p tok"),
                            src[:, t8, hh * 64:(hh + 1) * 64], ('xo', xoff[0] + 0), reads=[('slab', S_SWA), ('slab', S_RG)], slow=True)
                return
            for hf in range(2):
                for j, b0 in proj_fm('w_swa_out%d' % hf, 4, S_SWA, N):
                    t8 = hf * 4 + j
                    k = t8 % 2
                    STT(sf[k][:, 0:N], slab[S_K][:, t8, 0:N], 1.0, banks[b0][:, 0:N], ALU.add, ALU.mult,
                        [('slab', S_K)], [bk(b0), 'rt%d' % k])
                    TT('pool', slab[S_Q][:, t8, 0:N], slab[S_Q][:, t8, 0:N], sf[k][:, 0:N], ALU.add,
                       [('slab', S_Q), 'rt%d' % k], [('slab', S_Q)])

            def proj_tm_resid(wn, src_slab, scale, next_g=None):
                wis = [wload('%s%d' % (wn, hf)) for hf in range(2)]
                for tt in range(NT):
                    pt = pts[tt]
                    for hf in range(2):
                        wi = wis[hf]
                        b0 = nb()
                        for kt in range(8):
                            MM(banks[b0][0:pt, :], slab[src_slab][:, kt, tt * 128:tt * 128 + pt], wsl[wi][:, kt, :], kt == 0, kt == 7,
                               [('wsl', wi), ('slab', src_slab)], b0)
                        xa = XT[xoff[0] + tt][0:pt, hf * 512:(hf + 1) * 512]
                        STT(xa, banks[b0][0:pt, :], scale, xa, ALU.mult, ALU.add, [('xt', xoff[0] + tt)], [bk(b0), ('xt', xoff[0] + tt)])
                    if next_g is not None:
                        norm_a(tt, XT[xoff[0] + tt][0:pt, :], ('xt', xoff[0] + tt), pt, tt)
                        if tt >= 2:
                            norm_b(tt - 2, pts[tt - 2], next_g, S_XN)
                if next_g is not None:
                    for t_ in range(max(0, NT - 2), NT):
                        norm_b(t_, pts[t_], next_g, S_XN)

            proj_tm_resid('w_mix_out', S_Q, 0.5, None if dbg == 'mix' else 'g_cross')

            def dbg_store():
                for tt in range(NT):
                    DMA('pool', ydst[tt * 128:tt * 128 + pts[tt], :], XT[xoff[0] + tt][0:pts[tt], :], ('xo', xoff[0] + tt), reads=[('xt', xoff[0] + tt)])
            if dbg == 'mix':
                dbg_store()
                return

            if nxt is not None:
                xoff[0] ^= 4
                pro_load(*nxt)
                xoff[0] ^= 4
            def cross_finish(h):
                ce = ceb[h]
                cks = [('eb', 2 * h), ('eb', 2 * h + 1)]
                bd = nb()
                for kt2 in range(2):
                    MM(banks[bd][:, 0:N], ones[:], ce[:, kt2, 0:N], kt2 == 0, kt2 == 1, ['ones'] + cks, bd)
                ACT(rt[h][:, 0:N], banks[bd][:, 0:N], AF.Ln, [], [bk(bd), 'rt%d' % h])
                ACT(rt[h][:, 0:N], rt[h][:, 0:N], AF.Exp, ['rt%d' % h], ['rt%d' % h], scale=-1.0)
                for dt in range(2):
                    b2 = nb()
                    for kt2 in range(2):
                        MM(banks[b2][:, 0:N], mvt[:, kt2, h * 256 + dt * 128:h * 256 + (dt + 1) * 128], ce[:, kt2, 0:N],
                           kt2 == 0, kt2 == 1, ['mvt'] + cks, b2)
                    TT('dve', slab[S_K][:, h * 2 + dt, 0:N], banks[b2][:, 0:N], rt[h][:, 0:N], ALU.mult, ['rt%d' % h],
                       [bk(b2), ('slab', S_K)])

            for hf in range(2):
                for j, b0 in proj_fm('w_cq%d' % hf, 4, S_XN, N):
                    t8 = hf * 4 + j
                    CP('act', slab[S_Q][:, t8, 0:N], banks[b0][:, 0:N], [], [bk(b0), ('slab', S_Q)])
                    if t8 % 2 == 1:
                        h = t8 // 2
                        ce = ceb[h]
                        cks = [('eb', 2 * h), ('eb', 2 * h + 1)]
                        for kt2 in range(2):
                            b1 = nb()
                            for dt in range(2):
                                MM(banks[b1][:, 0:N], mkT[:, h * 2 + dt, kt2 * 128:(kt2 + 1) * 128], slab[S_Q][:, h * 2 + dt, 0:N],
                                   dt == 0, dt == 1, ['mkT', ('slab', S_Q)], b1)
                            ACT(ce[:, kt2, 0:N], banks[b1][:, 0:N], AF.Exp, [], [bk(b1)] + cks, scale=1.0 / 16.0)
                        if h >= 1:
                            cross_finish(h - 1)
            cross_finish(3)
            proj_tm_resid('w_co', S_K, 1.0, None if dbg == 'cross' else 'g_ffn')
            if dbg == 'cross':
                dbg_store()
                return

            for gi in range(11):
                wi = wload('gu%d' % gi)
                for j in range(2):
                    bg = nb()
                    bu = nb()
                    for kt in range(8):
                        MM(banks[bg][:, 0:N], wsl[wi][:, kt, j * 128:(j + 1) * 128], slab[S_XN][:, kt, 0:N], kt == 0, kt == 7,
                           [('wsl', wi), ('slab', S_XN)], bg)
                    for kt in range(8):
                        MM(banks[bu][:, 0:N], wsl[wi][:, kt, 256 + j * 128:256 + (j + 1) * 128], slab[S_XN][:, kt, 0:N], kt == 0, kt == 7,
                           [('wsl', wi), ('slab', S_XN)], bu)
                    hj = gi * 2 + j
                    k = hj % 2
                    ACT(tb[k][:, 0:N], banks[bg][:, 0:N], AF.Tanh, [], [bk(bg), ('tb', k)], scale=0.5)
                    STT(sf[k][:, 0:N], tb[k][:, 0:N], 1.0, banks[bg][:, 0:N], ALU.add, ALU.mult, [('tb', k)], [bk(bg), 'rt%d' % k])
                    hs = 1 + hj // 8
                    TT('dve', slab[hs][:, hj % 8, 0:N], sf[k][:, 0:N], banks[bu][:, 0:N], ALU.mult, ['rt%d' % k],
                       [bk(bu), ('slab', hs)])
            npend = []
            if nxt is not None:
                xo2 = xoff[0] ^ 4

                def mk_a(t_):
                    return lambda: norm_a(t_, XT[xo2 + t_][:], ('xt', xo2 + t_), 128, t_)

                def mk_b(t_):
                    return lambda: norm_b(t_, 128, 'g_attn', S_XN)
                mk_a(0)()
                mk_a(1)()
                mk_a(2)()
                npend = [mk_b(0), mk_a(3), mk_b(1), mk_b(2), mk_b(3)]
            for hf in range(2):
                bt = [nb() for _ in range(NT)]
                for sbi, (k0, nk) in enumerate(((0, 8), (8, 8), (16, 6))):
                    if npend and not (hf == 0 and sbi == 0):
                        npend.pop(0)()
                        if len(npend) == 4:
                            npend.pop(0)()
                    wi = wload('dn%d_%d' % (hf, sbi))
                    for tt in range(NT):
                        pt = pts[tt]
                        for kl in range(nk):
                            kt = k0 + kl
                            hs = 1 + kt // 8
                            MM(banks[bt[tt]][0:pt, :], slab[hs][:, kt % 8, tt * 128:tt * 128 + pt], wsl[wi][:, kl, :],
                               kt == 0, kt == 21, [('wsl', wi), ('slab', hs)], bt[tt])
                for tt in range(NT):
                    pt = pts[tt]
                    xa = XT[xoff[0] + tt][0:pt, hf * 512:(hf + 1) * 512]
                    STT(xa, banks[bt[tt]][0:pt, :], 0.5, xa, ALU.mult, ALU.add, [('xt', xoff[0] + tt)], [bk(bt[tt]), ('xt', xoff[0] + tt)])
            while npend:
                npend.pop(0)()

            for tt in range(NT):
                pt = pts[tt]
                xa = XT[xoff[0] + tt][0:pt, :]
                col = 4 + tt
                ACT(junk[0:pt, :], xa, AF.Square, [('xt', xoff[0] + tt)], ['u1', ('ssq', col)], accum=ssq[0:pt, col:col + 1])
                ACT(rstd[0:pt, col:col + 1], ssq[0:pt, col:col + 1], AF.Ln, [('ssq', col), 'epsb'], [('rstd', col)], scale=1.0 / D, bias=epsb[0:pt, 0:1])
                ACT(rstd[0:pt, col:col + 1], rstd[0:pt, col:col + 1], AF.Exp, [('rstd', col)], [('rstd', col)], scale=-0.5)
                STT(xa, xa, rstd[0:pt, col:col + 1], gfin[0:pt, :], ALU.mult, ALU.mult, [('xt', xoff[0] + tt), ('rstd', col), 'gfin'], [('xt', xoff[0] + tt)])
                DMA('pool', ydst[tt * 128:tt * 128 + pt, :], xa, ('xo', xoff[0] + tt), reads=[('xt', xoff[0] + tt)])

        P.strict = STRICT_SAME_ENGINE
        gblk = 0
        for b in range(2):
            xoff[0] = 4 * (gblk % 2)
            mst = mem_stages(b, 'p')
            if b == 0:
                mst = eb_stages() + mst
            MEMSET('dve', st[:], 0.0, ['st'])
            MEMSET('pool', stb[:], 0.0, ['stb'])
            for blk in range(NBLK):
                xoff[0] = 4 * (gblk % 2)
                nxt = None
                if blk + 1 < NBLK and not dbg:
                    nxt = (xp[b, (blk + 1) * 512:(blk + 2) * 512, :], 512, (blk + 1) * 512)
                elif b == 0 and not dbg:
                    nxt = (xp[1, 0:512, :], 512, 0)
                block(xp[b, blk * 512:(blk + 1) * 512, :], y_p[b, blk * 512:(blk + 1) * 512, :], 512, blk * 512,
                      [(u * 128, 128) for u in range(4)], blk * 8, blk == NBLK - 1, 'p', b, pro_done=((blk > 0 or b == 1) and not dbg), nxt=nxt,
                      mstages=(mst if blk == 0 else None))
                gblk += 1
        for b in range(2):
            xoff[0] = 4 * (gblk % 2)
            gblk += 1
            mst = mem_stages(b, 's')
            DMA('pool', st[:], cret[b].rearrange("h (t p) e -> p h t e", p=128), 'stl', writes=['st'])
            CP('pool', stb[:].rearrange("p h t e -> p (h t e)"), st[:].rearrange("p h t e -> p (h t e)"), ['st'], ['stb'])
            DMA('pool', rt[0][:, 0:128], cswk[b], 'ckl', writes=['rt0'])
            DMA('pool', rt[1][:, 0:128], cswv[b], 'cvl', writes=['rt1'])
            kd = junk[:, 0:256].rearrange("p (kv du d) -> p kv du d", kv=2, du=2)
            CP('dve', kd, rt[0][:, 0:128].rearrange("p (kv d) -> p kv d", kv=2).unsqueeze(2).to_broadcast([128, 2, 2, 64]),
               ['rt0'], ['u1'])
            b0 = nb()
            pv = banks[b0][:].bitcast(BF16)
            for kv in range(2):
                TR(pv[:, kv * 128:(kv + 1) * 128], junk[:, kv * 128:(kv + 1) * 128], ['u1'], b0)
            for kv in range(2):
                for var in range(2):
                    lo = var * 64
                    CP('act', KT[lo:lo + 64, kv, var, 0:128], pv[lo:lo + 64, kv * 128:(kv + 1) * 128], [], [bk(b0), 'KT'])
            for kv in range(2):
                CP('dve', VP[:, 0, kv, 0, 0:64], rt[1][:, kv * 64:(kv + 1) * 64], ['rt1'], ['VP'])
                CP('dve', VP[:, 0, kv, 1, 64:128], rt[1][:, kv * 64:(kv + 1) * 64], ['rt1'], ['VP'])
            DMA('pool', swk_s[b, 0:64, :], cswk[b, 64:128, :], 'pk')
            DMA('pool', swv_s[b, 0:64, :], cswv[b, 64:128, :], 'pvv')
            block(xs[b], y_s[b], 64, S, [(0, 64)], 64, True, 's', b, mstages=mst)

        P.emit(nc, es)
    return nc, consts


_CACHE = {}


DBG = False


def _get(S):
    if S not in _CACHE:
        _CACHE[S] = build(S, DBG)
    return _CACHE[S]


def kernel(x_prompt, x_sample, cache_ret_state, cache_swa_k, cache_swa_v, cache_mem_k, cache_mem_v,
           mem_prompt, rel_bias, g_attn, w_in, w_ret_out, w_swa_out, w_mix_out, swa_sinks,
           g_cross, g_mem, w_cq, w_mk, w_mv, w_co, g_ffn, w_gate, w_up, w_down, g_final):
    f = lambda a: np.ascontiguousarray(np.asarray(a, dtype=np.float32))
    S = x_prompt.shape[1]
    nc, consts = _get(S)
    ncore = x_prompt.shape[0] // 2
    shared = {'relb': f(rel_bias), 'sinks': f(swa_sinks).reshape(16),
              'g_attn': f(g_attn).reshape(D), 'g_cross': f(g_cross).reshape(D), 'g_mem': f(g_mem).reshape(D),
              'g_ffn': f(g_ffn).reshape(D), 'g_final': f(g_final).reshape(D),
              'w_in': f(w_in)[0], 'w_ret_out': f(w_ret_out)[0], 'w_swa_out': f(w_swa_out)[0],
              'w_mix_out': f(w_mix_out)[0], 'w_cq': f(w_cq)[0], 'w_mk': f(w_mk)[0], 'w_mv': f(w_mv)[0],
              'w_co': f(w_co)[0], 'w_gate': f(w_gate)[0], 'w_up': f(w_up)[0], 'w_down': f(w_down)[0]}
    for n, a in consts.items():
        shared['c_' + n] = a
    in_maps = []
    for c in range(ncore):
        sl = slice(2 * c, 2 * c + 2)
        m = dict(shared)
        m['xp'] = f(x_prompt[sl]); m['xs'] = f(x_sample[sl])
        m['cret'] = f(cache_ret_state[0, sl])
        m['cswk'] = f(cache_swa_k[0, sl]).reshape(2, 128, 128)
        m['cswv'] = f(cache_swa_v[0, sl]).reshape(2, 128, 128)
        m['cmk'] = f(cache_mem_k[0, sl]).reshape(2, 256, D)
        m['cmv'] = f(cache_mem_v[0, sl]).reshape(2, 256, D)
        m['memp'] = f(mem_prompt[sl])
        in_maps.append(m)
    res = run_bass_kernel_spmd(nc, in_maps, core_ids=list(range(ncore)))
    R = res.results
    cat = lambda k: np.concatenate([r[k] for r in R], axis=0)
    B = 2 * ncore
    return (cat('y_p'), cat('y_s'),
            cat('rst_p')[None], cat('rst_s')[None],
            cat('swk_p').reshape(1, B, 128, 2, 64), cat('swk_s').reshape(1, B, 128, 2, 64),
            cat('swv_p').reshape(1, B, 128, 2, 64), cat('swv_s').reshape(1, B, 128, 2, 64),
            cat('mk_p').reshape(1, B, 256, 4, 256), cat('mv_p').reshape(1, B, 256, 4, 256))
```
